# Optimizing a Trainium2 kernel written in Bass

```python
import math
import jax, jax.numpy as jnp
from jax import lax
import numpy as np

D_MODEL = 1024
BATCH = 8
SEQ = 2048
DEPTH = 4

N_MIXERS = 3
BLOCK = 128
N_BUCKETS = 32
MAX_DISTANCE = 128
N_BIAS_CH = 16
DA_HEADS = 8
DA_QK_DIM = 64
DA_V_DIM = 2 * DA_QK_DIM
SB_HEADS = 16
SB_DIM = D_MODEL // SB_HEADS
SW_Q_HEADS = 16
SW_KV_HEADS = 4
SW_DIM = 64
SW_WINDOW = 128
D_FF = 2752
CONV_W = 3
EPS = 1e-6
NEG = -1e30
N_A = (DEPTH + 2) // 3
N_B = (DEPTH + 1) // 3
N_C = DEPTH // 3

kernel_name = 'interleaved_diff_stickbreak_swa_convffn'


def rmsnorm(x, g):
    xf = x.astype(jnp.float32)
    y = xf * lax.rsqrt(jnp.mean(xf * xf, axis=-1, keepdims=True) + EPS)
    return (y * g.astype(jnp.float32)).astype(x.dtype)


def t5_bucket(dist):
    max_exact = N_BUCKETS // 2
    d = jnp.maximum(dist, 0)
    large = max_exact + (jnp.log(jnp.maximum(d, 1).astype(jnp.float32) / max_exact)
                         / math.log(MAX_DISTANCE / max_exact) * (N_BUCKETS - max_exact)).astype(jnp.int32)
    large = jnp.minimum(large, N_BUCKETS - 1)
    return jnp.where(d < max_exact, d, large)


def lambda_init_fn(layer):
    return 0.8 - 0.6 * math.exp(-0.3 * layer)


def diff_attention(h, w_qkv, lam, subln, rel_bias, lambda_init):
    B, S, _ = h.shape
    nb = S // BLOCK
    qk_w = DA_HEADS * 2 * DA_QK_DIM
    qkv = h @ w_qkv
    q, k, v = jnp.split(qkv, [qk_w, 2 * qk_w], axis=-1)
    q = q.reshape(B, S, DA_HEADS, 2, DA_QK_DIM)
    k = k.reshape(B, S, DA_HEADS, 2, DA_QK_DIM)
    v = v.reshape(B, S, DA_HEADS, DA_V_DIM)
    lf = lam.astype(jnp.float32)
    lam_full = jnp.exp(jnp.sum(lf[0] * lf[1])) - jnp.exp(jnp.sum(lf[2] * lf[3])) + lambda_init
    bias_tab = rel_bias.astype(jnp.float32).reshape(N_BUCKETS, DA_HEADS, 2)
    k_pos = jnp.arange(S)
    scale = DA_QK_DIM ** -0.5
    q_blocks = jnp.moveaxis(q.reshape(B, nb, BLOCK, DA_HEADS, 2, DA_QK_DIM), 1, 0)

    def block(args):
        qb, start = args
        q_pos = start + jnp.arange(BLOCK)
        dist = q_pos[:, None] - k_pos[None, :]
        bias = jnp.transpose(bias_tab[t5_bucket(dist)], (2, 3, 0, 1))
        s = jnp.einsum('bqhjd,bkhjd->bhjqk', qb, k).astype(jnp.float32) * scale + bias
        s = jnp.where(dist >= 0, s, NEG)
        p = jax.nn.softmax(s, axis=-1)
        a = p[:, :, 0] - lam_full * p[:, :, 1]
        return jnp.einsum('bhqk,bkhe->bqhe', a.astype(v.dtype), v)

    o = lax.map(block, (q_blocks, jnp.arange(nb) * BLOCK))
    o = jnp.moveaxis(o, 0, 1).reshape(B, S, DA_HEADS, DA_V_DIM)
    o = rmsnorm(o, subln) * (1.0 - lambda_init)
    return o.reshape(B, S, DA_HEADS * DA_V_DIM)


def stick_breaking_attention(h, w_qkv):
    B, S, _ = h.shape
    nb = S // BLOCK
    qkv = h @ w_qkv
    q, k, v = jnp.split(qkv, 3, axis=-1)
    q = q.reshape(B, S, SB_HEADS, SB_DIM)
    k = k.reshape(B, S, SB_HEADS, SB_DIM)
    v = v.reshape(B, S, SB_HEADS, SB_DIM)
    k_pos = jnp.arange(S)
    scale = SB_DIM ** -0.5
    q_blocks = jnp.moveaxis(q.reshape(B, nb, BLOCK, SB_HEADS, SB_DIM), 1, 0)

    def block(args):
        qb, start = args
        q_pos = start + jnp.arange(BLOCK)
        strict = k_pos[None, :] < q_pos[:, None]
        z = jnp.einsum('bqhd,bkhd->bhqk', qb, k).astype(jnp.float32) * scale
        log_1m_beta = jnp.where(strict, jax.nn.log_sigmoid(-z), 0.0)
        between = lax.cumsum(log_1m_beta, axis=3, reverse=True) - log_1m_beta
        a = jnp.where(strict, jnp.exp(jax.nn.log_sigmoid(z) + between), 0.0)
        return jnp.einsum('bhqk,bkhd->bqhd', a.astype(v.dtype), v)

    o = lax.map(block, (q_blocks, jnp.arange(nb) * BLOCK))
    return jnp.moveaxis(o, 0, 1).reshape(B, S, SB_HEADS * SB_DIM)


def sliding_window_attention(h, w_qkv, sinks, rel_bias):
    B, S, _ = h.shape
    nb = S // BLOCK
    G = SW_Q_HEADS // SW_KV_HEADS
    q_w = SW_Q_HEADS * SW_DIM
    kv_w = SW_KV_HEADS * SW_DIM
    qkv = h @ w_qkv
    q, k, v = jnp.split(qkv, [q_w, q_w + kv_w], axis=-1)
    q = q.reshape(B, nb, BLOCK, SW_KV_HEADS, G, SW_DIM)
    k = k.reshape(B, nb, BLOCK, SW_KV_HEADS, SW_DIM)
    v = v.reshape(B, nb, BLOCK, SW_KV_HEADS, SW_DIM)

    def band(t):
        prev = jnp.pad(t[:, :-1], ((0, 0), (1, 0), (0, 0), (0, 0), (0, 0)))
        return jnp.concatenate([prev, t], axis=2)

    kb, vb = band(k), band(v)
    i = jnp.arange(BLOCK)[:, None]
    j = jnp.arange(2 * BLOCK)[None, :]
    dist = i + BLOCK - j
    in_window = (dist >= 0) & (dist < SW_WINDOW)
    blk = jnp.arange(nb)[:, None, None]
    valid = in_window[None] & ((blk * BLOCK - BLOCK + j[None]) >= 0)
    bias = rel_bias.astype(jnp.float32)[t5_bucket(dist)]
    bias = bias.reshape(BLOCK, 2 * BLOCK, SW_KV_HEADS, G).transpose(2, 3, 0, 1)
    s = jnp.einsum('bnqgrd,bnkgd->bngrqk', q, kb).astype(jnp.float32) * (SW_DIM ** -0.5) + bias
    s = jnp.where(valid[None, :, None, None], s, NEG)
    sink = jnp.broadcast_to(sinks.astype(jnp.float32).reshape(SW_KV_HEADS, G, 1, 1), s.shape[:-1] + (1,))
    p = jax.nn.softmax(jnp.concatenate([s, sink], axis=-1), axis=-1)[..., :-1]
    o = jnp.einsum('bngrqk,bnkgd->bnqgrd', p.astype(vb.dtype), vb)
    return o.reshape(B, S, q_w)


def conv_gated_ffn(h, w_up, conv_w, conv_b, w_down):
    u = h @ w_up
    C = u.shape[-1]
    u = lax.conv_general_dilated(u, conv_w[:, None, :].astype(u.dtype), window_strides=(1,),
                                 padding=[(CONV_W - 1, 0)], dimension_numbers=('NWC', 'WIO', 'NWC'),
                                 feature_group_count=C) + conv_b
    gate, val = jnp.split(u, 2, axis=-1)
    return (jax.nn.silu(gate) * val) @ w_down


def setup_inputs(seed: int = 0) -> dict:
    key = jax.random.key(seed)
    ks = jax.random.split(key, 20)
    D, F = D_MODEL, D_FF
    nrm = lambda k, shape, fan_in: jax.random.normal(k, shape, jnp.float32) * fan_in ** -0.5
    gain = lambda k, shape: 1.0 + 0.05 * jax.random.normal(k, shape, jnp.float32)
    return {
        'x': jax.random.normal(ks[0], (BATCH, SEQ, D), jnp.float32),
        'rel_bias': 0.5 * jax.random.normal(ks[1], (N_BUCKETS, N_BIAS_CH), jnp.float32),
        'attn_norm': gain(ks[2], (DEPTH, D)),
        'ffn_norm': gain(ks[3], (DEPTH, D)),
        'w_o': nrm(ks[4], (DEPTH, D, D), D),
        'da_w_qkv': nrm(ks[5], (N_A, D, 2 * DA_HEADS * 2 * DA_QK_DIM + DA_HEADS * DA_V_DIM), D),
        'da_lambda': 0.1 * jax.random.normal(ks[6], (N_A, 4, DA_QK_DIM), jnp.float32),
        'da_subln': gain(ks[7], (N_A, DA_V_DIM)),
        'sb_w_qkv': nrm(ks[8], (N_B, D, 3 * SB_HEADS * SB_DIM), D),
        'sw_w_qkv': nrm(ks[9], (N_C, D, (SW_Q_HEADS + 2 * SW_KV_HEADS) * SW_DIM), D),
        'sw_sinks': 0.5 * jax.random.normal(ks[10], (N_C, SW_Q_HEADS), jnp.float32),
        'ffn_w_up': nrm(ks[11], (DEPTH, D, 2 * F), D),
        'ffn_conv_w': nrm(ks[12], (DEPTH, CONV_W, 2 * F), CONV_W),
        'ffn_conv_b': 0.02 * jax.random.normal(ks[13], (DEPTH, 2 * F), jnp.float32),
        'ffn_w_down': nrm(ks[14], (DEPTH, F, D), F),
        'final_norm': gain(ks[15], (D,)),
    }


def reference(x, rel_bias, attn_norm, ffn_norm, w_o, da_w_qkv, da_lambda, da_subln, sb_w_qkv,
              sw_w_qkv, sw_sinks, ffn_w_up, ffn_conv_w, ffn_conv_b, ffn_w_down, final_norm):
    for layer in range(DEPTH):
        mixer = layer % N_MIXERS
        slot = layer // N_MIXERS
        h = rmsnorm(x, attn_norm[layer])
        if mixer == 0:
            m = diff_attention(h, da_w_qkv[slot], da_lambda[slot], da_subln[slot], rel_bias,
                               lambda_init_fn(layer))
        elif mixer == 1:
            m = stick_breaking_attention(h, sb_w_qkv[slot])
        else:
            m = sliding_window_attention(h, sw_w_qkv[slot], sw_sinks[slot], rel_bias)
        x = x + m @ w_o[layer]
        h = rmsnorm(x, ffn_norm[layer])
        x = x + conv_gated_ffn(h, ffn_w_up[layer], ffn_conv_w[layer], ffn_conv_b[layer], ffn_w_down[layer])
    return rmsnorm(x, final_norm)
```

```python
import math
import numpy as np
from contextlib import ExitStack
import concourse.bass as bass
import concourse.mybir as mybir
from concourse.bass_utils import run_bass_kernel_spmd

F32 = mybir.dt.float32
BF16 = mybir.dt.bfloat16
AF = mybir.ActivationFunctionType
ALU = mybir.AluOpType

D = 1024
S = 2048
DEPTH = 4
DFF = 2752
NJ = 22
EPS = 1e-6
NEGM = -30000.0
FFN_GROUPS = [(0, 6), (6, 6), (12, 5), (17, 5)]
DEBUG_NOFFN = False
ATTACH_WAIT = True


class _I:
    __slots__ = ("eng", "fn", "deps", "dma", "sig", "sem", "val", "key")


class Prog:
    ENGS = ("pe", "act", "dve", "pool", "sp")
    SEM_LIMIT = 20000

    def __init__(self, nc, same_eng_sync=True):
        self.nc = nc
        self.same = same_eng_sync
        self.streams = {e: [] for e in self.ENGS}
        self.last_writer = {}
        self.readers = {}
        self.n = 0

    def add(self, eng, fn, reads=(), writes=(), dma=False, nodep=()):
        ins = _I()
        ins.eng, ins.fn, ins.dma, ins.sig, ins.sem, ins.val = eng, fn, dma, False, None, 0
        ins.key = writes[0] if (dma and writes) else None
        deps = set()
        for r in reads:
            w = self.last_writer.get(r)
            if w is not None:
                deps.add(w)
        reads = list(reads) + list(nodep)
        for r in writes:
            w = self.last_writer.get(r)
            if w is not None:
                deps.add(w)
            for rd in self.readers.get(r, ()):
                deps.add(rd)
        for r in reads:
            self.readers.setdefault(r, []).append(ins)
        for r in writes:
            self.last_writer[r] = ins
            self.readers[r] = []
        deps.discard(ins)
        ins.deps = deps
        self.streams[eng].append(ins)
        self.n += 1
        return ins

    def dma(self, q, out, in_, reads=(), writes=()):
        return self.add(q, lambda e: e.dma_start(out=out, in_=in_), reads=reads, writes=writes, dma=True)

    def barrier(self):
        lasts = set()
        for w in self.last_writer.values():
            lasts.add(w)
        for rl in self.readers.values():
            for r in rl:
                lasts.add(r)
        for e in self.ENGS:
            if self.streams[e]:
                lasts.add(self.streams[e][-1])
        for e in self.ENGS:
            ins = _I()
            ins.eng, ins.fn, ins.dma, ins.sig, ins.sem, ins.val, ins.key = e, None, False, False, None, 0, None
            ins.deps = set(lasts)
            self.streams[e].append(ins)
        self.last_writer = {}
        self.readers = {}

    def finish(self, outs):
        self.add("sp", None, reads=list(outs))

    def _skip(self, ins, d):
        if d.dma:
            return False
        if d.eng == ins.eng:
            if d.eng == "pe":
                return True
            if not self.same:
                return True
        return False

    def emit(self):
        nc = self.nc
        for e in self.ENGS:
            for ins in self.streams[e]:
                for d in ins.deps:
                    if d.fn is not None and not self._skip(ins, d):
                        d.sig = True
        semnames = []
        dmacnt = {}
        for e in self.ENGS:
            cnt, idx = 0, 0
            for ins in self.streams[e]:
                if ins.dma:
                    k = ("dma", ins.key)
                    dmacnt[k] = dmacnt.get(k, 0) + 1
                    ins.sem, ins.val = k, 16 * dmacnt[k]
                    if k not in semnames:
                        semnames.append(k)
                elif ins.sig:
                    cnt += 1
                    if cnt > self.SEM_LIMIT:
                        idx += 1
                        cnt = 1
                    ins.sem, ins.val = (e, idx), cnt
                    if ins.sem not in semnames:
                        semnames.append(ins.sem)
        self.nsems = len(semnames)
        with ExitStack() as es:
            sems = {}
            for i, k in enumerate(semnames):
                sems[k] = es.enter_context(nc.semaphore("s%d" % i))
            block = es.enter_context(nc.Block())

            def replay(ename):
                def body(eng):
                    known = {}
                    for ins in self.streams[ename]:
                        need = {}
                        for d in ins.deps:
                            if d.fn is None or self._skip(ins, d):
                                continue
                            if need.get(d.sem, 0) < d.val:
                                need[d.sem] = d.val
                        todo = [(k, v) for k, v in need.items() if known.get(k, 0) < v]
                        for k, v in todo:
                            known[k] = v
                        attach = None
                        if ATTACH_WAIT and ins.fn is not None and todo:
                            attach = todo.pop()
                        for k, v in todo:
                            eng.wait_ge(sems[k], v)
                        if ins.fn is not None:
                            bi = ins.fn(eng)
                            if attach is not None:
                                bi._wait_ge(sems[attach[0]], attach[1])
                            if ins.dma:
                                bi.then_inc(sems[ins.sem], 16)
                            elif ins.sig:
                                bi.then_inc(sems[ins.sem], 1)
                return body

            block.tensor(replay("pe"))
            block.scalar(replay("act"))
            block.vector(replay("dve"))
            block.gpsimd(replay("pool"))
            block.sync(replay("sp"))


def _t5_bucket(dist):
    max_exact = 16
    d = np.maximum(dist, 0)
    large = max_exact + (np.log(np.maximum(d, 1).astype(np.float32) / np.float32(max_exact))
                         / np.float32(math.log(128 / max_exact)) * np.float32(32 - max_exact)).astype(np.int32)
    large = np.minimum(large, 31)
    return np.where(d < max_exact, d, large)


V_AN = 0
V_FN = 32
V_FIN = 64
V_SUB = 72
V_SINK = 74
V_BC = 90
V_LAM = 106
V_CONV = 106 + 512
NV = V_CONV + 4 * NJ * 8


def _prep_shared(inp):
    rel_bias = np.asarray(inp["rel_bias"], np.float32)
    vecs = np.zeros((128, NV), np.float32)
    pc = lambda v: np.asarray(v, np.float32).reshape(8, 128).T
    for l in range(DEPTH):
        vecs[:, V_AN + 8 * l:V_AN + 8 * l + 8] = pc(inp["attn_norm"][l])
        vecs[:, V_FN + 8 * l:V_FN + 8 * l + 8] = pc(inp["ffn_norm"][l])
    vecs[:, V_FIN:V_FIN + 8] = pc(inp["final_norm"])
    vecs[:, V_SUB:V_SUB + 2] = np.asarray(inp["da_subln"], np.float32).T
    vecs[:, V_SINK:V_SINK + 16] = np.asarray(inp["sw_sinks"], np.float32).reshape(1, 16)
    vecs[:, V_BC:V_BC + 16] = rel_bias[31][None, :]
    vecs[:, V_LAM:V_LAM + 512] = np.asarray(inp["da_lambda"], np.float32).reshape(1, 512)
    cw = np.asarray(inp["ffn_conv_w"], np.float32)
    cb = np.asarray(inp["ffn_conv_b"], np.float32)
    for l in range(DEPTH):
        for half in range(2):
            w = np.zeros((3, NJ * 128), np.float32)
            b = np.zeros((NJ * 128,), np.float32)
            w[:, :DFF] = cw[l][:, half * DFF:(half + 1) * DFF]
            b[:DFF] = cb[l][half * DFF:(half + 1) * DFF]
            w = w.reshape(3, NJ, 128)
            b = b.reshape(NJ, 128)
            for j in range(NJ):
                base = V_CONV + (l * NJ + j) * 8 + half * 4
                vecs[:, base + 0] = w[0, j]
                vecs[:, base + 1] = w[1, j]
                vecs[:, base + 2] = w[2, j]
                vecs[:, base + 3] = b[j]
    kk = np.arange(128)[:, None]
    qq = np.arange(128)[None, :]
    d0 = qq - kk
    d1 = qq - kk + 128
    b0 = rel_bias[_t5_bucket(d0)]
    b1 = rel_bias[_t5_bucket(d1)]
    bm = np.zeros((128, 3, 16, 128), np.float32)
    neg = np.float32(NEGM)
    bm[:, 0] = np.where((d0 >= 0)[:, :, None], b0, neg).transpose(0, 2, 1)
    bm[:, 1] = b1.transpose(0, 2, 1)
    bm[:, 2] = np.where((d1 < 128)[:, :, None], b1, neg).transpose(0, 2, 1)
    masks = np.zeros((128, 2, 128), np.float32)
    masks[:, 0] = (kk < qq).astype(np.float32)
    masks[:, 1] = np.where(kk < qq, np.float32(0), neg)
    tri = (kk >= qq).astype(np.float32)
    consts = np.zeros((128, 2, 128), np.float32)
    consts[:, 0] = 1.0
    consts[:, 1] = tri
    return dict(vecs=vecs, bm=bm, masks=masks, consts=consts)


def _prep_layer(inp, l):
    mixer, slot = l % 3, l // 3
    out = {}
    if mixer in (0, 1):
        w = np.asarray(inp["da_w_qkv" if mixer == 0 else "sb_w_qkv"][slot], np.float32)
        q = w[:, 0:1024].reshape(D, 8, 128)
        k = w[:, 1024:2048].reshape(D, 8, 128)
        v = w[:, 2048:3072].reshape(D, 8, 128)
        wh = np.concatenate([q, k, v], axis=2)
        wh = wh.reshape(8, 128, 8, 384).transpose(2, 1, 0, 3)
        out["wqkv%d" % l] = np.ascontiguousarray(wh)
    else:
        w = np.asarray(inp["sw_w_qkv"][slot], np.float32)
        q = w[:, 0:1024]
        k = w[:, 1024:1280].reshape(D, 4, 64)
        v = w[:, 1280:1536].reshape(D, 4, 64)
        kd = np.concatenate([k, k], axis=2)
        vd = np.concatenate([v, v], axis=2)
        gi = np.arange(8) // 2
        wh = np.concatenate([q.reshape(D, 8, 128), kd[:, gi, :], vd[:, gi, :]], axis=2)
        wh = wh.reshape(8, 128, 8, 384).transpose(2, 1, 0, 3)
        out["wqkv%d" % l] = np.ascontiguousarray(wh)
    wo = np.asarray(inp["w_o"][l], np.float32)
    wo = wo.reshape(8, 128, 8, 128).transpose(2, 1, 0, 3)
    out["wo%d" % l] = np.ascontiguousarray(wo)
    wu = np.asarray(inp["ffn_w_up"][l], np.float32)
    g = np.zeros((D, NJ * 128), np.float32)
    v = np.zeros((D, NJ * 128), np.float32)
    g[:, :DFF] = wu[:, :DFF]
    v[:, :DFF] = wu[:, DFF:]
    gv = np.concatenate([g.reshape(D, NJ, 128), v.reshape(D, NJ, 128)], axis=2)
    gv = gv.reshape(8, 128, NJ, 256).transpose(2, 1, 0, 3)
    out["wup%d" % l] = np.ascontiguousarray(gv)
    wd = np.zeros((NJ * 128, D), np.float32)
    wd[:DFF] = np.asarray(inp["ffn_w_down"][l], np.float32)
    out["wdn%d" % l] = np.ascontiguousarray(wd.reshape(NJ, 128, D))
    return out


def _lambda_init(layer):
    return 0.8 - 0.6 * math.exp(-0.3 * layer)


class Pool_:
    def __init__(self, ap_u16, nbytes):
        self.ap = ap_u16
        self.n = nbytes
        self.off = 0

    def alloc(self, free_shape, dt):
        esz = 4 if dt == F32 else 2
        nel = int(np.prod(free_shape))
        nb = nel * esz
        off = (self.off + 63) // 64 * 64
        assert off + nb <= self.n, ("SBUF pool overflow", off, nb, self.n)
        self.off = off + nb
        a = self.ap[:, off // 2:(off + nb) // 2]
        if dt == F32:
            a = a.bitcast(F32)
        if len(free_shape) == 2:
            a = a.rearrange("p (a b) -> p a b", a=free_shape[0])
        elif len(free_shape) == 3:
            a = a.rearrange("p (a b c) -> p a b c", a=free_shape[0], b=free_shape[1])
        return a


def build_nc(layers, do_final):
    nc = bass.Bass("TRN2", target_bir_lowering=False)
    nc.allow_low_precision("bf16 matmul operands with fp32 PSUM accumulation by design")
    dram = lambda n, s, k="ExternalInput": nc.dram_tensor(n, list(s), F32, kind=k).ap()
    xin = dram("xin", [D, S])
    yout = dram("yout", [D, S], "ExternalOutput")
    vecs_d = dram("vecs", [128, NV])
    bm_d = dram("bm", [128, 3, 16, 128])
    masks_d = dram("masks", [128, 2, 128])
    consts_d = dram("consts", [128, 2, 128])
    wd = {}
    for l in layers:
        mixer = l % 3
        wd["wqkv%d" % l] = dram("wqkv%d" % l, [8, 128, 8, 384])
        wd["wo%d" % l] = dram("wo%d" % l, [8, 128, 8, 128])
        wd["wup%d" % l] = dram("wup%d" % l, [NJ, 128, 8, 256])
        wd["wdn%d" % l] = dram("wdn%d" % l, [NJ, 128, D])

    with ExitStack() as es:
        POOLB = 212480
        pool_t = es.enter_context(nc.sbuf_tensor("pool", [128, POOLB // 2], BF16))
        A = Pool_(pool_t[:, :], POOLB)
        ps = [es.enter_context(nc.psum_tensor("ps%d" % i, [128, 512], F32))[:, :] for i in range(8)]
        P = Prog(nc)

        xT = A.alloc([8, S], F32)
        hT = A.alloc([8, S], BF16)
        vecs = A.alloc([NV], F32)
        ones_b = A.alloc([128], BF16)
        tri_b = A.alloc([128], BF16)
        nones_b = A.alloc([128], BF16)
        ntri_b = A.alloc([128], BF16)
        ones_dmb = A.alloc([128], BF16)
        ones_eb = A.alloc([128], BF16)
        ones_dm = A.alloc([128], F32)
        ones_e = A.alloc([128], F32)
        cst_f = A.alloc([2, 128], F32)
        small = A.alloc([64], F32)
        persist_end = A.off

        eps_ap = small[:, 4:5]
        NEGLAM = 0
        SUBS = 2
        ESINK = 8

        for c in range(8):
            P.dma("sp", xT[:, c, :], xin[c * 128:(c + 1) * 128, :], writes=["x%d_%d" % (c, t) for t in range(4)])
        P.dma("sp", vecs, vecs_d, writes=["vecs"])
        P.dma("sp", cst_f, consts_d, writes=["cstf"])
        P.add("dve", lambda e: e.memset(small[:, 4:5], EPS), writes=["small"])
        P.add("dve", lambda e: e.tensor_copy(ones_b, cst_f[:, 0, :]), reads=["cstf"], writes=["ones_b"])
        P.add("dve", lambda e: e.tensor_copy(tri_b, cst_f[:, 1, :]), reads=["cstf"], writes=["tri_b"])
        P.add("dve", lambda e: e.tensor_scalar(nones_b, cst_f[:, 0, :], -1.0, None, ALU.mult), reads=["cstf"], writes=["ones_b"])
        P.add("dve", lambda e: e.tensor_scalar(ntri_b, cst_f[:, 1, :], -1.0, None, ALU.mult), reads=["cstf"], writes=["tri_b"])
        P.add("dve", lambda e: e.tensor_scalar(ones_dm, cst_f[:, 0, :], 1.0 / D, None, ALU.mult), reads=["cstf"], writes=["ones_dm"])
        P.add("dve", lambda e: e.tensor_scalar(ones_dmb, cst_f[:, 0, :], 1.0 / D, None, ALU.mult), reads=["cstf"], writes=["ones_dm"])
        P.add("dve", lambda e: e.tensor_scalar(ones_eb, cst_f[:, 0, :], 1.0 / 128, None, ALU.mult), reads=["cstf"], writes=["ones_e"])
        P.add("dve", lambda e: e.tensor_scalar(ones_e, cst_f[:, 0, :], 1.0 / 128, None, ALU.mult), reads=["cstf"], writes=["ones_e"])

        def rmsnorm_to_h(gcol, tag):
            sq = [A.alloc([512], BF16) for _ in range(4)]
            rstd = [A.alloc([512], F32) for _ in range(2)]
            k = 0
            for tg in range(4):
                sl = slice(tg * 512, (tg + 1) * 512)
                bank = 6 + (tg % 2)
                for c in range(8):
                    b = k % 4
                    k += 1
                    eng = "act"
                    if eng == "act":
                        P.add("act", lambda e, o=sq[b], i=xT[:, c, sl]: e.activation(o, i, AF.Square),
                              reads=["x%d_%d" % (c, tg)], writes=["sq%d" % b])
                    else:
                        P.add("pool", lambda e, o=sq[b], i=xT[:, c, sl]: e.tensor_tensor(o, i, i, ALU.mult),
                              reads=["x%d_%d" % (c, tg)], writes=["sq%d" % b])
                    P.add("pe", lambda e, o=ps[bank], r=sq[b], st=(c == 0), sp_=(c == 7): e.matmul(o, ones_dmb, r, start=st, stop=sp_),
                          reads=["sq%d" % b, "ones_dm"], writes=["ps%d" % bank])
                rb = rstd[tg % 2]
                P.add("act", lambda e, o=rb, i=ps[bank]: e.activation(o, i, AF.Ln, bias=eps_ap, scale=1.0),
                      reads=["ps%d" % bank, "small"], writes=["rstd%d" % (tg % 2)])
                P.add("act", lambda e, o=rb: e.activation(o, o, AF.Exp, scale=-0.5),
                      reads=["rstd%d" % (tg % 2)], writes=["rstd%d" % (tg % 2)])
                for c in range(8):
                    P.add("dve", lambda e, o=hT[:, c, sl], i=xT[:, c, sl], g=vecs[:, gcol + c:gcol + c + 1], r=rb:
                          e.scalar_tensor_tensor(o, i, g, r, ALU.mult, ALU.mult),
                          reads=["x%d_%d" % (c, tg), "vecs", "rstd%d" % (tg % 2)], writes=["h%d_%d" % (c, tg)])

        H_ALL = ["h%d_%d" % (c, t) for c in range(8) for t in range(4)]

        def load_w(q, dst, src, name):
            P.dma(q, dst, src, writes=[name])

        def proj_qkv_head(wt, wname, qT, kT, vS, qscale, tag, banks, vtag="v", kpad=None, qpad=None, do_kv=True, evac="act"):
            bi = 0
            for which, dst in ((0, qT), (1, kT)):
                if which == 1 and not do_kv:
                    continue
                for tg in range(4):
                    sl = slice(tg * 512, (tg + 1) * 512)
                    bank = banks[bi % len(banks)]
                    bi += 1
                    for kc in range(8):
                        P.add("pe", lambda e, o=ps[bank], w=wt[:, kc, which * 128:(which + 1) * 128], r=hT[:, kc, sl], st=(kc == 0), sp_=(kc == 7):
                              e.matmul(o, w, r, start=st, stop=sp_),
                              reads=[wname, "h%d_%d" % (kc, tg)], writes=["ps%d" % bank])
                    rn = ("q" if which == 0 else "k") + tag + "_%d" % tg
                    if which == 0 and qpad is not None:
                        P.add("act", lambda e, o=qpad[0][0:64, sl], i=ps[bank][0:64, :]: e.activation(o, i, AF.Identity, scale=qscale),
                              reads=["ps%d" % bank], writes=[rn + "a"])
                        P.add("act", lambda e, o=qpad[1][64:128, sl], i=ps[bank][64:128, :]: e.activation(o, i, AF.Identity, scale=qscale),
                              reads=["ps%d" % bank], writes=[rn + "b"])
                    elif which == 0 and qscale != 1.0:
                        if evac == "dve":
                            P.add("dve", lambda e, o=dst[:, sl], i=ps[bank]: e.tensor_scalar(o, i, qscale, None, ALU.mult),
                                  reads=["ps%d" % bank], writes=[rn])
                        else:
                            P.add("act", lambda e, o=dst[:, sl], i=ps[bank]: e.activation(o, i, AF.Identity, scale=qscale),
                                  reads=["ps%d" % bank], writes=[rn])
                    elif which == 1 and kpad is not None:
                        P.add("dve", lambda e, o=kpad[0][0:64, sl], i=ps[bank][0:64, :]: e.tensor_copy(o, i),
                              reads=["ps%d" % bank], writes=[rn + "a"])
                        if evac == "dve":
                            P.add("dve", lambda e, o=kpad[1][64:128, sl], i=ps[bank][64:128, :]: e.tensor_copy(o, i),
                                  reads=["ps%d" % bank], writes=[rn + "b"])
                        else:
                            P.add("act", lambda e, o=kpad[1][64:128, sl], i=ps[bank][64:128, :]: e.activation(o, i, AF.Identity),
                                  reads=["ps%d" % bank], writes=[rn + "b"])
                    else:
                        P.add("dve", lambda e, o=dst[:, sl], i=ps[bank]: e.tensor_copy(o, i),
                              reads=["ps%d" % bank], writes=[rn])
            for t4 in range(4 if do_kv else 0):
                bank = banks[bi % len(banks)]
                bi += 1
                for tt in range(4):
                    tok = slice((t4 * 4 + tt) * 128, (t4 * 4 + tt + 1) * 128)
                    for kc in range(8):
                        P.add("pe", lambda e, o=ps[bank][:, tt * 128:(tt + 1) * 128], l=hT[:, kc, tok], r=wt[:, kc, 256:384], st=(kc == 0), sp_=(kc == 7):
                              e.matmul(o, l, r, start=st, stop=sp_),
                              reads=[wname, "h%d_%d" % (kc, t4)], writes=["ps%d" % bank])
                if evac == "dve":
                    P.add("dve", lambda e, o=vS[:, t4 * 4:(t4 + 1) * 4, :], i=ps[bank].rearrange("p (a b) -> p a b", a=4): e.tensor_copy(o, i),
                          reads=["ps%d" % bank], writes=[vtag + "_%d" % t4])
                else:
                    P.add("act", lambda e, o=vS[:, t4 * 4:(t4 + 1) * 4, :], i=ps[bank].rearrange("p (a b) -> p a b", a=4): e.activation(o, i, AF.Identity),
                          reads=["ps%d" % bank], writes=[vtag + "_%d" % t4])

        def oproj(l, parts):
            wo = [A.alloc([8, 128], BF16) for _ in range(2)]
            for dc in range(8):
                wb = dc % 2
                load_w("pool", wo[wb], wd["wo%d" % l][dc], "wo%d" % wb)
                for tg in range(4):
                    sl = slice(tg * 512, (tg + 1) * 512)
                    bank = 6 + ((dc * 4 + tg) % 2)
                    for kc in range(8):
                        P.add("pe", lambda e, o=ps[bank], w=wo[wb][:, kc, :], r=oT[:, kc, sl], st=(kc == 0), sp_=(kc == 7):
                              e.matmul(o, w, r, start=st, stop=sp_),
                              reads=["wo%d" % wb] + (["o%d_%d_0" % (kc, tg), "o%d_%d_1" % (kc, tg)] if parts else ["o%d_%d" % (kc, tg)]), writes=["ps%d" % bank])
                    P.add("dve", lambda e, o=xT[:, dc, sl], i=ps[bank]: e.tensor_tensor(o, i, o, ALU.add),
                          reads=["ps%d" % bank, "x%d_%d" % (dc, tg)], writes=["x%d_%d" % (dc, tg)])

        def run_pipeline(N, stages):
            maxlag = max(lg for lg, _ in stages)
            for it in range(N + maxlag):
                for lg, fn in stages:
                    n = it - lg
                    if 0 <= n < N:
                        fn(n)

        def attn_da(l):
            slot = l // 3
            li = _lambda_init(l)
            nonlocal oT
            oT = A.alloc([8, S], BF16)
            bmh = [A.alloc([2, 2, 128], F32) for _ in range(2)]
            wts = [A.alloc([8, 384], BF16) for _ in range(2)]
            qT = A.alloc([S], BF16)
            kT = None
            kz = [A.alloc([S], BF16) for _ in range(2)]
            P.add("pool", lambda e: e.memset(kz[0][64:128, :], 0.0), writes=["kz0pad"])
            P.add("pool", lambda e: e.memset(kz[1][0:64, :], 0.0), writes=["kz1pad"])
            vSb = [A.alloc([16, 128], BF16) for _ in range(2)]
            NS, NP = 3, 3
            Pb = [A.alloc([512], BF16) for _ in range(NP)]
            tD = [A.alloc([256], F32) for _ in range(2)]
            tmp = [A.alloc([512], F32) for _ in range(5)]
            sqb = tmp[3].bitcast(BF16)[:, 0:512]
            lam = vecs[:, V_LAM + slot * 256:V_LAM + slot * 256 + 256]
            P.add("dve", lambda e: e.tensor_tensor(tmp[0][:, 0:64], lam[:, 0:64], lam[:, 64:128], ALU.mult), reads=["vecs"], writes=["tmp0"])
            P.add("dve", lambda e: e.tensor_tensor(tmp[0][:, 64:128], lam[:, 128:192], lam[:, 192:256], ALU.mult), reads=["vecs", "tmp0"], writes=["tmp0"])
            P.add("dve", lambda e: e.reduce_sum(tmp[1][:, 0:2], tmp[0][:, 0:128].rearrange("p (a b) -> p a b", a=2), mybir.AxisListType.X),
                  reads=["tmp0"], writes=["tmp1"])
            P.add("act", lambda e: e.activation(tmp[1][:, 2:4], tmp[1][:, 0:2], AF.Exp), reads=["tmp1"], writes=["tmp1"])
            P.add("dve", lambda e: e.scalar_tensor_tensor(small[:, NEGLAM + slot:NEGLAM + slot + 1], tmp[1][:, 3:4], -li, tmp[1][:, 2:3], ALU.add, ALU.subtract),
                  reads=["tmp1"], writes=["small"])
            P.add("dve", lambda e: e.tensor_scalar(small[:, SUBS + slot:SUBS + slot + 1], vecs[:, V_SUB + slot:V_SUB + slot + 1], 1.0 - li, None, ALU.mult),
                  reads=["vecs", "small"], writes=["small"])
            neglam = small[:, NEGLAM + slot:NEGLAM + slot + 1]
            subs = small[:, SUBS + slot:SUBS + slot + 1]

            def prefetch(h):
                load_w("pool", wts[h % 2], wd["wqkv%d" % l][h], "wt%d" % (h % 2))

            def prefetch_bm(h):
                P.dma("sp", bmh[h % 2], bm_d[:, 0:2, 2 * h:2 * h + 2, :], writes=["bmh%d" % (h % 2)])
                for j_ in range(2):
                    ch_ = 2 * h + j_
                    P.add("pool", lambda e, o=bmh[h % 2][:, :, j_, :], c_=vecs[:, V_BC + ch_:V_BC + ch_ + 1]: e.tensor_scalar(o, o, c_, None, ALU.subtract),
                          reads=["vecs", "bmh%d" % (h % 2)], writes=["bmh%d" % (h % 2)])

            tiles = []
            gi = 0
            for h in range(8):
                for g in range(4):
                    for j in range(2):
                        for kt in range(4 * g + 4):
                            tiles.append(dict(h=h, g=g, j=j, kt=kt, gi=gi, first=(kt == 0), last=(kt == 4 * g + 3),
                                              head_first=(g == 0 and j == 0 and kt == 0)))
                        gi += 1
            N = len(tiles)
            deferred = []

            def sS(n):
                t = tiles[n]
                h, g, j, kt = t["h"], t["g"], t["j"], t["kt"]
                if t["head_first"]:
                    if h == 0:
                        prefetch(0)
                        prefetch_bm(0)
                    if h + 1 < 8:
                        prefetch(h + 1)
                    proj_qkv_head(wts[h % 2], "wt%d" % (h % 2), qT, kT, vSb[h % 2], 0.125, "", banks=(7, n % NS), vtag="v%d" % (h % 2), kpad=kz, evac="act")
                if g == 1 and j == 0 and kt == 0 and h + 1 < 8:
                    prefetch_bm(h + 1)
                i = kt - 4 * g
                q0 = max(i, 0) * 128
                sb = n % NS
                rows = slice(j * 64, (j + 1) * 64)
                P.add("pe", lambda e, o=ps[sb][:, q0:512], l_=kz[j][:, kt * 128:(kt + 1) * 128], r=qT[:, g * 512 + q0:(g + 1) * 512]:
                      e.matmul(o, l_, r, start=True, stop=True),
                      reads=["k_%da" % (kt // 4), "k_%db" % (kt // 4), "kz0pad", "kz1pad", "q_%d" % g], writes=["ps%d" % sb])

            def sB(n):
                t = tiles[n]
                h, g, j, kt = t["h"], t["g"], t["j"], t["kt"]
                i = kt - 4 * g
                if i < -1:
                    return
                q0 = max(i, 0) * 128
                sb = n % NS
                bm = bmh[h % 2]
                bn = "bmh%d" % (h % 2)
                if i >= 0 and i < 3:
                    P.add("dve", lambda e, o=ps[sb][:, q0:q0 + 256].rearrange("p (a b) -> p a b", a=2), b=bm[:, 0:2, j, :]: e.tensor_tensor(o, o, b, ALU.add),
                          reads=["ps%d" % sb, bn], writes=["ps%d" % sb])
                elif i == 3:
                    P.add("dve", lambda e, o=ps[sb][:, q0:q0 + 128], b=bm[:, 0, j, :]: e.tensor_tensor(o, o, b, ALU.add),
                          reads=["ps%d" % sb, bn], writes=["ps%d" % sb])
                else:
                    P.add("dve", lambda e, o=ps[sb][:, 0:128], b=bm[:, 1, j, :]: e.tensor_tensor(o, o, b, ALU.add),
                          reads=["ps%d" % sb, bn], writes=["ps%d" % sb])

            def sP(n):
                t = tiles[n]
                h, g, j, kt = t["h"], t["g"], t["j"], t["kt"]
                i = kt - 4 * g
                q0 = max(i, 0) * 128
                sb = n % NS
                Pt = Pb[n % NP]
                pn = "P%d" % (n % NP)
                P.add("act", lambda e, o=Pt[:, q0:512], i_=ps[sb][:, q0:512]: e.activation(o, i_, AF.Exp),
                      reads=["ps%d" % sb], writes=[pn + "a"])

            def sV(n):
                t = tiles[n]
                h, g, j, kt, gi_ = t["h"], t["g"], t["j"], t["kt"], t["gi"]
                i = kt - 4 * g
                q0 = max(i, 0) * 128
                Pt = Pb[n % NP]
                pn = "P%d" % (n % NP)
                ob, db = 3 + gi_ % 2, 5 + gi_ % 2
                first, last = t["first"], t["last"]
                vS = vSb[h % 2]
                P.add("pe", lambda e, o=ps[ob][:, q0:512], l_=vS[:, kt, :], r=Pt[:, q0:512]: e.matmul(o, l_, r, start=first, stop=last),
                      reads=[pn + "a", pn + "b", "v%d_%d" % (h % 2, kt // 4)], writes=["ps%d" % ob])
                P.add("pe", lambda e, o=ps[db][:, q0:512], r=Pt[:, q0:512]: e.matmul(o, ones_b, r, start=first, stop=last),
                      reads=[pn + "a", pn + "b", "ones_b"], writes=["ps%d" % db])
                if not last:
                    return
                r0, a0, a1, sq, rs = tmp
                aj = a0 if j == 0 else a1
                an = "tmp1" if j == 0 else "tmp2"
                Q = deferred.append
                Q(lambda d_=ps[db], dn="ps%d" % db: P.add("act", lambda e: e.activation(r0, d_, AF.Ln), reads=[dn], writes=["tmp0"]))
                Q(lambda: P.add("act", lambda e: e.activation(r0, r0, AF.Exp, scale=-1.0), reads=["tmp0"], writes=["tmp0"]))
                Q(lambda o_=ps[ob], a_=aj, on_="ps%d" % ob, an_=an: P.add("dve", lambda e: e.tensor_tensor(a_, o_, r0, ALU.mult), reads=[on_, "tmp0"], writes=[an_]))
                if j == 1:
                    sl = slice(g * 512, (g + 1) * 512)
                    Q(lambda: P.add("dve", lambda e: e.scalar_tensor_tensor(a0, a1, neglam, a0, ALU.mult, ALU.add), reads=["tmp1", "tmp2", "small"], writes=["tmp1"]))
                    Q(lambda: P.add("act", lambda e: e.activation(sqb, a0, AF.Square), reads=["tmp1"], writes=["tmp3"]))
                    Q(lambda: P.add("pe", lambda e: e.matmul(ps[7], ones_eb, sqb, start=True, stop=True), reads=["tmp3", "ones_e"], writes=["ps7"]))
                    Q(lambda: P.add("act", lambda e: e.activation(rs, ps[7], AF.Ln, bias=eps_ap, scale=1.0), reads=["ps7", "small"], writes=["tmp4"]))
                    Q(lambda: P.add("act", lambda e: e.activation(rs, rs, AF.Exp, scale=-0.5), reads=["tmp4"], writes=["tmp4"]))
                    Q(lambda o=oT[:, h, sl], on_="o%d_%d" % (h, g): P.add("dve", lambda e: e.scalar_tensor_tensor(o, a0, subs, rs, ALU.mult, ALU.mult),
                                                                  reads=["tmp1", "tmp4", "small"], writes=[on_]))

            def sD(n):
                k_ = 2 if len(deferred) > 6 else 1
                for _ in range(k_):
                    if deferred:
                        deferred.pop(0)()

            run_pipeline(N, [(0, sS), (1, sB), (2, sP), (3, sV), (3, sD)])
            while deferred:
                deferred.pop(0)()

        def attn_sb(l):
            nonlocal oT
            oT = A.alloc([8, S], BF16)
            mk = A.alloc([2, 128], F32)
            P.dma("sp", mk, masks_d, writes=["mk"])
            wts = [A.alloc([8, 384], BF16) for _ in range(2)]
            qT = A.alloc([S], BF16)
            kT = None
            kz = [A.alloc([S], BF16) for _ in range(2)]
            P.add("pool", lambda e: e.memset(kz[0][64:128, :], 0.0), writes=["kz0pad"])
            P.add("pool", lambda e: e.memset(kz[1][0:64, :], 0.0), writes=["kz1pad"])
            vSb = [A.alloc([16, 128], BF16) for _ in range(2)]
            NZ, NW, NL, NA = 5, 4, 2, 2
            HSEL = ["dve"]
            LSEL = ["pool"]
            Wb = [A.alloc([512], F32) for _ in range(NW)]
            NE = 2 if all(h_ == "dve" for h_ in HSEL) else NW
            Eb = [A.alloc([512], F32) for _ in range(NE)]
            Lh = [A.alloc([512], BF16) for _ in range(NL)]
            Ll = [A.alloc([512], BF16) for _ in range(NL)]
            Ab = [A.alloc([512], BF16) for _ in range(NA)]
            Cb = [A.alloc([512], F32) for _ in range(2)]

            tiles = []
            for c in range(8):
                for r in range(2):
                    for g in range(4):
                        for kt in range(4 * g + 3, -1, -1):
                            tiles.append(dict(c=c, r=r, g=g, kt=kt, first=(kt == 4 * g + 3), last=(kt == 0),
                                              pair_first=(r == 0 and g == 0 and kt == 3)))
            N = len(tiles)
            for n in range(N):
                t = tiles[n]
                i = t["kt"] - 4 * t["g"]
                t["i"] = i
                t["q0"] = max(i, 0) * 128
                t["pq0"] = tiles[n - 1]["q0"] if not t["first"] else 512

            def prefetch(c):
                load_w("pool", wts[c % 2], wd["wqkv%d" % l][c], "wt%d" % (c % 2))

            def sZ(n):
                t = tiles[n]
                c, r, g, kt, q0 = t["c"], t["r"], t["g"], t["kt"], t["q0"]
                if t["pair_first"]:
                    if c == 0:
                        prefetch(0)
                    if c + 1 < 8:
                        prefetch(c + 1)
                    proj_qkv_head(wts[c % 2], "wt%d" % (c % 2), qT, kT, vSb[c % 2], 0.125, "", banks=(5, 6), vtag="v%d" % (c % 2), kpad=kz)
                rows = slice(r * 64, (r + 1) * 64)
                zb = n % NZ
                P.add("pe", lambda e, o=ps[zb][:, q0:512], l_=kz[r][:, kt * 128:(kt + 1) * 128], r_=qT[:, g * 512 + q0:(g + 1) * 512]:
                      e.matmul(o, l_, r_, start=True, stop=False, skip_group_check=True),
                      reads=["k_%da" % (kt // 4), "k_%db" % (kt // 4), "kz0pad", "kz1pad", "q_%d" % g], writes=["ps%d" % zb])

            def sE(n):
                t = tiles[n]
                q0 = t["q0"]
                zb = n % NZ
                W = Wb[n % NW]
                wn = "W%d" % (n % NW)
                E = Eb[n % NE]
                en = "E%d" % (n % NE)
                P.add("act", lambda e, o=E[:, q0:512], i_=ps[zb][:, q0:512]: e.activation(o, i_, AF.Exp), reads=["ps%d" % zb], writes=[en])
                P.add("act", lambda e, o=W[:, q0:512], i_=E[:, q0:512]: e.activation(o, i_, AF.Ln, bias=1.0, scale=1.0), reads=[en], writes=[wn])

            def sM(n):
                t = tiles[n]
                q0 = t["q0"]
                W = Wb[n % NW]
                wn = "W%d" % (n % NW)
                if t["i"] >= 0:
                    P.add("dve", lambda e, o=W[:, q0:q0 + 128], m=mk[:, 0, :]: e.tensor_tensor(o, o, m, ALU.mult), reads=[wn, "mk"], writes=[wn])

            def sL(n):
                t = tiles[n]
                q0 = t["q0"]
                W = Wb[n % NW]
                wn = "W%d" % (n % NW)
                LH, LL = Lh[n % NL], Ll[n % NL]
                hsel = HSEL[n % len(HSEL)]
                if hsel == "act":
                    P.add("act", lambda e, o=LH[:, q0:512], i_=Eb[n % NE][:, q0:512]: e.activation(o, i_, AF.Ln, bias=1.0, scale=1.0),
                          reads=["E%d" % (n % NE)], writes=["LH%d" % (n % NL)])
                else:
                    P.add("dve", lambda e, o=LH[:, q0:512], i_=W[:, q0:512]: e.tensor_copy(o, i_), reads=[wn], writes=["LH%d" % (n % NL)])
                P.add(LSEL[n % len(LSEL)], lambda e, o=LL[:, q0:512], i_=W[:, q0:512], hh=LH[:, q0:512]: e.tensor_tensor(o, i_, hh, ALU.subtract),
                      reads=[wn, "LH%d" % (n % NL)], writes=["LL%d" % (n % NL)])

            def sT(n):
                t = tiles[n]
                q0 = t["q0"]
                zb = n % NZ
                cb = 5 + n % 2
                LH, LL = Lh[n % NL], Ll[n % NL]
                hn, ln_ = "LH%d" % (n % NL), "LL%d" % (n % NL)
                P.add("pe", lambda e, o=ps[zb][:, q0:512], r_=LH[:, q0:512]: e.matmul(o, ntri_b, r_, start=False, stop=False, skip_group_check=True),
                      reads=[hn, "tri_b"], writes=["ps%d" % zb])
                P.add("pe", lambda e, o=ps[zb][:, q0:512], r_=LL[:, q0:512]: e.matmul(o, ntri_b, r_, start=False, stop=True, skip_group_check=True),
                      reads=[ln_, "tri_b"], writes=["ps%d" % zb])
                if not t["last"]:
                    P.add("pe", lambda e, o=ps[cb][:, q0:512], r_=LH[:, q0:512]: e.matmul(o, nones_b, r_, start=True, stop=False),
                          reads=[hn, "ones_b"], writes=["ps%d" % cb])
                    P.add("pe", lambda e, o=ps[cb][:, q0:512], r_=LL[:, q0:512]: e.matmul(o, nones_b, r_, start=False, stop=True),
                          reads=[ln_, "ones_b"], writes=["ps%d" % cb])

            def sC(n):
                t = tiles[n]
                if t["last"]:
                    return
                q0, pq0 = t["q0"], t["pq0"]
                cb = 5 + n % 2
                C, Cp = Cb[n % 2], Cb[(n - 1) % 2]
                cn, cpn = "C%d" % (n % 2), "C%d" % ((n - 1) % 2)
                if pq0 > q0:
                    P.add("dve", lambda e, o=C[:, q0:pq0], i_=ps[cb][:, q0:pq0]: e.tensor_copy(o, i_), reads=["ps%d" % cb], writes=[cn + "a"])
                if pq0 < 512:
                    P.add("dve", lambda e, o=C[:, pq0:512], i_=ps[cb][:, pq0:512], p_=Cp[:, pq0:512]: e.tensor_tensor(o, i_, p_, ALU.add),
                          reads=["ps%d" % cb, cpn + "a", cpn + "b"], writes=[cn + "b"])

            def sX(n):
                t = tiles[n]
                q0, pq0 = t["q0"], t["pq0"]
                zb = n % NZ
                W = Wb[n % NW]
                wn = "W%d" % (n % NW)
                Cp = Cb[(n - 1) % 2]
                cpn = "C%d" % ((n - 1) % 2)
                c0 = q0
                if t["i"] >= 0:
                    P.add("dve", lambda e, o=W[:, q0:q0 + 128], i_=ps[zb][:, q0:q0 + 128], m=mk[:, 1, :]: e.tensor_tensor(o, i_, m, ALU.add),
                          reads=["ps%d" % zb, "mk", wn], writes=[wn])
                    c0 = q0 + 128
                if c0 < 512:
                    P.add("dve", lambda e, o=W[:, c0:512], i_=ps[zb][:, c0:512], c_=Cp[:, c0:512]: e.tensor_tensor(o, i_, c_, ALU.add),
                          reads=["ps%d" % zb, cpn + "a", cpn + "b", wn], writes=[wn])

            def sA(n):
                t = tiles[n]
                q0 = t["q0"]
                W = Wb[n % NW]
                wn = "W%d" % (n % NW)
                P.add("act", lambda e, o=Ab[n % NA][:, q0:512], i_=W[:, q0:512]: e.activation(o, i_, AF.Exp), reads=[wn], writes=["A%d" % (n % NA)])

            def sV(n):
                t = tiles[n]
                c, r, g, kt, q0 = t["c"], t["r"], t["g"], t["kt"], t["q0"]
                rows = slice(r * 64, (r + 1) * 64)
                vS = vSb[c % 2]
                P.add("pe", lambda e, o=ps[7][:, q0:512], l_=vS[:, kt, :], r_=Ab[n % NA][:, q0:512], st=t["first"], sp_=t["last"]:
                      e.matmul(o, l_, r_, start=st, stop=sp_, skip_group_check=True),
                      reads=["A%d" % (n % NA), "v%d_%d" % (c % 2, kt // 4)], writes=["ps7"])
                if t["last"]:
                    sl = slice(g * 512, (g + 1) * 512)
                    P.add("act", lambda e, o=oT[rows, c, sl], i_=ps[7][rows, :]: e.activation(o, i_, AF.Identity),
                          reads=["ps7"], writes=["o%d_%d_%d" % (c, g, r)])

            run_pipeline(N, list(zip([0, 1, 2, 2, 3, 3, 3, 4, 5], [sZ, sE, sM, sL, sT, sC, sX, sA, sV])))

        def attn_swa(l):
            nonlocal oT
            oT = A.alloc([8, S], BF16)
            bmh = [A.alloc([2, 2, 128], F32) for _ in range(2)]
            wts = [A.alloc([8, 384], BF16) for _ in range(2)]
            qz = [A.alloc([S], BF16) for _ in range(2)]
            P.add("pool", lambda e: e.memset(qz[0][64:128, :], 0.0), writes=["qz0pad"])
            P.add("pool", lambda e: e.memset(qz[1][0:64, :], 0.0), writes=["qz1pad"])
            kT = A.alloc([S], BF16)
            vSb = [A.alloc([16, 128], BF16) for _ in range(2)]
            Tt = [[A.alloc([512], F32) for _ in range(2)] for _ in range(2)]
            Pt = [[A.alloc([512], BF16) for _ in range(2)] for _ in range(3)]
            rr = [A.alloc([512], F32) for _ in range(2)]
            P.add("act", lambda e: e.activation(small[:, ESINK:ESINK + 16], vecs[:, V_SINK:V_SINK + 16], AF.Exp), reads=["vecs", "small"], writes=["small"])

            def prefetch(c):
                load_w("pool", wts[c % 2], wd["wqkv%d" % l][c], "wt%d" % (c % 2))

            def prefetch_bm(c):
                P.dma("sp", bmh[c % 2][:, 0], bm_d[:, 0, 2 * c:2 * c + 2, :], writes=["bmh%da" % (c % 2)])
                P.dma("sp", bmh[c % 2][:, 1], bm_d[:, 2, 2 * c:2 * c + 2, :], writes=["bmh%db" % (c % 2)])

            its = [(c, r, qq) for c in range(8) for r in range(2) for qq in range(4)]
            N = len(its)

            def sS(n):
                c, r, qq = its[n]
                if r == 0 and qq == 0:
                    if c == 0:
                        prefetch(0)
                        prefetch_bm(0)
                    if c + 1 < 8:
                        prefetch(c + 1)
                    proj_qkv_head(wts[c % 2], "wt%d" % (c % 2), None, kT, vSb[(c // 2) % 2], 0.125, "", banks=(0, 1),
                                  vtag="v%d" % ((c // 2) % 2), qpad=qz, do_kv=(c % 2 == 0))
                if r == 1 and qq == 0 and c + 1 < 8:
                    prefetch_bm(c + 1)
                b0, b1 = 2 * (n % 2), 2 * (n % 2) + 1
                for t4 in range(4):
                    qt = qq * 4 + t4
                    cs = slice(t4 * 128, (t4 + 1) * 128)
                    qrd = ["q_%da" % qq, "q_%db" % qq, "qz0pad", "qz1pad"]
                    P.add("pe", lambda e, o=ps[b0][:, cs], l_=kT[:, qt * 128:(qt + 1) * 128], r_=qz[r][:, qt * 128:(qt + 1) * 128]:
                          e.matmul(o, l_, r_, start=True, stop=True), reads=["k_%d" % qq] + qrd, writes=["ps%d" % b0])
                    if qt > 0:
                        P.add("pe", lambda e, o=ps[b1][:, cs], l_=kT[:, (qt - 1) * 128:qt * 128], r_=qz[r][:, qt * 128:(qt + 1) * 128]:
                              e.matmul(o, l_, r_, start=True, stop=True), reads=["k_%d" % ((qt - 1) // 4)] + qrd, writes=["ps%d" % b1])

            def sB(n):
                c, r, qq = its[n]
                b0, b1 = 2 * (n % 2), 2 * (n % 2) + 1
                T0, T1 = Tt[n % 2]
                p0 = 0 if qq > 0 else 128
                np1 = (512 - p0) // 128
                bm = bmh[c % 2]
                P.add("dve", lambda e, o=T0.rearrange("p (a b) -> p a b", a=4), i_=ps[b0].rearrange("p (a b) -> p a b", a=4),
                      b=bm[:, 0, r, :].unsqueeze(1).broadcast_to([128, 4, 128]): e.tensor_tensor(o, i_, b, ALU.add),
                      reads=["ps%d" % b0, "bmh%da" % (c % 2)], writes=["Tt%d_0" % (n % 2)])
                P.add("dve", lambda e, o=T1[:, p0:512].rearrange("p (a b) -> p a b", a=np1), i_=ps[b1][:, p0:512].rearrange("p (a b) -> p a b", a=np1),
                      b=bm[:, 1, r, :].unsqueeze(1).broadcast_to([128, np1, 128]): e.tensor_tensor(o, i_, b, ALU.add),
                      reads=["ps%d" % b1, "bmh%db" % (c % 2)], writes=["Tt%d_1" % (n % 2)])

            def sP(n):
                c, r, qq = its[n]
                T0, T1 = Tt[n % 2]
                P0, P1 = Pt[n % 3]
                p0 = 0 if qq > 0 else 128
                P.add("act", lambda e, o=P0, i_=T0: e.activation(o, i_, AF.Exp), reads=["Tt%d_0" % (n % 2)], writes=["Pt%d_0" % (n % 3)])
                P.add("act", lambda e, o=P1[:, p0:512], i_=T1[:, p0:512]: e.activation(o, i_, AF.Exp), reads=["Tt%d_1" % (n % 2)], writes=["Pt%d_1" % (n % 3)])

            def sV(n):
                c, r, qq = its[n]
                P0, P1 = Pt[n % 3]
                ob, db = 4 + n % 2, 6 + n % 2
                vS = vSb[(c // 2) % 2]
                vt = "v%d" % ((c // 2) % 2)
                for t4 in range(4):
                    qt = qq * 4 + t4
                    cs = slice(t4 * 128, (t4 + 1) * 128)
                    has_prev = qt > 0
                    P.add("pe", lambda e, o=ps[ob][:, cs], l_=vS[:, qt, :], r_=P0[:, cs], sp_=(not has_prev):
                          e.matmul(o, l_, r_, start=True, stop=sp_), reads=["Pt%d_0" % (n % 3), vt + "_%d" % qq], writes=["ps%d" % ob])
                    if has_prev:
                        P.add("pe", lambda e, o=ps[ob][:, cs], l_=vS[:, qt - 1, :], r_=P1[:, cs]:
                              e.matmul(o, l_, r_, start=False, stop=True), reads=["Pt%d_1" % (n % 3), vt + "_%d" % ((qt - 1) // 4)], writes=["ps%d" % ob])
                    P.add("pe", lambda e, o=ps[db][:, cs], r_=P0[:, cs], sp_=(not has_prev):
                          e.matmul(o, ones_b, r_, start=True, stop=sp_), reads=["Pt%d_0" % (n % 3), "ones_b"], writes=["ps%d" % db])
                    if has_prev:
                        P.add("pe", lambda e, o=ps[db][:, cs], r_=P1[:, cs]:
                              e.matmul(o, ones_b, r_, start=False, stop=True), reads=["Pt%d_1" % (n % 3), "ones_b"], writes=["ps%d" % db])

            def sN(n):
                c, r, qq = its[n]
                hq = 2 * c + r
                rows = slice(r * 64, (r + 1) * 64)
                sl = slice(qq * 512, (qq + 1) * 512)
                ob, db = 4 + n % 2, 6 + n % 2
                rb = rr[n % 2]
                rn = "rr%d" % (n % 2)
                P.add("act", lambda e, o=rb, i_=ps[db], s_=small[:, ESINK + hq:ESINK + hq + 1]: e.activation(o, i_, AF.Ln, bias=s_, scale=1.0),
                      reads=["ps%d" % db, "small"], writes=[rn])
                P.add("act", lambda e, o=rb: e.activation(o, o, AF.Exp, scale=-1.0), reads=[rn], writes=[rn])
                P.add("dve", lambda e, o=oT[rows, c, sl], i_=ps[ob][rows, :], r_=rb[rows, :]: e.tensor_tensor(o, i_, r_, ALU.mult),
                      reads=["ps%d" % ob, rn], writes=["o%d_%d_%d" % (c, qq, r)])

            run_pipeline(N, [(0, sS), (0, sB), (1, sP), (1, sV), (2, sN)])

        def ffn(l):
            wup = [A.alloc([8, 256], BF16) for _ in range(3)]
            wdn = [A.alloc([6, D], BF16) for _ in range(2)]
            ag = A.alloc([6, S], BF16)
            Ub = [[A.alloc([514], F32) for _ in range(2)] for _ in range(2)]
            Tb = [[A.alloc([512], F32) for _ in range(2)] for _ in range(2)]
            Sg = [A.alloc([512], F32) for _ in range(2)]
            for j in range(2):
                load_w("pool", wup[j], wd["wup%d" % l][j], "wup%d" % j)
            j0_, nj_ = FFN_GROUPS[0]
            load_w("pool", wdn[0][:, 0:nj_, :], wd["wdn%d" % l][j0_:j0_ + nj_].rearrange("j p m -> p j m"), "wdn0")
            cnt = [0, 0]
            for gi, (j0, nj) in enumerate(FFN_GROUPS):
                ab = gi % 2
                if gi + 1 < len(FFN_GROUPS):
                    j1, nj1 = FFN_GROUPS[gi + 1]
                    load_w("pool", wdn[1 - ab][:, 0:nj1, :], wd["wdn%d" % l][j1:j1 + nj1].rearrange("j p m -> p j m"), "wdn%d" % (1 - ab))
                tl = [(jj, tg) for jj in range(nj) for tg in range(4)]
                base = cnt[0]
                cnt[0] += len(tl)

                def s_up(m):
                    jj, tg = tl[m]
                    j = j0 + jj
                    n = base + m
                    wb = j % 3
                    if tg == 0 and j + 2 < NJ:
                        load_w("pool", wup[(j + 2) % 3], wd["wup%d" % l][j + 2], "wup%d" % ((j + 2) % 3))
                    sl = slice(tg * 512, (tg + 1) * 512)
                    pb = (n % 2) * 2
                    for half in (1, 0):
                        for kc in range(8):
                            P.add("pe", lambda e, o=ps[pb + half], w=wup[wb][:, kc, half * 128:(half + 1) * 128], r=hT[:, kc, sl], st=(kc == 0), sp_=(kc == 7):
                                  e.matmul(o, w, r, start=st, stop=sp_), reads=["wup%d" % wb, "h%d_%d" % (kc, tg)], writes=["ps%d" % (pb + half)])

                def s_conv(m):
                    jj, tg = tl[m]
                    j = j0 + jj
                    n = base + m
                    b = n % 2
                    pb = (n % 2) * 2
                    cv = V_CONV + (l * NJ + j) * 8
                    taps = []
                    for half in range(2):
                        U = ps[pb + half]
                        un = "ps%d" % (pb + half)
                        ub, T = Ub[half][b], Tb[half][b]
                        umn, uhn, tn = "Um%d_%d" % (half, b), "Uh%d_%d" % (half, b), "T%d_%d" % (half, b)
                        w0 = vecs[:, cv + half * 4 + 0:cv + half * 4 + 1]
                        w1 = vecs[:, cv + half * 4 + 1:cv + half * 4 + 2]
                        w2 = vecs[:, cv + half * 4 + 2:cv + half * 4 + 3]
                        bb = vecs[:, cv + half * 4 + 3:cv + half * 4 + 4]
                        P.add("act", lambda e, o=ub[:, 2:514], i=U: e.activation(o, i, AF.Identity), reads=[un], writes=[umn])
                        P.add("act", lambda e, o=T, i=U, s_=w2, b_=bb: e.activation(o, i, AF.Identity, bias=b_, scale=s_), reads=[un, "vecs"], writes=[tn])
                        if tg == 0:
                            P.add("pool", lambda e, o=ub[:, 0:2]: e.memset(o, 0.0), writes=[uhn])
                        else:
                            P.add("pool", lambda e, o=ub[:, 0:2], i=Ub[half][1 - b][:, 512:514]: e.tensor_copy(o, i),
                                  reads=["Um%d_%d" % (half, 1 - b)], writes=[uhn])
                        taps.append((ub, T, umn, uhn, tn, w0, w1))
                    for which in (1, 0):
                        for (ub, T, umn, uhn, tn, w0, w1) in taps:
                            if which == 1:
                                P.add("dve", lambda e, o=T, i=ub[:, 1:513], s_=w1: e.scalar_tensor_tensor(o, i, s_, o, ALU.mult, ALU.add),
                                      reads=[umn, uhn, "vecs", tn], writes=[tn])
                            else:
                                P.add("dve", lambda e, o=T, i=ub[:, 0:512], s_=w0: e.scalar_tensor_tensor(o, i, s_, o, ALU.mult, ALU.add),
                                      reads=[umn, uhn, "vecs", tn], writes=[tn])

                def s_gate(m):
                    jj, tg = tl[m]
                    n = base + m
                    b = n % 2
                    sl = slice(tg * 512, (tg + 1) * 512)
                    P.add("act", lambda e, o=Sg[b], i=Tb[0][b]: e.activation(o, i, AF.Silu), reads=["T0_%d" % b], writes=["Sg%d" % b])
                    P.add("pool", lambda e, o=ag[:, jj, sl], a_=Sg[b], b_=Tb[1][b]: e.tensor_tensor(o, a_, b_, ALU.mult),
                          reads=["Sg%d" % b, "T1_%d" % b], writes=["a_%d_%d" % (jj, tg)])

                run_pipeline(len(tl), [(0, s_up), (0, s_conv), (1, s_gate)])
                for dc in range(8):
                    for tg in range(4):
                        sl = slice(tg * 512, (tg + 1) * 512)
                        bank = 4 + (cnt[1] % 2)
                        cnt[1] += 1
                        for jj in range(nj):
                            P.add("pe", lambda e, o=ps[bank], w=wdn[ab][:, jj, dc * 128:(dc + 1) * 128], r=ag[:, jj, sl], st=(jj == 0), sp_=(jj == nj - 1):
                                  e.matmul(o, w, r, start=st, stop=sp_), reads=["wdn%d" % ab, "a_%d_%d" % (jj, tg)], writes=["ps%d" % bank])
                        P.add("dve", lambda e, o=xT[:, dc, sl], i=ps[bank]: e.tensor_tensor(o, i, o, ALU.add),
                              reads=["ps%d" % bank, "x%d_%d" % (dc, tg)], writes=["x%d_%d" % (dc, tg)])

        def final_norm():
            sq = [A.alloc([512], BF16) for _ in range(3)]
            rstd = [A.alloc([512], F32) for _ in range(2)]
            yb = [A.alloc([512], F32) for _ in range(4)]
            k = 0
            m = 0
            outs = []
            for tg in range(4):
                sl = slice(tg * 512, (tg + 1) * 512)
                bank = 6 + (tg % 2)
                for c in range(8):
                    b = k % 3
                    k += 1
                    P.add("act", lambda e, o=sq[b], i=xT[:, c, sl]: e.activation(o, i, AF.Square), reads=["x%d_%d" % (c, tg)], writes=["sq%d" % b])
                    P.add("pe", lambda e, o=ps[bank], r=sq[b], st=(c == 0), sp_=(c == 7): e.matmul(o, ones_dmb, r, start=st, stop=sp_),
                          reads=["sq%d" % b, "ones_dm"], writes=["ps%d" % bank])
                rb = rstd[tg % 2]
                P.add("act", lambda e, o=rb, i=ps[bank]: e.activation(o, i, AF.Ln, bias=eps_ap, scale=1.0),
                      reads=["ps%d" % bank, "small"], writes=["rstd%d" % (tg % 2)])
                P.add("act", lambda e, o=rb: e.activation(o, o, AF.Exp, scale=-0.5),
                      reads=["rstd%d" % (tg % 2)], writes=["rstd%d" % (tg % 2)])
                for c in range(8):
                    y = yb[m % 4]
                    yn = "y%d" % (m % 4)
                    m += 1
                    P.add("dve", lambda e, o=y, i=xT[:, c, sl], g=vecs[:, V_FIN + c:V_FIN + c + 1], r=rb: e.scalar_tensor_tensor(o, i, g, r, ALU.mult, ALU.mult),
                          reads=["x%d_%d" % (c, tg), "vecs", "rstd%d" % (tg % 2)], writes=[yn])
                    on = "out%d" % ((m - 1) % 4)
                    P.dma("sp", yout[c * 128:(c + 1) * 128, sl], y, reads=[yn], writes=[on])
                    if on not in outs:
                        outs.append(on)
            return outs

        oT = None
        for l in layers:
            mixer = l % 3
            A.off = persist_end
            P.barrier()
            rmsnorm_to_h(V_AN + 8 * l, "a%d" % l)
            if mixer == 0:
                attn_da(l)
            elif mixer == 1:
                attn_sb(l)
            else:
                attn_swa(l)
            oproj(l, mixer != 0)
            A.off = persist_end
            P.barrier()
            rmsnorm_to_h(V_FN + 8 * l, "f%d" % l)
            if not DEBUG_NOFFN:
                ffn(l)
        outs = []
        A.off = persist_end
        P.barrier()
        if do_final:
            outs = final_norm()
        else:
            for c in range(8):
                on = "out%d" % c
                P.dma("sp", yout[c * 128:(c + 1) * 128, :], xT[:, c, :], reads=["x%d_%d" % (c, t) for t in range(4)], writes=[on])
                outs.append(on)
        P.finish(outs)
        P.emit()
    return nc


LAYER_GROUPS = [[0, 1, 2, 3]]


def kernel(**inputs):
    x = np.asarray(inputs["x"], np.float32)
    B = x.shape[0]
    shared = _prep_shared(inputs)
    cur = [np.ascontiguousarray(x[b].T) for b in range(B)]
    for gi, layers in enumerate(LAYER_GROUPS):
        do_final = (gi == len(LAYER_GROUPS) - 1)
        nc = build_nc(layers, do_final)
        lw = {}
        for l in layers:
            lw.update(_prep_layer(inputs, l))
        in_maps = []
        for b in range(B):
            m = dict(shared)
            m.update(lw)
            m["xin"] = cur[b]
            in_maps.append(m)
        res = run_bass_kernel_spmd(nc, in_maps, core_ids=list(range(B)))
        cur = [np.asarray(res.results[b]["yout"], np.float32) for b in range(B)]
    out = np.stack([c.T for c in cur], axis=0)
    return np.ascontiguousarray(out.astype(np.float32))
```

```python
import math
import numpy as np
from contextlib import ExitStack
import concourse.bass as bass
import concourse.mybir as mybir
from concourse.bass_utils import run_bass_kernel_spmd

F32 = mybir.dt.float32
BF16 = mybir.dt.bfloat16
AF = mybir.ActivationFunctionType
ALU = mybir.AluOpType

D = 1024
S = 2048
DEPTH = 4
DFF = 2752
NJ = 22
EPS = 1e-6
NEGM = -30000.0
FFN_GROUPS = [(0, 6), (6, 6), (12, 5), (17, 5)]
DEBUG_NOFFN = False
FFN_PRE = 2
ATTACH_WAIT = True


class _I:
    __slots__ = ("eng", "fn", "deps", "dma", "sig", "sem", "val", "key")


class Prog:
    ENGS = ("pe", "act", "dve", "pool", "sp")
    SEM_LIMIT = 20000

    def __init__(self, nc, same_eng_sync=True):
        self.nc = nc
        self.same = same_eng_sync
        self.streams = {e: [] for e in self.ENGS}
        self.last_writer = {}
        self.readers = {}
        self.n = 0

    def add(self, eng, fn, reads=(), writes=(), dma=False, nodep=()):
        ins = _I()
        ins.eng, ins.fn, ins.dma, ins.sig, ins.sem, ins.val = eng, fn, dma, False, None, 0
        ins.key = writes[0] if (dma and writes) else None
        deps = set()
        for r in reads:
            w = self.last_writer.get(r)
            if w is not None:
                deps.add(w)
        reads = list(reads) + list(nodep)
        for r in writes:
            w = self.last_writer.get(r)
            if w is not None:
                deps.add(w)
            for rd in self.readers.get(r, ()):
                deps.add(rd)
        for r in reads:
            self.readers.setdefault(r, []).append(ins)
        for r in writes:
            self.last_writer[r] = ins
            self.readers[r] = []
        deps.discard(ins)
        ins.deps = deps
        self.streams[eng].append(ins)
        self.n += 1
        return ins

    def dma(self, q, out, in_, reads=(), writes=()):
        return self.add(q, lambda e: e.dma_start(out=out, in_=in_), reads=reads, writes=writes, dma=True)

    def barrier(self):
        lasts = set()
        for w in self.last_writer.values():
            lasts.add(w)
        for rl in self.readers.values():
            for r in rl:
                lasts.add(r)
        for e in self.ENGS:
            if self.streams[e]:
                lasts.add(self.streams[e][-1])
        for e in self.ENGS:
            ins = _I()
            ins.eng, ins.fn, ins.dma, ins.sig, ins.sem, ins.val, ins.key = e, None, False, False, None, 0, None
            ins.deps = set(lasts)
            self.streams[e].append(ins)
        self.last_writer = {}
        self.readers = {}

    def finish(self, outs):
        self.add("sp", None, reads=list(outs))

    def _skip(self, ins, d):
        if d.dma:
            return False
        if d.eng == ins.eng:
            if d.eng == "pe":
                return True
            if not self.same:
                return True
        return False

    def emit(self):
        nc = self.nc
        for e in self.ENGS:
            for ins in self.streams[e]:
                for d in ins.deps:
                    if d.fn is not None and not self._skip(ins, d):
                        d.sig = True
        semnames = []
        dmacnt = {}
        for e in self.ENGS:
            cnt, idx = 0, 0
            for ins in self.streams[e]:
                if ins.dma:
                    k = ("dma", ins.key)
                    dmacnt[k] = dmacnt.get(k, 0) + 1
                    ins.sem, ins.val = k, 16 * dmacnt[k]
                    if k not in semnames:
                        semnames.append(k)
                elif ins.sig:
                    cnt += 1
                    if cnt > self.SEM_LIMIT:
                        idx += 1
                        cnt = 1
                    ins.sem, ins.val = (e, idx), cnt
                    if ins.sem not in semnames:
                        semnames.append(ins.sem)
        self.nsems = len(semnames)
        with ExitStack() as es:
            sems = {}
            for i, k in enumerate(semnames):
                sems[k] = es.enter_context(nc.semaphore("s%d" % i))
            block = es.enter_context(nc.Block())

            def replay(ename):
                def body(eng):
                    known = {}
                    for ins in self.streams[ename]:
                        need = {}
                        for d in ins.deps:
                            if d.fn is None or self._skip(ins, d):
                                continue
                            if need.get(d.sem, 0) < d.val:
                                need[d.sem] = d.val
                        todo = [(k, v) for k, v in need.items() if known.get(k, 0) < v]
                        for k, v in todo:
                            known[k] = v
                        attach = None
                        if ATTACH_WAIT and ins.fn is not None and todo:
                            attach = todo.pop()
                        for k, v in todo:
                            eng.wait_ge(sems[k], v)
                        if ins.fn is not None:
                            bi = ins.fn(eng)
                            if attach is not None:
                                bi._wait_ge(sems[attach[0]], attach[1])
                            if ins.dma:
                                bi.then_inc(sems[ins.sem], 16)
                            elif ins.sig:
                                bi.then_inc(sems[ins.sem], 1)
                return body

            block.tensor(replay("pe"))
            block.scalar(replay("act"))
            block.vector(replay("dve"))
            block.gpsimd(replay("pool"))
            block.sync(replay("sp"))


def _t5_bucket(dist):
    max_exact = 16
    d = np.maximum(dist, 0)
    large = max_exact + (np.log(np.maximum(d, 1).astype(np.float32) / np.float32(max_exact))
                         / np.float32(math.log(128 / max_exact)) * np.float32(32 - max_exact)).astype(np.int32)
    large = np.minimum(large, 31)
    return np.where(d < max_exact, d, large)


V_AN = 0
V_FN = 32
V_FIN = 64
V_SUB = 72
V_SINK = 74
V_BC = 90
V_LAM = 106
V_CONV = 106 + 512
NV = V_CONV + 4 * NJ * 8


def _prep_shared(inp):
    rel_bias = np.asarray(inp["rel_bias"], np.float32)
    vecs = np.zeros((128, NV), np.float32)
    pc = lambda v: np.asarray(v, np.float32).reshape(8, 128).T
    for l in range(DEPTH):
        vecs[:, V_AN + 8 * l:V_AN + 8 * l + 8] = pc(inp["attn_norm"][l])
        vecs[:, V_FN + 8 * l:V_FN + 8 * l + 8] = pc(inp["ffn_norm"][l])
    vecs[:, V_FIN:V_FIN + 8] = pc(inp["final_norm"])
    vecs[:, V_SUB:V_SUB + 2] = np.asarray(inp["da_subln"], np.float32).T
    vecs[:, V_SINK:V_SINK + 16] = np.asarray(inp["sw_sinks"], np.float32).reshape(1, 16)
    vecs[:, V_BC:V_BC + 16] = rel_bias[31][None, :]
    vecs[:, V_LAM:V_LAM + 512] = np.asarray(inp["da_lambda"], np.float32).reshape(1, 512)
    cw = np.asarray(inp["ffn_conv_w"], np.float32)
    cb = np.asarray(inp["ffn_conv_b"], np.float32)
    for l in range(DEPTH):
        for half in range(2):
            w = np.zeros((3, NJ * 128), np.float32)
            b = np.zeros((NJ * 128,), np.float32)
            w[:, :DFF] = cw[l][:, half * DFF:(half + 1) * DFF]
            b[:DFF] = cb[l][half * DFF:(half + 1) * DFF]
            w = w.reshape(3, NJ, 128)
            b = b.reshape(NJ, 128)
            for j in range(NJ):
                base = V_CONV + (l * NJ + j) * 8 + half * 4
                vecs[:, base + 0] = w[0, j]
                vecs[:, base + 1] = w[1, j]
                vecs[:, base + 2] = w[2, j]
                vecs[:, base + 3] = b[j]
    kk = np.arange(128)[:, None]
    qq = np.arange(128)[None, :]
    d0 = qq - kk
    d1 = qq - kk + 128
    b0 = rel_bias[_t5_bucket(d0)]
    b1 = rel_bias[_t5_bucket(d1)]
    bm = np.zeros((128, 3, 16, 128), np.float32)
    neg = np.float32(NEGM)
    bm[:, 0] = np.where((d0 >= 0)[:, :, None], b0, neg).transpose(0, 2, 1)
    bm[:, 1] = b1.transpose(0, 2, 1)
    bm[:, 2] = np.where((d1 < 128)[:, :, None], b1, neg).transpose(0, 2, 1)
    masks = np.zeros((128, 2, 128), np.float32)
    masks[:, 0] = (kk < qq).astype(np.float32)
    masks[:, 1] = np.where(kk < qq, np.float32(0), neg)
    tri = (kk >= qq).astype(np.float32)
    consts = np.zeros((128, 2, 128), np.float32)
    consts[:, 0] = 1.0
    consts[:, 1] = tri
    return dict(vecs=vecs, bm=bm, masks=masks, consts=consts)


def _prep_layer(inp, l):
    mixer, slot = l % 3, l // 3
    out = {}
    if mixer in (0, 1):
        w = np.asarray(inp["da_w_qkv" if mixer == 0 else "sb_w_qkv"][slot], np.float32)
        q = w[:, 0:1024].reshape(D, 8, 128)
        k = w[:, 1024:2048].reshape(D, 8, 128)
        v = w[:, 2048:3072].reshape(D, 8, 128)
        wh = np.concatenate([q, k, v], axis=2)
        wh = wh.reshape(8, 128, 8, 384).transpose(2, 1, 0, 3)
        out["wqkv%d" % l] = np.ascontiguousarray(wh)
    else:
        w = np.asarray(inp["sw_w_qkv"][slot], np.float32)
        q = w[:, 0:1024]
        k = w[:, 1024:1280].reshape(D, 4, 64)
        v = w[:, 1280:1536].reshape(D, 4, 64)
        kd = np.concatenate([k, k], axis=2)
        vd = np.concatenate([v, v], axis=2)
        gi = np.arange(8) // 2
        wh = np.concatenate([q.reshape(D, 8, 128), kd[:, gi, :], vd[:, gi, :]], axis=2)
        wh = wh.reshape(8, 128, 8, 384).transpose(2, 1, 0, 3)
        out["wqkv%d" % l] = np.ascontiguousarray(wh)
    wo = np.asarray(inp["w_o"][l], np.float32)
    wo = wo.reshape(8, 128, 8, 128).transpose(2, 1, 0, 3)
    out["wo%d" % l] = np.ascontiguousarray(wo)
    wu = np.asarray(inp["ffn_w_up"][l], np.float32)
    g = np.zeros((D, NJ * 128), np.float32)
    v = np.zeros((D, NJ * 128), np.float32)
    g[:, :DFF] = wu[:, :DFF]
    v[:, :DFF] = wu[:, DFF:]
    gv = np.concatenate([g.reshape(D, NJ, 128), v.reshape(D, NJ, 128)], axis=2)
    gv = gv.reshape(8, 128, NJ, 256).transpose(2, 1, 0, 3)
    out["wup%d" % l] = np.ascontiguousarray(gv)
    wd = np.zeros((NJ * 128, D), np.float32)
    wd[:DFF] = np.asarray(inp["ffn_w_down"][l], np.float32)
    out["wdn%d" % l] = np.ascontiguousarray(wd.reshape(NJ, 128, D))
    return out


def _lambda_init(layer):
    return 0.8 - 0.6 * math.exp(-0.3 * layer)


class Pool_:
    def __init__(self, ap_u16, nbytes):
        self.ap = ap_u16
        self.n = nbytes
        self.off = 0

    def alloc(self, free_shape, dt):
        esz = 4 if dt == F32 else 2
        nel = int(np.prod(free_shape))
        nb = nel * esz
        off = (self.off + 63) // 64 * 64
        assert off + nb <= self.n, ("SBUF pool overflow", off, nb, self.n)
        self.off = off + nb
        a = self.ap[:, off // 2:(off + nb) // 2]
        if dt == F32:
            a = a.bitcast(F32)
        if len(free_shape) == 2:
            a = a.rearrange("p (a b) -> p a b", a=free_shape[0])
        elif len(free_shape) == 3:
            a = a.rearrange("p (a b c) -> p a b c", a=free_shape[0], b=free_shape[1])
        return a


def build_nc(layers, do_final):
    nc = bass.Bass("TRN2", target_bir_lowering=False)
    nc.allow_low_precision("bf16 matmul operands with fp32 PSUM accumulation by design")
    dram = lambda n, s, k="ExternalInput": nc.dram_tensor(n, list(s), F32, kind=k).ap()
    xin = dram("xin", [D, S])
    yout = dram("yout", [D, S], "ExternalOutput")
    vecs_d = dram("vecs", [128, NV])
    bm_d = dram("bm", [128, 3, 16, 128])
    masks_d = dram("masks", [128, 2, 128])
    consts_d = dram("consts", [128, 2, 128])
    wd = {}
    for l in layers:
        mixer = l % 3
        wd["wqkv%d" % l] = dram("wqkv%d" % l, [8, 128, 8, 384])
        wd["wo%d" % l] = dram("wo%d" % l, [8, 128, 8, 128])
        wd["wup%d" % l] = dram("wup%d" % l, [NJ, 128, 8, 256])
        wd["wdn%d" % l] = dram("wdn%d" % l, [NJ, 128, D])

    with ExitStack() as es:
        POOLB = 212480
        pool_t = es.enter_context(nc.sbuf_tensor("pool", [128, POOLB // 2], BF16))
        A = Pool_(pool_t[:, :], POOLB)
        ps = [es.enter_context(nc.psum_tensor("ps%d" % i, [128, 512], F32))[:, :] for i in range(8)]
        P = Prog(nc)

        xT = A.alloc([8, S], F32)
        hT = A.alloc([8, S], BF16)
        vecs = A.alloc([NV], F32)
        ones_b = A.alloc([128], BF16)
        tri_b = A.alloc([128], BF16)
        nones_b = A.alloc([128], BF16)
        ntri_b = A.alloc([128], BF16)
        ones_dmb = A.alloc([128], BF16)
        ones_eb = A.alloc([128], BF16)
        ones_dm = A.alloc([128], F32)
        ones_e = A.alloc([128], F32)
        cst_f = A.alloc([2, 128], F32)
        small = A.alloc([64], F32)
        persist_end = A.off

        eps_ap = small[:, 4:5]
        NEGLAM = 0
        SUBS = 2
        ESINK = 8

        for c in range(8):
            P.dma("sp", xT[:, c, :], xin[c * 128:(c + 1) * 128, :], writes=["x%d_%d" % (c, t) for t in range(4)])
        P.dma("sp", vecs, vecs_d, writes=["vecs"])
        P.dma("sp", cst_f, consts_d, writes=["cstf"])
        P.add("dve", lambda e: e.memset(small[:, 4:5], EPS), writes=["small"])
        P.add("dve", lambda e: e.tensor_copy(ones_b, cst_f[:, 0, :]), reads=["cstf"], writes=["ones_b"])
        P.add("dve", lambda e: e.tensor_copy(tri_b, cst_f[:, 1, :]), reads=["cstf"], writes=["tri_b"])
        P.add("dve", lambda e: e.tensor_scalar(nones_b, cst_f[:, 0, :], -1.0, None, ALU.mult), reads=["cstf"], writes=["ones_b"])
        P.add("dve", lambda e: e.tensor_scalar(ntri_b, cst_f[:, 1, :], -1.0, None, ALU.mult), reads=["cstf"], writes=["tri_b"])
        P.add("dve", lambda e: e.tensor_scalar(ones_dm, cst_f[:, 0, :], 1.0 / D, None, ALU.mult), reads=["cstf"], writes=["ones_dm"])
        P.add("dve", lambda e: e.tensor_scalar(ones_dmb, cst_f[:, 0, :], 1.0 / D, None, ALU.mult), reads=["cstf"], writes=["ones_dm"])
        P.add("dve", lambda e: e.tensor_scalar(ones_eb, cst_f[:, 0, :], 1.0 / 128, None, ALU.mult), reads=["cstf"], writes=["ones_e"])
        P.add("dve", lambda e: e.tensor_scalar(ones_e, cst_f[:, 0, :], 1.0 / 128, None, ALU.mult), reads=["cstf"], writes=["ones_e"])

        def rmsnorm_to_h(gcol, tag):
            sq = [A.alloc([512], BF16) for _ in range(4)]
            rstd = [A.alloc([512], F32) for _ in range(2)]
            k = 0
            for tg in range(4):
                sl = slice(tg * 512, (tg + 1) * 512)
                bank = 6 + (tg % 2)
                for c in range(8):
                    b = k % 4
                    k += 1
                    eng = "act"
                    if eng == "act":
                        P.add("act", lambda e, o=sq[b], i=xT[:, c, sl]: e.activation(o, i, AF.Square),
                              reads=["x%d_%d" % (c, tg)], writes=["sq%d" % b])
                    else:
                        P.add("pool", lambda e, o=sq[b], i=xT[:, c, sl]: e.tensor_tensor(o, i, i, ALU.mult),
                              reads=["x%d_%d" % (c, tg)], writes=["sq%d" % b])
                    P.add("pe", lambda e, o=ps[bank], r=sq[b], st=(c == 0), sp_=(c == 7): e.matmul(o, ones_dmb, r, start=st, stop=sp_),
                          reads=["sq%d" % b, "ones_dm"], writes=["ps%d" % bank])
                rb = rstd[tg % 2]
                P.add("act", lambda e, o=rb, i=ps[bank]: e.activation(o, i, AF.Ln, bias=eps_ap, scale=1.0),
                      reads=["ps%d" % bank, "small"], writes=["rstd%d" % (tg % 2)])
                P.add("act", lambda e, o=rb: e.activation(o, o, AF.Exp, scale=-0.5),
                      reads=["rstd%d" % (tg % 2)], writes=["rstd%d" % (tg % 2)])
                for c in range(8):
                    P.add("dve", lambda e, o=hT[:, c, sl], i=xT[:, c, sl], g=vecs[:, gcol + c:gcol + c + 1], r=rb:
                          e.scalar_tensor_tensor(o, i, g, r, ALU.mult, ALU.mult),
                          reads=["x%d_%d" % (c, tg), "vecs", "rstd%d" % (tg % 2)], writes=["h%d_%d" % (c, tg)])

        H_ALL = ["h%d_%d" % (c, t) for c in range(8) for t in range(4)]

        def load_w(q, dst, src, name):
            P.dma(q, dst, src, writes=[name])

        def proj_qkv_head(wt, wname, qT, kT, vS, qscale, tag, banks, vtag="v", kpad=None, qpad=None, do_kv=True, evac="act"):
            bi = 0
            for which, dst in ((0, qT), (1, kT)):
                if which == 1 and not do_kv:
                    continue
                for tg in range(4):
                    sl = slice(tg * 512, (tg + 1) * 512)
                    bank = banks[bi % len(banks)]
                    bi += 1
                    for kc in range(8):
                        P.add("pe", lambda e, o=ps[bank], w=wt[:, kc, which * 128:(which + 1) * 128], r=hT[:, kc, sl], st=(kc == 0), sp_=(kc == 7):
                              e.matmul(o, w, r, start=st, stop=sp_),
                              reads=[wname, "h%d_%d" % (kc, tg)], writes=["ps%d" % bank])
                    rn = ("q" if which == 0 else "k") + tag + "_%d" % tg
                    if which == 0 and qpad is not None:
                        P.add("act", lambda e, o=qpad[0][0:64, sl], i=ps[bank][0:64, :]: e.activation(o, i, AF.Identity, scale=qscale),
                              reads=["ps%d" % bank], writes=[rn + "a"])
                        P.add("act", lambda e, o=qpad[1][64:128, sl], i=ps[bank][64:128, :]: e.activation(o, i, AF.Identity, scale=qscale),
                              reads=["ps%d" % bank], writes=[rn + "b"])
                    elif which == 0 and qscale != 1.0:
                        if evac == "dve":
                            P.add("dve", lambda e, o=dst[:, sl], i=ps[bank]: e.tensor_scalar(o, i, qscale, None, ALU.mult),
                                  reads=["ps%d" % bank], writes=[rn])
                        else:
                            P.add("act", lambda e, o=dst[:, sl], i=ps[bank]: e.activation(o, i, AF.Identity, scale=qscale),
                                  reads=["ps%d" % bank], writes=[rn])
                    elif which == 1 and kpad is not None:
                        P.add("dve", lambda e, o=kpad[0][0:64, sl], i=ps[bank][0:64, :]: e.tensor_copy(o, i),
                              reads=["ps%d" % bank], writes=[rn + "a"])
                        if evac == "dve":
                            P.add("dve", lambda e, o=kpad[1][64:128, sl], i=ps[bank][64:128, :]: e.tensor_copy(o, i),
                                  reads=["ps%d" % bank], writes=[rn + "b"])
                        else:
                            P.add("act", lambda e, o=kpad[1][64:128, sl], i=ps[bank][64:128, :]: e.activation(o, i, AF.Identity),
                                  reads=["ps%d" % bank], writes=[rn + "b"])
                    else:
                        P.add("dve", lambda e, o=dst[:, sl], i=ps[bank]: e.tensor_copy(o, i),
                              reads=["ps%d" % bank], writes=[rn])
            for t4 in range(4 if do_kv else 0):
                bank = banks[bi % len(banks)]
                bi += 1
                for tt in range(4):
                    tok = slice((t4 * 4 + tt) * 128, (t4 * 4 + tt + 1) * 128)
                    for kc in range(8):
                        P.add("pe", lambda e, o=ps[bank][:, tt * 128:(tt + 1) * 128], l=hT[:, kc, tok], r=wt[:, kc, 256:384], st=(kc == 0), sp_=(kc == 7):
                              e.matmul(o, l, r, start=st, stop=sp_),
                              reads=[wname, "h%d_%d" % (kc, t4)], writes=["ps%d" % bank])
                if evac == "dve":
                    P.add("dve", lambda e, o=vS[:, t4 * 4:(t4 + 1) * 4, :], i=ps[bank].rearrange("p (a b) -> p a b", a=4): e.tensor_copy(o, i),
                          reads=["ps%d" % bank], writes=[vtag + "_%d" % t4])
                else:
                    P.add("act", lambda e, o=vS[:, t4 * 4:(t4 + 1) * 4, :], i=ps[bank].rearrange("p (a b) -> p a b", a=4): e.activation(o, i, AF.Identity),
                          reads=["ps%d" % bank], writes=[vtag + "_%d" % t4])

        def oproj(l, parts):
            wo = [A.alloc([8, 128], BF16) for _ in range(2)]
            for dc in range(8):
                wb = dc % 2
                load_w("pool", wo[wb], wd["wo%d" % l][dc], "wo%d" % wb)
                for tg in range(4):
                    sl = slice(tg * 512, (tg + 1) * 512)
                    bank = 6 + ((dc * 4 + tg) % 2)
                    for kc in range(8):
                        P.add("pe", lambda e, o=ps[bank], w=wo[wb][:, kc, :], r=oT[:, kc, sl], st=(kc == 0), sp_=(kc == 7):
                              e.matmul(o, w, r, start=st, stop=sp_),
                              reads=["wo%d" % wb] + (["o%d_%d_0" % (kc, tg), "o%d_%d_1" % (kc, tg)] if parts else ["o%d_%d" % (kc, tg)]), writes=["ps%d" % bank])
                    P.add("dve", lambda e, o=xT[:, dc, sl], i=ps[bank]: e.tensor_tensor(o, i, o, ALU.add),
                          reads=["ps%d" % bank, "x%d_%d" % (dc, tg)], writes=["x%d_%d" % (dc, tg)])

        def run_pipeline(N, stages):
            maxlag = max(lg for lg, _ in stages)
            for it in range(N + maxlag):
                for lg, fn in stages:
                    n = it - lg
                    if 0 <= n < N:
                        fn(n)

        def attn_da(l):
            slot = l // 3
            li = _lambda_init(l)
            nonlocal oT
            oT = A.alloc([8, S], BF16)
            bmh = [A.alloc([2, 2, 128], F32) for _ in range(2)]
            wts = [A.alloc([8, 384], BF16) for _ in range(2)]
            qT = A.alloc([S], BF16)
            kT = None
            kz = [A.alloc([S], BF16) for _ in range(2)]
            P.add("pool", lambda e: e.memset(kz[0][64:128, :], 0.0), writes=["kz0pad"])
            P.add("pool", lambda e: e.memset(kz[1][0:64, :], 0.0), writes=["kz1pad"])
            vSb = [A.alloc([16, 128], BF16) for _ in range(2)]
            NS, NP = 3, 3
            Pb = [A.alloc([512], BF16) for _ in range(NP)]
            tD = [A.alloc([256], F32) for _ in range(2)]
            tmp = [A.alloc([512], F32) for _ in range(5)]
            sqb = tmp[3].bitcast(BF16)[:, 0:512]
            lam = vecs[:, V_LAM + slot * 256:V_LAM + slot * 256 + 256]
            P.add("dve", lambda e: e.tensor_tensor(tmp[0][:, 0:64], lam[:, 0:64], lam[:, 64:128], ALU.mult), reads=["vecs"], writes=["tmp0"])
            P.add("dve", lambda e: e.tensor_tensor(tmp[0][:, 64:128], lam[:, 128:192], lam[:, 192:256], ALU.mult), reads=["vecs", "tmp0"], writes=["tmp0"])
            P.add("dve", lambda e: e.reduce_sum(tmp[1][:, 0:2], tmp[0][:, 0:128].rearrange("p (a b) -> p a b", a=2), mybir.AxisListType.X),
                  reads=["tmp0"], writes=["tmp1"])
            P.add("act", lambda e: e.activation(tmp[1][:, 2:4], tmp[1][:, 0:2], AF.Exp), reads=["tmp1"], writes=["tmp1"])
            P.add("dve", lambda e: e.scalar_tensor_tensor(small[:, NEGLAM + slot:NEGLAM + slot + 1], tmp[1][:, 3:4], -li, tmp[1][:, 2:3], ALU.add, ALU.subtract),
                  reads=["tmp1"], writes=["small"])
            P.add("dve", lambda e: e.tensor_scalar(small[:, SUBS + slot:SUBS + slot + 1], vecs[:, V_SUB + slot:V_SUB + slot + 1], 1.0 - li, None, ALU.mult),
                  reads=["vecs", "small"], writes=["small"])
            neglam = small[:, NEGLAM + slot:NEGLAM + slot + 1]
            subs = small[:, SUBS + slot:SUBS + slot + 1]

            def prefetch(h):
                load_w("pool", wts[h % 2], wd["wqkv%d" % l][h], "wt%d" % (h % 2))

            def prefetch_bm(h):
                P.dma("sp", bmh[h % 2], bm_d[:, 0:2, 2 * h:2 * h + 2, :], writes=["bmh%d" % (h % 2)])
                for j_ in range(2):
                    ch_ = 2 * h + j_
                    P.add("pool", lambda e, o=bmh[h % 2][:, :, j_, :], c_=vecs[:, V_BC + ch_:V_BC + ch_ + 1]: e.tensor_scalar(o, o, c_, None, ALU.subtract),
                          reads=["vecs", "bmh%d" % (h % 2)], writes=["bmh%d" % (h % 2)])

            tiles = []
            gi = 0
            for h in range(8):
                for g in range(4):
                    for j in range(2):
                        for kt in range(4 * g + 4):
                            tiles.append(dict(h=h, g=g, j=j, kt=kt, gi=gi, first=(kt == 0), last=(kt == 4 * g + 3),
                                              head_first=(g == 0 and j == 0 and kt == 0)))
                        gi += 1
            N = len(tiles)
            deferred = []

            def sS(n):
                t = tiles[n]
                h, g, j, kt = t["h"], t["g"], t["j"], t["kt"]
                if t["head_first"]:
                    if h == 0:
                        prefetch(0)
                        prefetch_bm(0)
                    if h + 1 < 8:
                        prefetch(h + 1)
                    proj_qkv_head(wts[h % 2], "wt%d" % (h % 2), qT, kT, vSb[h % 2], 0.125, "", banks=(7, n % NS), vtag="v%d" % (h % 2), kpad=kz, evac="act")
                if g == 1 and j == 0 and kt == 0 and h + 1 < 8:
                    prefetch_bm(h + 1)
                i = kt - 4 * g
                q0 = max(i, 0) * 128
                sb = n % NS
                rows = slice(j * 64, (j + 1) * 64)
                P.add("pe", lambda e, o=ps[sb][:, q0:512], l_=kz[j][:, kt * 128:(kt + 1) * 128], r=qT[:, g * 512 + q0:(g + 1) * 512]:
                      e.matmul(o, l_, r, start=True, stop=True),
                      reads=["k_%da" % (kt // 4), "k_%db" % (kt // 4), "kz0pad", "kz1pad", "q_%d" % g], writes=["ps%d" % sb])

            def sB(n):
                t = tiles[n]
                h, g, j, kt = t["h"], t["g"], t["j"], t["kt"]
                i = kt - 4 * g
                if i < -1:
                    return
                q0 = max(i, 0) * 128
                sb = n % NS
                bm = bmh[h % 2]
                bn = "bmh%d" % (h % 2)
                if i >= 0 and i < 3:
                    P.add("dve", lambda e, o=ps[sb][:, q0:q0 + 256].rearrange("p (a b) -> p a b", a=2), b=bm[:, 0:2, j, :]: e.tensor_tensor(o, o, b, ALU.add),
                          reads=["ps%d" % sb, bn], writes=["ps%d" % sb])
                elif i == 3:
                    P.add("dve", lambda e, o=ps[sb][:, q0:q0 + 128], b=bm[:, 0, j, :]: e.tensor_tensor(o, o, b, ALU.add),
                          reads=["ps%d" % sb, bn], writes=["ps%d" % sb])
                else:
                    P.add("dve", lambda e, o=ps[sb][:, 0:128], b=bm[:, 1, j, :]: e.tensor_tensor(o, o, b, ALU.add),
                          reads=["ps%d" % sb, bn], writes=["ps%d" % sb])

            def sP(n):
                t = tiles[n]
                h, g, j, kt = t["h"], t["g"], t["j"], t["kt"]
                i = kt - 4 * g
                q0 = max(i, 0) * 128
                sb = n % NS
                Pt = Pb[n % NP]
                pn = "P%d" % (n % NP)
                P.add("act", lambda e, o=Pt[:, q0:512], i_=ps[sb][:, q0:512]: e.activation(o, i_, AF.Exp),
                      reads=["ps%d" % sb], writes=[pn + "a"])

            def sV(n):
                t = tiles[n]
                h, g, j, kt, gi_ = t["h"], t["g"], t["j"], t["kt"], t["gi"]
                i = kt - 4 * g
                q0 = max(i, 0) * 128
                Pt = Pb[n % NP]
                pn = "P%d" % (n % NP)
                ob, db = 3 + gi_ % 2, 5 + gi_ % 2
                first, last = t["first"], t["last"]
                vS = vSb[h % 2]
                P.add("pe", lambda e, o=ps[ob][:, q0:512], l_=vS[:, kt, :], r=Pt[:, q0:512]: e.matmul(o, l_, r, start=first, stop=last),
                      reads=[pn + "a", pn + "b", "v%d_%d" % (h % 2, kt // 4)], writes=["ps%d" % ob])
                P.add("pe", lambda e, o=ps[db][:, q0:512], r=Pt[:, q0:512]: e.matmul(o, ones_b, r, start=first, stop=last),
                      reads=[pn + "a", pn + "b", "ones_b"], writes=["ps%d" % db])
                if not last:
                    return
                r0, a0, a1, sq, rs = tmp
                aj = a0 if j == 0 else a1
                an = "tmp1" if j == 0 else "tmp2"
                Q = deferred.append
                Q(lambda d_=ps[db], dn="ps%d" % db: P.add("act", lambda e: e.activation(r0, d_, AF.Ln), reads=[dn], writes=["tmp0"]))
                Q(lambda: P.add("act", lambda e: e.activation(r0, r0, AF.Exp, scale=-1.0), reads=["tmp0"], writes=["tmp0"]))
                Q(lambda o_=ps[ob], a_=aj, on_="ps%d" % ob, an_=an: P.add("dve", lambda e: e.tensor_tensor(a_, o_, r0, ALU.mult), reads=[on_, "tmp0"], writes=[an_]))
                if j == 1:
                    sl = slice(g * 512, (g + 1) * 512)
                    Q(lambda: P.add("dve", lambda e: e.scalar_tensor_tensor(a0, a1, neglam, a0, ALU.mult, ALU.add), reads=["tmp1", "tmp2", "small"], writes=["tmp1"]))
                    Q(lambda: P.add("act", lambda e: e.activation(sqb, a0, AF.Square), reads=["tmp1"], writes=["tmp3"]))
                    Q(lambda: P.add("pe", lambda e: e.matmul(ps[7], ones_eb, sqb, start=True, stop=True), reads=["tmp3", "ones_e"], writes=["ps7"]))
                    Q(lambda: P.add("act", lambda e: e.activation(rs, ps[7], AF.Ln, bias=eps_ap, scale=1.0), reads=["ps7", "small"], writes=["tmp4"]))
                    Q(lambda: P.add("act", lambda e: e.activation(rs, rs, AF.Exp, scale=-0.5), reads=["tmp4"], writes=["tmp4"]))
                    Q(lambda o=oT[:, h, sl], on_="o%d_%d" % (h, g): P.add("dve", lambda e: e.scalar_tensor_tensor(o, a0, subs, rs, ALU.mult, ALU.mult),
                                                                  reads=["tmp1", "tmp4", "small"], writes=[on_]))

            def sD(n):
                k_ = 2 if len(deferred) > 6 else 1
                for _ in range(k_):
                    if deferred:
                        deferred.pop(0)()

            run_pipeline(N, [(0, sS), (1, sB), (2, sP), (3, sV), (3, sD)])
            while deferred:
                deferred.pop(0)()

        def attn_sb(l):
            nonlocal oT
            oT = A.alloc([8, S], BF16)
            mk = A.alloc([2, 128], F32)
            P.dma("sp", mk, masks_d, writes=["mk"])
            wts = [A.alloc([8, 384], BF16) for _ in range(2)]
            qT = A.alloc([S], BF16)
            kT = None
            kz = [A.alloc([S], BF16) for _ in range(2)]
            P.add("pool", lambda e: e.memset(kz[0][64:128, :], 0.0), writes=["kz0pad"])
            P.add("pool", lambda e: e.memset(kz[1][0:64, :], 0.0), writes=["kz1pad"])
            vSb = [A.alloc([16, 128], BF16) for _ in range(2)]
            NZ, NW, NL, NA = 5, 4, 2, 2
            HSEL = ["dve"]
            LSEL = ["pool"]
            Wb = [A.alloc([512], F32) for _ in range(NW)]
            NE = 2 if all(h_ == "dve" for h_ in HSEL) else NW
            Eb = [A.alloc([512], F32) for _ in range(NE)]
            Lh = [A.alloc([512], BF16) for _ in range(NL)]
            Ll = [A.alloc([512], BF16) for _ in range(NL)]
            Ab = [A.alloc([512], BF16) for _ in range(NA)]
            Cb = [A.alloc([512], F32) for _ in range(2)]

            tiles = []
            for c in range(8):
                for r in range(2):
                    for g in range(4):
                        for kt in range(4 * g + 3, -1, -1):
                            tiles.append(dict(c=c, r=r, g=g, kt=kt, first=(kt == 4 * g + 3), last=(kt == 0),
                                              pair_first=(r == 0 and g == 0 and kt == 3)))
            N = len(tiles)
            for n in range(N):
                t = tiles[n]
                i = t["kt"] - 4 * t["g"]
                t["i"] = i
                t["q0"] = max(i, 0) * 128
                t["pq0"] = tiles[n - 1]["q0"] if not t["first"] else 512

            def prefetch(c):
                load_w("pool", wts[c % 2], wd["wqkv%d" % l][c], "wt%d" % (c % 2))

            def sZ(n):
                t = tiles[n]
                c, r, g, kt, q0 = t["c"], t["r"], t["g"], t["kt"], t["q0"]
                if t["pair_first"]:
                    if c == 0:
                        prefetch(0)
                    if c + 1 < 8:
                        prefetch(c + 1)
                    proj_qkv_head(wts[c % 2], "wt%d" % (c % 2), qT, kT, vSb[c % 2], 0.125, "", banks=(5, 6), vtag="v%d" % (c % 2), kpad=kz)
                rows = slice(r * 64, (r + 1) * 64)
                zb = n % NZ
                P.add("pe", lambda e, o=ps[zb][:, q0:512], l_=kz[r][:, kt * 128:(kt + 1) * 128], r_=qT[:, g * 512 + q0:(g + 1) * 512]:
                      e.matmul(o, l_, r_, start=True, stop=False, skip_group_check=True),
                      reads=["k_%da" % (kt // 4), "k_%db" % (kt // 4), "kz0pad", "kz1pad", "q_%d" % g], writes=["ps%d" % zb])

            def sE(n):
                t = tiles[n]
                q0 = t["q0"]
                zb = n % NZ
                W = Wb[n % NW]
                wn = "W%d" % (n % NW)
                E = Eb[n % NE]
                en = "E%d" % (n % NE)
                P.add("act", lambda e, o=E[:, q0:512], i_=ps[zb][:, q0:512]: e.activation(o, i_, AF.Exp), reads=["ps%d" % zb], writes=[en])
                P.add("act", lambda e, o=W[:, q0:512], i_=E[:, q0:512]: e.activation(o, i_, AF.Ln, bias=1.0, scale=1.0), reads=[en], writes=[wn])

            def sM(n):
                t = tiles[n]
                q0 = t["q0"]
                W = Wb[n % NW]
                wn = "W%d" % (n % NW)
                if t["i"] >= 0:
                    P.add("dve", lambda e, o=W[:, q0:q0 + 128], m=mk[:, 0, :]: e.tensor_tensor(o, o, m, ALU.mult), reads=[wn, "mk"], writes=[wn])

            def sL(n):
                t = tiles[n]
                q0 = t["q0"]
                W = Wb[n % NW]
                wn = "W%d" % (n % NW)
                LH, LL = Lh[n % NL], Ll[n % NL]
                hsel = HSEL[n % len(HSEL)]
                if hsel == "act":
                    P.add("act", lambda e, o=LH[:, q0:512], i_=Eb[n % NE][:, q0:512]: e.activation(o, i_, AF.Ln, bias=1.0, scale=1.0),
                          reads=["E%d" % (n % NE)], writes=["LH%d" % (n % NL)])
                else:
                    P.add("dve", lambda e, o=LH[:, q0:512], i_=W[:, q0:512]: e.tensor_copy(o, i_), reads=[wn], writes=["LH%d" % (n % NL)])
                P.add(LSEL[n % len(LSEL)], lambda e, o=LL[:, q0:512], i_=W[:, q0:512], hh=LH[:, q0:512]: e.tensor_tensor(o, i_, hh, ALU.subtract),
                      reads=[wn, "LH%d" % (n % NL)], writes=["LL%d" % (n % NL)])

            def sT(n):
                t = tiles[n]
                q0 = t["q0"]
                zb = n % NZ
                cb = 5 + n % 2
                LH, LL = Lh[n % NL], Ll[n % NL]
                hn, ln_ = "LH%d" % (n % NL), "LL%d" % (n % NL)
                P.add("pe", lambda e, o=ps[zb][:, q0:512], r_=LH[:, q0:512]: e.matmul(o, ntri_b, r_, start=False, stop=False, skip_group_check=True),
                      reads=[hn, "tri_b"], writes=["ps%d" % zb])
                P.add("pe", lambda e, o=ps[zb][:, q0:512], r_=LL[:, q0:512]: e.matmul(o, ntri_b, r_, start=False, stop=True, skip_group_check=True),
                      reads=[ln_, "tri_b"], writes=["ps%d" % zb])
                if not t["last"]:
                    P.add("pe", lambda e, o=ps[cb][:, q0:512], r_=LH[:, q0:512]: e.matmul(o, nones_b, r_, start=True, stop=False),
                          reads=[hn, "ones_b"], writes=["ps%d" % cb])
                    P.add("pe", lambda e, o=ps[cb][:, q0:512], r_=LL[:, q0:512]: e.matmul(o, nones_b, r_, start=False, stop=True),
                          reads=[ln_, "ones_b"], writes=["ps%d" % cb])

            def sC(n):
                t = tiles[n]
                if t["last"]:
                    return
                q0, pq0 = t["q0"], t["pq0"]
                cb = 5 + n % 2
                C, Cp = Cb[n % 2], Cb[(n - 1) % 2]
                cn, cpn = "C%d" % (n % 2), "C%d" % ((n - 1) % 2)
                if pq0 > q0:
                    P.add("dve", lambda e, o=C[:, q0:pq0], i_=ps[cb][:, q0:pq0]: e.tensor_copy(o, i_), reads=["ps%d" % cb], writes=[cn + "a"])
                if pq0 < 512:
                    P.add("dve", lambda e, o=C[:, pq0:512], i_=ps[cb][:, pq0:512], p_=Cp[:, pq0:512]: e.tensor_tensor(o, i_, p_, ALU.add),
                          reads=["ps%d" % cb, cpn + "a", cpn + "b"], writes=[cn + "b"])

            def sX(n):
                t = tiles[n]
                q0, pq0 = t["q0"], t["pq0"]
                zb = n % NZ
                W = Wb[n % NW]
                wn = "W%d" % (n % NW)
                Cp = Cb[(n - 1) % 2]
                cpn = "C%d" % ((n - 1) % 2)
                c0 = q0
                if t["i"] >= 0:
                    P.add("dve", lambda e, o=W[:, q0:q0 + 128], i_=ps[zb][:, q0:q0 + 128], m=mk[:, 1, :]: e.tensor_tensor(o, i_, m, ALU.add),
                          reads=["ps%d" % zb, "mk", wn], writes=[wn])
                    c0 = q0 + 128
                if c0 < 512:
                    P.add("dve", lambda e, o=W[:, c0:512], i_=ps[zb][:, c0:512], c_=Cp[:, c0:512]: e.tensor_tensor(o, i_, c_, ALU.add),
                          reads=["ps%d" % zb, cpn + "a", cpn + "b", wn], writes=[wn])

            def sA(n):
                t = tiles[n]
                q0 = t["q0"]
                W = Wb[n % NW]
                wn = "W%d" % (n % NW)
                P.add("act", lambda e, o=Ab[n % NA][:, q0:512], i_=W[:, q0:512]: e.activation(o, i_, AF.Exp), reads=[wn], writes=["A%d" % (n % NA)])

            def sV(n):
                t = tiles[n]
                c, r, g, kt, q0 = t["c"], t["r"], t["g"], t["kt"], t["q0"]
                rows = slice(r * 64, (r + 1) * 64)
                vS = vSb[c % 2]
                P.add("pe", lambda e, o=ps[7][:, q0:512], l_=vS[:, kt, :], r_=Ab[n % NA][:, q0:512], st=t["first"], sp_=t["last"]:
                      e.matmul(o, l_, r_, start=st, stop=sp_, skip_group_check=True),
                      reads=["A%d" % (n % NA), "v%d_%d" % (c % 2, kt // 4)], writes=["ps7"])
                if t["last"]:
                    sl = slice(g * 512, (g + 1) * 512)
                    P.add("act", lambda e, o=oT[rows, c, sl], i_=ps[7][rows, :]: e.activation(o, i_, AF.Identity),
                          reads=["ps7"], writes=["o%d_%d_%d" % (c, g, r)])

            run_pipeline(N, list(zip([0, 1, 2, 2, 3, 3, 3, 4, 5], [sZ, sE, sM, sL, sT, sC, sX, sA, sV])))

        def attn_swa(l):
            nonlocal oT
            oT = A.alloc([8, S], BF16)
            bmh = [A.alloc([2, 2, 128], F32) for _ in range(2)]
            wts = [A.alloc([8, 384], BF16) for _ in range(2)]
            qz = [A.alloc([S], BF16) for _ in range(2)]
            P.add("pool", lambda e: e.memset(qz[0][64:128, :], 0.0), writes=["qz0pad"])
            P.add("pool", lambda e: e.memset(qz[1][0:64, :], 0.0), writes=["qz1pad"])
            kT = A.alloc([S], BF16)
            vSb = [A.alloc([16, 128], BF16) for _ in range(2)]
            Tt = [[A.alloc([512], F32) for _ in range(2)] for _ in range(2)]
            Pt = [[A.alloc([512], BF16) for _ in range(2)] for _ in range(3)]
            rr = [A.alloc([512], F32) for _ in range(2)]
            P.add("act", lambda e: e.activation(small[:, ESINK:ESINK + 16], vecs[:, V_SINK:V_SINK + 16], AF.Exp), reads=["vecs", "small"], writes=["small"])

            def prefetch(c):
                load_w("pool", wts[c % 2], wd["wqkv%d" % l][c], "wt%d" % (c % 2))

            def prefetch_bm(c):
                P.dma("sp", bmh[c % 2][:, 0], bm_d[:, 0, 2 * c:2 * c + 2, :], writes=["bmh%da" % (c % 2)])
                P.dma("sp", bmh[c % 2][:, 1], bm_d[:, 2, 2 * c:2 * c + 2, :], writes=["bmh%db" % (c % 2)])

            its = [(c, r, qq) for c in range(8) for r in range(2) for qq in range(4)]
            N = len(its)

            def sS(n):
                c, r, qq = its[n]
                if r == 0 and qq == 0:
                    if c == 0:
                        prefetch(0)
                        prefetch_bm(0)
                    if c + 1 < 8:
                        prefetch(c + 1)
                    proj_qkv_head(wts[c % 2], "wt%d" % (c % 2), None, kT, vSb[(c // 2) % 2], 0.125, "", banks=(0, 1),
                                  vtag="v%d" % ((c // 2) % 2), qpad=qz, do_kv=(c % 2 == 0))
                if r == 1 and qq == 0 and c + 1 < 8:
                    prefetch_bm(c + 1)
                b0, b1 = 2 * (n % 2), 2 * (n % 2) + 1
                for t4 in range(4):
                    qt = qq * 4 + t4
                    cs = slice(t4 * 128, (t4 + 1) * 128)
                    qrd = ["q_%da" % qq, "q_%db" % qq, "qz0pad", "qz1pad"]
                    P.add("pe", lambda e, o=ps[b0][:, cs], l_=kT[:, qt * 128:(qt + 1) * 128], r_=qz[r][:, qt * 128:(qt + 1) * 128]:
                          e.matmul(o, l_, r_, start=True, stop=True), reads=["k_%d" % qq] + qrd, writes=["ps%d" % b0])
                    if qt > 0:
                        P.add("pe", lambda e, o=ps[b1][:, cs], l_=kT[:, (qt - 1) * 128:qt * 128], r_=qz[r][:, qt * 128:(qt + 1) * 128]:
                              e.matmul(o, l_, r_, start=True, stop=True), reads=["k_%d" % ((qt - 1) // 4)] + qrd, writes=["ps%d" % b1])

            def sB(n):
                c, r, qq = its[n]
                b0, b1 = 2 * (n % 2), 2 * (n % 2) + 1
                T0, T1 = Tt[n % 2]
                p0 = 0 if qq > 0 else 128
                np1 = (512 - p0) // 128
                bm = bmh[c % 2]
                P.add("dve", lambda e, o=T0.rearrange("p (a b) -> p a b", a=4), i_=ps[b0].rearrange("p (a b) -> p a b", a=4),
                      b=bm[:, 0, r, :].unsqueeze(1).broadcast_to([128, 4, 128]): e.tensor_tensor(o, i_, b, ALU.add),
                      reads=["ps%d" % b0, "bmh%da" % (c % 2)], writes=["Tt%d_0" % (n % 2)])
                P.add("dve", lambda e, o=T1[:, p0:512].rearrange("p (a b) -> p a b", a=np1), i_=ps[b1][:, p0:512].rearrange("p (a b) -> p a b", a=np1),
                      b=bm[:, 1, r, :].unsqueeze(1).broadcast_to([128, np1, 128]): e.tensor_tensor(o, i_, b, ALU.add),
                      reads=["ps%d" % b1, "bmh%db" % (c % 2)], writes=["Tt%d_1" % (n % 2)])

            def sP(n):
                c, r, qq = its[n]
                T0, T1 = Tt[n % 2]
                P0, P1 = Pt[n % 3]
                p0 = 0 if qq > 0 else 128
                P.add("act", lambda e, o=P0, i_=T0: e.activation(o, i_, AF.Exp), reads=["Tt%d_0" % (n % 2)], writes=["Pt%d_0" % (n % 3)])
                P.add("act", lambda e, o=P1[:, p0:512], i_=T1[:, p0:512]: e.activation(o, i_, AF.Exp), reads=["Tt%d_1" % (n % 2)], writes=["Pt%d_1" % (n % 3)])

            def sV(n):
                c, r, qq = its[n]
                P0, P1 = Pt[n % 3]
                ob, db = 4 + n % 2, 6 + n % 2
                vS = vSb[(c // 2) % 2]
                vt = "v%d" % ((c // 2) % 2)
                for t4 in range(4):
                    qt = qq * 4 + t4
                    cs = slice(t4 * 128, (t4 + 1) * 128)
                    has_prev = qt > 0
                    P.add("pe", lambda e, o=ps[ob][:, cs], l_=vS[:, qt, :], r_=P0[:, cs], sp_=(not has_prev):
                          e.matmul(o, l_, r_, start=True, stop=sp_), reads=["Pt%d_0" % (n % 3), vt + "_%d" % qq], writes=["ps%d" % ob])
                    if has_prev:
                        P.add("pe", lambda e, o=ps[ob][:, cs], l_=vS[:, qt - 1, :], r_=P1[:, cs]:
                              e.matmul(o, l_, r_, start=False, stop=True), reads=["Pt%d_1" % (n % 3), vt + "_%d" % ((qt - 1) // 4)], writes=["ps%d" % ob])
                    P.add("pe", lambda e, o=ps[db][:, cs], r_=P0[:, cs], sp_=(not has_prev):
                          e.matmul(o, ones_b, r_, start=True, stop=sp_), reads=["Pt%d_0" % (n % 3), "ones_b"], writes=["ps%d" % db])
                    if has_prev:
                        P.add("pe", lambda e, o=ps[db][:, cs], r_=P1[:, cs]:
                              e.matmul(o, ones_b, r_, start=False, stop=True), reads=["Pt%d_1" % (n % 3), "ones_b"], writes=["ps%d" % db])

            def sN(n):
                c, r, qq = its[n]
                hq = 2 * c + r
                rows = slice(r * 64, (r + 1) * 64)
                sl = slice(qq * 512, (qq + 1) * 512)
                ob, db = 4 + n % 2, 6 + n % 2
                rb = rr[n % 2]
                rn = "rr%d" % (n % 2)
                P.add("act", lambda e, o=rb, i_=ps[db], s_=small[:, ESINK + hq:ESINK + hq + 1]: e.activation(o, i_, AF.Ln, bias=s_, scale=1.0),
                      reads=["ps%d" % db, "small"], writes=[rn])
                P.add("act", lambda e, o=rb: e.activation(o, o, AF.Exp, scale=-1.0), reads=[rn], writes=[rn])
                P.add("dve", lambda e, o=oT[rows, c, sl], i_=ps[ob][rows, :], r_=rb[rows, :]: e.tensor_tensor(o, i_, r_, ALU.mult),
                      reads=["ps%d" % ob, rn], writes=["o%d_%d_%d" % (c, qq, r)])

            run_pipeline(N, [(0, sS), (0, sB), (1, sP), (1, sV), (2, sN)])

        def ffn(l):
            wup = [A.alloc([8, 256], BF16) for _ in range(3)]
            wdn = [A.alloc([6, D], BF16) for _ in range(2)]
            ag = A.alloc([6, S], BF16)
            Ub = [[A.alloc([514], F32) for _ in range(2)] for _ in range(2)]
            Tb = [[A.alloc([512], F32) for _ in range(2)] for _ in range(2)]
            Sg = [A.alloc([512], F32) for _ in range(2)]
            for j in range(2):
                load_w("pool", wup[j], wd["wup%d" % l][j], "wup%d" % j)
            j0_, nj_ = FFN_GROUPS[0]
            load_w("pool", wdn[0][:, 0:nj_, :], wd["wdn%d" % l][j0_:j0_ + nj_].rearrange("j p m -> p j m"), "wdn0")
            cnt = [0, 0]
            pre_up = set()
            pending_up = []
            for gi, (j0, nj) in enumerate(FFN_GROUPS):
                ab = gi % 2
                if gi + 1 < len(FFN_GROUPS):
                    j1, nj1 = FFN_GROUPS[gi + 1]
                    load_w("pool", wdn[1 - ab][:, 0:nj1, :], wd["wdn%d" % l][j1:j1 + nj1].rearrange("j p m -> p j m"), "wdn%d" % (1 - ab))
                tl = [(jj, tg) for jj in range(nj) for tg in range(4)]
                base = cnt[0]
                cnt[0] += len(tl)

                def s_up(m, gi=gi, j0=j0, tl=tl, base=base):
                    if (gi, m) in pre_up:
                        return
                    pre_up.add((gi, m))
                    jj, tg = tl[m]
                    j = j0 + jj
                    n = base + m
                    wb = j % 3
                    if tg == 0 and j + 2 < NJ:
                        load_w("pool", wup[(j + 2) % 3], wd["wup%d" % l][j + 2], "wup%d" % ((j + 2) % 3))
                    sl = slice(tg * 512, (tg + 1) * 512)
                    pb = (n % 2) * 2
                    for half in (1, 0):
                        for kc in range(8):
                            P.add("pe", lambda e, o=ps[pb + half], w=wup[wb][:, kc, half * 128:(half + 1) * 128], r=hT[:, kc, sl], st=(kc == 0), sp_=(kc == 7):
                                  e.matmul(o, w, r, start=st, stop=sp_), reads=["wup%d" % wb, "h%d_%d" % (kc, tg)], writes=["ps%d" % (pb + half)])

                def s_conv(m):
                    jj, tg = tl[m]
                    j = j0 + jj
                    n = base + m
                    b = n % 2
                    pb = (n % 2) * 2
                    cv = V_CONV + (l * NJ + j) * 8
                    taps = []
                    for half in range(2):
                        U = ps[pb + half]
                        un = "ps%d" % (pb + half)
                        ub, T = Ub[half][b], Tb[half][b]
                        umn, uhn, tn = "Um%d_%d" % (half, b), "Uh%d_%d" % (half, b), "T%d_%d" % (half, b)
                        w0 = vecs[:, cv + half * 4 + 0:cv + half * 4 + 1]
                        w1 = vecs[:, cv + half * 4 + 1:cv + half * 4 + 2]
                        w2 = vecs[:, cv + half * 4 + 2:cv + half * 4 + 3]
                        bb = vecs[:, cv + half * 4 + 3:cv + half * 4 + 4]
                        P.add("act", lambda e, o=ub[:, 2:514], i=U: e.activation(o, i, AF.Identity), reads=[un], writes=[umn])
                        P.add("act", lambda e, o=T, i=U, s_=w2, b_=bb: e.activation(o, i, AF.Identity, bias=b_, scale=s_), reads=[un, "vecs"], writes=[tn])
                        if tg == 0:
                            P.add("pool", lambda e, o=ub[:, 0:2]: e.memset(o, 0.0), writes=[uhn])
                        else:
                            P.add("pool", lambda e, o=ub[:, 0:2], i=Ub[half][1 - b][:, 512:514]: e.tensor_copy(o, i),
                                  reads=["Um%d_%d" % (half, 1 - b)], writes=[uhn])
                        taps.append((ub, T, umn, uhn, tn, w0, w1))
                    for which in (1, 0):
                        for (ub, T, umn, uhn, tn, w0, w1) in taps:
                            if which == 1:
                                P.add("dve", lambda e, o=T, i=ub[:, 1:513], s_=w1: e.scalar_tensor_tensor(o, i, s_, o, ALU.mult, ALU.add),
                                      reads=[umn, uhn, "vecs", tn], writes=[tn])
                            else:
                                P.add("dve", lambda e, o=T, i=ub[:, 0:512], s_=w0: e.scalar_tensor_tensor(o, i, s_, o, ALU.mult, ALU.add),
                                      reads=[umn, uhn, "vecs", tn], writes=[tn])

                def s_gate(m):
                    jj, tg = tl[m]
                    n = base + m
                    b = n % 2
                    sl = slice(tg * 512, (tg + 1) * 512)
                    P.add("act", lambda e, o=Sg[b], i=Tb[0][b]: e.activation(o, i, AF.Silu), reads=["T0_%d" % b], writes=["Sg%d" % b])
                    P.add("pool", lambda e, o=ag[:, jj, sl], a_=Sg[b], b_=Tb[1][b]: e.tensor_tensor(o, a_, b_, ALU.mult),
                          reads=["Sg%d" % b, "T1_%d" % b], writes=["a_%d_%d" % (jj, tg)])

                run_pipeline(len(tl), [(0, s_up), (0, s_conv), (1, s_gate)])
                if gi + 1 < len(FFN_GROUPS):
                    j0n, njn = FFN_GROUPS[gi + 1]
                    tln = [(jj, tg) for jj in range(njn) for tg in range(4)]
                    basen = cnt[0]
                    for m_ in range(FFN_PRE):
                        s_up(m_, gi=gi + 1, j0=j0n, tl=tln, base=basen)
                for dc in range(8):
                    for tg in range(4):
                        sl = slice(tg * 512, (tg + 1) * 512)
                        bank = 4 + (cnt[1] % 2)
                        cnt[1] += 1
                        for jj in range(nj):
                            P.add("pe", lambda e, o=ps[bank], w=wdn[ab][:, jj, dc * 128:(dc + 1) * 128], r=ag[:, jj, sl], st=(jj == 0), sp_=(jj == nj - 1):
                                  e.matmul(o, w, r, start=st, stop=sp_), reads=["wdn%d" % ab, "a_%d_%d" % (jj, tg)], writes=["ps%d" % bank])
                        P.add("dve", lambda e, o=xT[:, dc, sl], i=ps[bank]: e.tensor_tensor(o, i, o, ALU.add),
                              reads=["ps%d" % bank, "x%d_%d" % (dc, tg)], writes=["x%d_%d" % (dc, tg)])

        def final_norm():
            sq = [A.alloc([512], BF16) for _ in range(3)]
            rstd = [A.alloc([512], F32) for _ in range(2)]
            yb = [A.alloc([512], F32) for _ in range(4)]
            k = 0
            m = 0
            outs = []
            for tg in range(4):
                sl = slice(tg * 512, (tg + 1) * 512)
                bank = 6 + (tg % 2)
                for c in range(8):
                    b = k % 3
                    k += 1
                    P.add("act", lambda e, o=sq[b], i=xT[:, c, sl]: e.activation(o, i, AF.Square), reads=["x%d_%d" % (c, tg)], writes=["sq%d" % b])
                    P.add("pe", lambda e, o=ps[bank], r=sq[b], st=(c == 0), sp_=(c == 7): e.matmul(o, ones_dmb, r, start=st, stop=sp_),
                          reads=["sq%d" % b, "ones_dm"], writes=["ps%d" % bank])
                rb = rstd[tg % 2]
                P.add("act", lambda e, o=rb, i=ps[bank]: e.activation(o, i, AF.Ln, bias=eps_ap, scale=1.0),
                      reads=["ps%d" % bank, "small"], writes=["rstd%d" % (tg % 2)])
                P.add("act", lambda e, o=rb: e.activation(o, o, AF.Exp, scale=-0.5),
                      reads=["rstd%d" % (tg % 2)], writes=["rstd%d" % (tg % 2)])
                for c in range(8):
                    y = yb[m % 4]
                    yn = "y%d" % (m % 4)
                    m += 1
                    P.add("dve", lambda e, o=y, i=xT[:, c, sl], g=vecs[:, V_FIN + c:V_FIN + c + 1], r=rb: e.scalar_tensor_tensor(o, i, g, r, ALU.mult, ALU.mult),
                          reads=["x%d_%d" % (c, tg), "vecs", "rstd%d" % (tg % 2)], writes=[yn])
                    on = "out%d" % ((m - 1) % 4)
                    P.dma("sp", yout[c * 128:(c + 1) * 128, sl], y, reads=[yn], writes=[on])
                    if on not in outs:
                        outs.append(on)
            return outs

        oT = None
        for l in layers:
            mixer = l % 3
            A.off = persist_end
            P.barrier()
            rmsnorm_to_h(V_AN + 8 * l, "a%d" % l)
            if mixer == 0:
                attn_da(l)
            elif mixer == 1:
                attn_sb(l)
            else:
                attn_swa(l)
            oproj(l, mixer != 0)
            A.off = persist_end
            P.barrier()
            rmsnorm_to_h(V_FN + 8 * l, "f%d" % l)
            if not DEBUG_NOFFN:
                ffn(l)
        outs = []
        A.off = persist_end
        P.barrier()
        if do_final:
            outs = final_norm()
        else:
            for c in range(8):
                on = "out%d" % c
                P.dma("sp", yout[c * 128:(c + 1) * 128, :], xT[:, c, :], reads=["x%d_%d" % (c, t) for t in range(4)], writes=[on])
                outs.append(on)
        P.finish(outs)
        P.emit()
    return nc


LAYER_GROUPS = [[0, 1, 2, 3]]


def kernel(**inputs):
    x = np.asarray(inputs["x"], np.float32)
    B = x.shape[0]
    shared = _prep_shared(inputs)
    cur = [np.ascontiguousarray(x[b].T) for b in range(B)]
    for gi, layers in enumerate(LAYER_GROUPS):
        do_final = (gi == len(LAYER_GROUPS) - 1)
        nc = build_nc(layers, do_final)
        lw = {}
        for l in layers:
            lw.update(_prep_layer(inputs, l))
        in_maps = []
        for b in range(B):
            m = dict(shared)
            m.update(lw)
            m["xin"] = cur[b]
            in_maps.append(m)
        res = run_bass_kernel_spmd(nc, in_maps, core_ids=list(range(B)))
        cur = [np.asarray(res.results[b]["yout"], np.float32) for b in range(B)]
    out = np.stack([c.T for c in cur], axis=0)
    return np.ascontiguousarray(out.astype(np.float32))
```

```python
import math
import numpy as np
from contextlib import ExitStack
import concourse.bass as bass
import concourse.mybir as mybir
from concourse.bass_utils import run_bass_kernel_spmd

F32 = mybir.dt.float32
BF16 = mybir.dt.bfloat16
AF = mybir.ActivationFunctionType
ALU = mybir.AluOpType

D = 1024
S = 2048
DEPTH = 4
DFF = 2752
NJ = 22
EPS = 1e-6
NEGM = -30000.0
FFN_GROUPS = [(0, 6), (6, 6), (12, 5), (17, 5)]
DEBUG_NOFFN = False
FFN_PRE = 2
ATTACH_WAIT = True


class _I:
    __slots__ = ("eng", "fn", "deps", "dma", "sig", "sem", "val", "key")


class Prog:
    ENGS = ("pe", "act", "dve", "pool", "sp")
    SEM_LIMIT = 20000

    def __init__(self, nc, same_eng_sync=True):
        self.nc = nc
        self.same = same_eng_sync
        self.streams = {e: [] for e in self.ENGS}
        self.last_writer = {}
        self.readers = {}
        self.n = 0

    def add(self, eng, fn, reads=(), writes=(), dma=False, nodep=()):
        ins = _I()
        ins.eng, ins.fn, ins.dma, ins.sig, ins.sem, ins.val = eng, fn, dma, False, None, 0
        ins.key = writes[0] if (dma and writes) else None
        deps = set()
        for r in reads:
            w = self.last_writer.get(r)
            if w is not None:
                deps.add(w)
        reads = list(reads) + list(nodep)
        for r in writes:
            w = self.last_writer.get(r)
            if w is not None:
                deps.add(w)
            for rd in self.readers.get(r, ()):
                deps.add(rd)
        for r in reads:
            self.readers.setdefault(r, []).append(ins)
        for r in writes:
            self.last_writer[r] = ins
            self.readers[r] = []
        deps.discard(ins)
        ins.deps = deps
        self.streams[eng].append(ins)
        self.n += 1
        return ins

    def dma(self, q, out, in_, reads=(), writes=()):
        return self.add(q, lambda e: e.dma_start(out=out, in_=in_), reads=reads, writes=writes, dma=True)

    def barrier(self):
        lasts = set()
        for w in self.last_writer.values():
            lasts.add(w)
        for rl in self.readers.values():
            for r in rl:
                lasts.add(r)
        for e in self.ENGS:
            if self.streams[e]:
                lasts.add(self.streams[e][-1])
        for e in self.ENGS:
            ins = _I()
            ins.eng, ins.fn, ins.dma, ins.sig, ins.sem, ins.val, ins.key = e, None, False, False, None, 0, None
            ins.deps = set(lasts)
            self.streams[e].append(ins)
        self.last_writer = {}
        self.readers = {}

    def finish(self, outs):
        self.add("sp", None, reads=list(outs))

    def _skip(self, ins, d):
        if d.dma:
            return False
        if d.eng == ins.eng:
            if d.eng == "pe":
                return True
            if not self.same:
                return True
        return False

    def emit(self):
        nc = self.nc
        for e in self.ENGS:
            for ins in self.streams[e]:
                for d in ins.deps:
                    if d.fn is not None and not self._skip(ins, d):
                        d.sig = True
        semnames = []
        dmacnt = {}
        for e in self.ENGS:
            cnt, idx = 0, 0
            for ins in self.streams[e]:
                if ins.dma:
                    k = ("dma", ins.key)
                    dmacnt[k] = dmacnt.get(k, 0) + 1
                    ins.sem, ins.val = k, 16 * dmacnt[k]
                    if k not in semnames:
                        semnames.append(k)
                elif ins.sig:
                    cnt += 1
                    if cnt > self.SEM_LIMIT:
                        idx += 1
                        cnt = 1
                    ins.sem, ins.val = (e, idx), cnt
                    if ins.sem not in semnames:
                        semnames.append(ins.sem)
        self.nsems = len(semnames)
        with ExitStack() as es:
            sems = {}
            for i, k in enumerate(semnames):
                sems[k] = es.enter_context(nc.semaphore("s%d" % i))
            block = es.enter_context(nc.Block())

            def replay(ename):
                def body(eng):
                    known = {}
                    for ins in self.streams[ename]:
                        need = {}
                        for d in ins.deps:
                            if d.fn is None or self._skip(ins, d):
                                continue
                            if need.get(d.sem, 0) < d.val:
                                need[d.sem] = d.val
                        todo = [(k, v) for k, v in need.items() if known.get(k, 0) < v]
                        for k, v in todo:
                            known[k] = v
                        attach = None
                        if ATTACH_WAIT and ins.fn is not None and todo:
                            attach = todo.pop()
                        for k, v in todo:
                            eng.wait_ge(sems[k], v)
                        if ins.fn is not None:
                            bi = ins.fn(eng)
                            if attach is not None:
                                bi._wait_ge(sems[attach[0]], attach[1])
                            if ins.dma:
                                bi.then_inc(sems[ins.sem], 16)
                            elif ins.sig:
                                bi.then_inc(sems[ins.sem], 1)
                return body

            block.tensor(replay("pe"))
            block.scalar(replay("act"))
            block.vector(replay("dve"))
            block.gpsimd(replay("pool"))
            block.sync(replay("sp"))


def _t5_bucket(dist):
    max_exact = 16
    d = np.maximum(dist, 0)
    large = max_exact + (np.log(np.maximum(d, 1).astype(np.float32) / np.float32(max_exact))
                         / np.float32(math.log(128 / max_exact)) * np.float32(32 - max_exact)).astype(np.int32)
    large = np.minimum(large, 31)
    return np.where(d < max_exact, d, large)


V_AN = 0
V_FN = 32
V_FIN = 64
V_SUB = 72
V_SINK = 74
V_BC = 90
V_LAM = 106
V_CONV = 106 + 512
NV = V_CONV + 4 * NJ * 8


def _prep_shared(inp):
    rel_bias = np.asarray(inp["rel_bias"], np.float32)
    vecs = np.zeros((128, NV), np.float32)
    pc = lambda v: np.asarray(v, np.float32).reshape(8, 128).T
    for l in range(DEPTH):
        vecs[:, V_AN + 8 * l:V_AN + 8 * l + 8] = pc(inp["attn_norm"][l])
        vecs[:, V_FN + 8 * l:V_FN + 8 * l + 8] = pc(inp["ffn_norm"][l])
    vecs[:, V_FIN:V_FIN + 8] = pc(inp["final_norm"])
    vecs[:, V_SUB:V_SUB + 2] = np.asarray(inp["da_subln"], np.float32).T
    vecs[:, V_SINK:V_SINK + 16] = np.asarray(inp["sw_sinks"], np.float32).reshape(1, 16)
    vecs[:, V_BC:V_BC + 16] = rel_bias[31][None, :]
    vecs[:, V_LAM:V_LAM + 512] = np.asarray(inp["da_lambda"], np.float32).reshape(1, 512)
    cw = np.asarray(inp["ffn_conv_w"], np.float32)
    cb = np.asarray(inp["ffn_conv_b"], np.float32)
    for l in range(DEPTH):
        for half in range(2):
            w = np.zeros((3, NJ * 128), np.float32)
            b = np.zeros((NJ * 128,), np.float32)
            w[:, :DFF] = cw[l][:, half * DFF:(half + 1) * DFF]
            b[:DFF] = cb[l][half * DFF:(half + 1) * DFF]
            w = w.reshape(3, NJ, 128)
            b = b.reshape(NJ, 128)
            for j in range(NJ):
                base = V_CONV + (l * NJ + j) * 8 + half * 4
                vecs[:, base + 0] = w[0, j]
                vecs[:, base + 1] = w[1, j]
                vecs[:, base + 2] = w[2, j]
                vecs[:, base + 3] = b[j]
    kk = np.arange(128)[:, None]
    qq = np.arange(128)[None, :]
    d0 = qq - kk
    d1 = qq - kk + 128
    b0 = rel_bias[_t5_bucket(d0)]
    b1 = rel_bias[_t5_bucket(d1)]
    bm = np.zeros((128, 3, 16, 128), np.float32)
    neg = np.float32(NEGM)
    bm[:, 0] = np.where((d0 >= 0)[:, :, None], b0, neg).transpose(0, 2, 1)
    bm[:, 1] = b1.transpose(0, 2, 1)
    bm[:, 2] = np.where((d1 < 128)[:, :, None], b1, neg).transpose(0, 2, 1)
    masks = np.zeros((128, 2, 128), np.float32)
    masks[:, 0] = (kk < qq).astype(np.float32)
    masks[:, 1] = np.where(kk < qq, np.float32(0), neg)
    tri = (kk >= qq).astype(np.float32)
    consts = np.zeros((128, 2, 128), np.float32)
    consts[:, 0] = 1.0
    consts[:, 1] = tri
    return dict(vecs=vecs, bm=bm, masks=masks, consts=consts)


def _prep_layer(inp, l):
    mixer, slot = l % 3, l // 3
    out = {}
    if mixer in (0, 1):
        w = np.asarray(inp["da_w_qkv" if mixer == 0 else "sb_w_qkv"][slot], np.float32)
        q = w[:, 0:1024].reshape(D, 8, 128)
        k = w[:, 1024:2048].reshape(D, 8, 128)
        v = w[:, 2048:3072].reshape(D, 8, 128)
        wh = np.concatenate([q, k, v], axis=2)
        wh = wh.reshape(8, 128, 8, 384).transpose(2, 1, 0, 3)
        out["wqkv%d" % l] = np.ascontiguousarray(wh)
    else:
        w = np.asarray(inp["sw_w_qkv"][slot], np.float32)
        q = w[:, 0:1024]
        k = w[:, 1024:1280].reshape(D, 4, 64)
        v = w[:, 1280:1536].reshape(D, 4, 64)
        kd = np.concatenate([k, k], axis=2)
        vd = np.concatenate([v, v], axis=2)
        gi = np.arange(8) // 2
        wh = np.concatenate([q.reshape(D, 8, 128), kd[:, gi, :], vd[:, gi, :]], axis=2)
        wh = wh.reshape(8, 128, 8, 384).transpose(2, 1, 0, 3)
        out["wqkv%d" % l] = np.ascontiguousarray(wh)
    wo = np.asarray(inp["w_o"][l], np.float32)
    wo = wo.reshape(8, 128, 8, 128).transpose(2, 1, 0, 3)
    out["wo%d" % l] = np.ascontiguousarray(wo)
    wu = np.asarray(inp["ffn_w_up"][l], np.float32)
    g = np.zeros((D, NJ * 128), np.float32)
    v = np.zeros((D, NJ * 128), np.float32)
    g[:, :DFF] = wu[:, :DFF]
    v[:, :DFF] = wu[:, DFF:]
    gv = np.concatenate([g.reshape(D, NJ, 128), v.reshape(D, NJ, 128)], axis=2)
    gv = gv.reshape(8, 128, NJ, 256).transpose(2, 1, 0, 3)
    out["wup%d" % l] = np.ascontiguousarray(gv)
    wd = np.zeros((NJ * 128, D), np.float32)
    wd[:DFF] = np.asarray(inp["ffn_w_down"][l], np.float32)
    out["wdn%d" % l] = np.ascontiguousarray(wd.reshape(NJ, 128, D))
    return out


def _lambda_init(layer):
    return 0.8 - 0.6 * math.exp(-0.3 * layer)


class Pool_:
    def __init__(self, ap_u16, nbytes):
        self.ap = ap_u16
        self.n = nbytes
        self.off = 0

    def alloc(self, free_shape, dt):
        esz = 4 if dt == F32 else 2
        nel = int(np.prod(free_shape))
        nb = nel * esz
        off = (self.off + 63) // 64 * 64
        assert off + nb <= self.n, ("SBUF pool overflow", off, nb, self.n)
        self.off = off + nb
        a = self.ap[:, off // 2:(off + nb) // 2]
        if dt == F32:
            a = a.bitcast(F32)
        if len(free_shape) == 2:
            a = a.rearrange("p (a b) -> p a b", a=free_shape[0])
        elif len(free_shape) == 3:
            a = a.rearrange("p (a b c) -> p a b c", a=free_shape[0], b=free_shape[1])
        return a


def build_nc(layers, do_final):
    nc = bass.Bass("TRN2", target_bir_lowering=False)
    nc.allow_low_precision("bf16 matmul operands with fp32 PSUM accumulation by design")
    dram = lambda n, s, k="ExternalInput": nc.dram_tensor(n, list(s), F32, kind=k).ap()
    xin = dram("xin", [D, S])
    yout = dram("yout", [D, S], "ExternalOutput")
    vecs_d = dram("vecs", [128, NV])
    bm_d = dram("bm", [128, 3, 16, 128])
    masks_d = dram("masks", [128, 2, 128])
    consts_d = dram("consts", [128, 2, 128])
    wd = {}
    for l in layers:
        mixer = l % 3
        wd["wqkv%d" % l] = dram("wqkv%d" % l, [8, 128, 8, 384])
        wd["wo%d" % l] = dram("wo%d" % l, [8, 128, 8, 128])
        wd["wup%d" % l] = dram("wup%d" % l, [NJ, 128, 8, 256])
        wd["wdn%d" % l] = dram("wdn%d" % l, [NJ, 128, D])

    with ExitStack() as es:
        POOLB = 212480
        pool_t = es.enter_context(nc.sbuf_tensor("pool", [128, POOLB // 2], BF16))
        A = Pool_(pool_t[:, :], POOLB)
        ps = [es.enter_context(nc.psum_tensor("ps%d" % i, [128, 512], F32))[:, :] for i in range(8)]
        P = Prog(nc)

        xT = A.alloc([8, S], F32)
        hT = A.alloc([8, S], BF16)
        vecs = A.alloc([NV], F32)
        ones_b = A.alloc([128], BF16)
        tri_b = A.alloc([128], BF16)
        nones_b = A.alloc([128], BF16)
        ntri_b = A.alloc([128], BF16)
        ones_dmb = A.alloc([128], BF16)
        ones_eb = A.alloc([128], BF16)
        ones_dm = A.alloc([128], F32)
        ones_e = A.alloc([128], F32)
        cst_f = A.alloc([2, 128], F32)
        small = A.alloc([64], F32)
        persist_end = A.off

        eps_ap = small[:, 4:5]
        NEGLAM = 0
        SUBS = 2
        ESINK = 8

        for c in range(8):
            P.dma("sp", xT[:, c, :], xin[c * 128:(c + 1) * 128, :], writes=["x%d_%d" % (c, t) for t in range(4)])
        P.dma("sp", vecs, vecs_d, writes=["vecs"])
        P.dma("sp", cst_f, consts_d, writes=["cstf"])
        P.add("dve", lambda e: e.memset(small[:, 4:5], EPS), writes=["small"])
        P.add("dve", lambda e: e.tensor_copy(ones_b, cst_f[:, 0, :]), reads=["cstf"], writes=["ones_b"])
        P.add("dve", lambda e: e.tensor_copy(tri_b, cst_f[:, 1, :]), reads=["cstf"], writes=["tri_b"])
        P.add("dve", lambda e: e.tensor_scalar(nones_b, cst_f[:, 0, :], -1.0, None, ALU.mult), reads=["cstf"], writes=["ones_b"])
        P.add("dve", lambda e: e.tensor_scalar(ntri_b, cst_f[:, 1, :], -1.0, None, ALU.mult), reads=["cstf"], writes=["tri_b"])
        P.add("dve", lambda e: e.tensor_scalar(ones_dm, cst_f[:, 0, :], 1.0 / D, None, ALU.mult), reads=["cstf"], writes=["ones_dm"])
        P.add("dve", lambda e: e.tensor_scalar(ones_dmb, cst_f[:, 0, :], 1.0 / D, None, ALU.mult), reads=["cstf"], writes=["ones_dm"])
        P.add("dve", lambda e: e.tensor_scalar(ones_eb, cst_f[:, 0, :], 1.0 / 128, None, ALU.mult), reads=["cstf"], writes=["ones_e"])
        P.add("dve", lambda e: e.tensor_scalar(ones_e, cst_f[:, 0, :], 1.0 / 128, None, ALU.mult), reads=["cstf"], writes=["ones_e"])

        def rmsnorm_to_h(gcol, tag):
            sq = [A.alloc([512], BF16) for _ in range(4)]
            rstd = [A.alloc([512], F32) for _ in range(2)]
            k = 0
            for tg in range(4):
                sl = slice(tg * 512, (tg + 1) * 512)
                bank = 6 + (tg % 2)
                for c in range(8):
                    b = k % 4
                    k += 1
                    eng = "act"
                    if eng == "act":
                        P.add("act", lambda e, o=sq[b], i=xT[:, c, sl]: e.activation(o, i, AF.Square),
                              reads=["x%d_%d" % (c, tg)], writes=["sq%d" % b])
                    else:
                        P.add("pool", lambda e, o=sq[b], i=xT[:, c, sl]: e.tensor_tensor(o, i, i, ALU.mult),
                              reads=["x%d_%d" % (c, tg)], writes=["sq%d" % b])
                    P.add("pe", lambda e, o=ps[bank], r=sq[b], st=(c == 0), sp_=(c == 7): e.matmul(o, ones_dmb, r, start=st, stop=sp_),
                          reads=["sq%d" % b, "ones_dm"], writes=["ps%d" % bank])
                rb = rstd[tg % 2]
                P.add("act", lambda e, o=rb, i=ps[bank]: e.activation(o, i, AF.Ln, bias=eps_ap, scale=1.0),
                      reads=["ps%d" % bank, "small"], writes=["rstd%d" % (tg % 2)])
                P.add("act", lambda e, o=rb: e.activation(o, o, AF.Exp, scale=-0.5),
                      reads=["rstd%d" % (tg % 2)], writes=["rstd%d" % (tg % 2)])
                for c in range(8):
                    P.add("dve", lambda e, o=hT[:, c, sl], i=xT[:, c, sl], g=vecs[:, gcol + c:gcol + c + 1], r=rb:
                          e.scalar_tensor_tensor(o, i, g, r, ALU.mult, ALU.mult),
                          reads=["x%d_%d" % (c, tg), "vecs", "rstd%d" % (tg % 2)], writes=["h%d_%d" % (c, tg)])

        H_ALL = ["h%d_%d" % (c, t) for c in range(8) for t in range(4)]

        def load_w(q, dst, src, name):
            P.dma(q, dst, src, writes=[name])

        def proj_qkv_head(wt, wname, qT, kT, vS, qscale, tag, banks, vtag="v", kpad=None, qpad=None, do_kv=True, evac="act"):
            bi = 0
            for which, dst in ((0, qT), (1, kT)):
                if which == 1 and not do_kv:
                    continue
                for tg in range(4):
                    sl = slice(tg * 512, (tg + 1) * 512)
                    bank = banks[bi % len(banks)]
                    bi += 1
                    for kc in range(8):
                        P.add("pe", lambda e, o=ps[bank], w=wt[:, kc, which * 128:(which + 1) * 128], r=hT[:, kc, sl], st=(kc == 0), sp_=(kc == 7):
                              e.matmul(o, w, r, start=st, stop=sp_),
                              reads=[wname, "h%d_%d" % (kc, tg)], writes=["ps%d" % bank])
                    rn = ("q" if which == 0 else "k") + tag + "_%d" % tg
                    if which == 0 and qpad is not None:
                        P.add("act", lambda e, o=qpad[0][0:64, sl], i=ps[bank][0:64, :]: e.activation(o, i, AF.Identity, scale=qscale),
                              reads=["ps%d" % bank], writes=[rn + "a"])
                        P.add("act", lambda e, o=qpad[1][64:128, sl], i=ps[bank][64:128, :]: e.activation(o, i, AF.Identity, scale=qscale),
                              reads=["ps%d" % bank], writes=[rn + "b"])
                    elif which == 0 and qscale != 1.0:
                        if evac == "dve":
                            P.add("dve", lambda e, o=dst[:, sl], i=ps[bank]: e.tensor_scalar(o, i, qscale, None, ALU.mult),
                                  reads=["ps%d" % bank], writes=[rn])
                        else:
                            P.add("act", lambda e, o=dst[:, sl], i=ps[bank]: e.activation(o, i, AF.Identity, scale=qscale),
                                  reads=["ps%d" % bank], writes=[rn])
                    elif which == 1 and kpad is not None:
                        P.add("dve", lambda e, o=kpad[0][0:64, sl], i=ps[bank][0:64, :]: e.tensor_copy(o, i),
                              reads=["ps%d" % bank], writes=[rn + "a"])
                        if evac == "dve":
                            P.add("dve", lambda e, o=kpad[1][64:128, sl], i=ps[bank][64:128, :]: e.tensor_copy(o, i),
                                  reads=["ps%d" % bank], writes=[rn + "b"])
                        else:
                            P.add("act", lambda e, o=kpad[1][64:128, sl], i=ps[bank][64:128, :]: e.activation(o, i, AF.Identity),
                                  reads=["ps%d" % bank], writes=[rn + "b"])
                    else:
                        P.add("dve", lambda e, o=dst[:, sl], i=ps[bank]: e.tensor_copy(o, i),
                              reads=["ps%d" % bank], writes=[rn])
            for t4 in range(4 if do_kv else 0):
                bank = banks[bi % len(banks)]
                bi += 1
                for tt in range(4):
                    tok = slice((t4 * 4 + tt) * 128, (t4 * 4 + tt + 1) * 128)
                    for kc in range(8):
                        P.add("pe", lambda e, o=ps[bank][:, tt * 128:(tt + 1) * 128], l=hT[:, kc, tok], r=wt[:, kc, 256:384], st=(kc == 0), sp_=(kc == 7):
                              e.matmul(o, l, r, start=st, stop=sp_),
                              reads=[wname, "h%d_%d" % (kc, t4)], writes=["ps%d" % bank])
                if evac == "dve":
                    P.add("dve", lambda e, o=vS[:, t4 * 4:(t4 + 1) * 4, :], i=ps[bank].rearrange("p (a b) -> p a b", a=4): e.tensor_copy(o, i),
                          reads=["ps%d" % bank], writes=[vtag + "_%d" % t4])
                else:
                    P.add("act", lambda e, o=vS[:, t4 * 4:(t4 + 1) * 4, :], i=ps[bank].rearrange("p (a b) -> p a b", a=4): e.activation(o, i, AF.Identity),
                          reads=["ps%d" % bank], writes=[vtag + "_%d" % t4])

        def oproj(l, parts):
            wo = [A.alloc([8, 128], BF16) for _ in range(2)]
            for dc in range(8):
                wb = dc % 2
                load_w("pool", wo[wb], wd["wo%d" % l][dc], "wo%d" % wb)
                for tg in range(4):
                    sl = slice(tg * 512, (tg + 1) * 512)
                    bank = 6 + ((dc * 4 + tg) % 2)
                    for kc in range(8):
                        P.add("pe", lambda e, o=ps[bank], w=wo[wb][:, kc, :], r=oT[:, kc, sl], st=(kc == 0), sp_=(kc == 7):
                              e.matmul(o, w, r, start=st, stop=sp_),
                              reads=["wo%d" % wb] + (["o%d_%d_0" % (kc, tg), "o%d_%d_1" % (kc, tg)] if parts else ["o%d_%d" % (kc, tg)]), writes=["ps%d" % bank])
                    P.add("dve", lambda e, o=xT[:, dc, sl], i=ps[bank]: e.tensor_tensor(o, i, o, ALU.add),
                          reads=["ps%d" % bank, "x%d_%d" % (dc, tg)], writes=["x%d_%d" % (dc, tg)])

        def run_pipeline(N, stages):
            maxlag = max(lg for lg, _ in stages)
            for it in range(N + maxlag):
                for lg, fn in stages:
                    n = it - lg
                    if 0 <= n < N:
                        fn(n)

        def attn_da(l):
            slot = l // 3
            li = _lambda_init(l)
            nonlocal oT
            oT = A.alloc([8, S], BF16)
            bmh = [A.alloc([2, 2, 128], F32) for _ in range(2)]
            wts = [A.alloc([8, 384], BF16) for _ in range(2)]
            qT = A.alloc([S], BF16)
            kT = None
            kz = [A.alloc([S], BF16) for _ in range(2)]
            P.add("pool", lambda e: e.memset(kz[0][64:128, :], 0.0), writes=["kz0pad"])
            P.add("pool", lambda e: e.memset(kz[1][0:64, :], 0.0), writes=["kz1pad"])
            vSb = [A.alloc([16, 128], BF16) for _ in range(2)]
            NS, NP = 3, 3
            Pb = [A.alloc([512], BF16) for _ in range(NP)]
            tD = [A.alloc([256], F32) for _ in range(2)]
            tmp = [A.alloc([512], F32) for _ in range(5)]
            sqb = tmp[3].bitcast(BF16)[:, 0:512]
            lam = vecs[:, V_LAM + slot * 256:V_LAM + slot * 256 + 256]
            P.add("dve", lambda e: e.tensor_tensor(tmp[0][:, 0:64], lam[:, 0:64], lam[:, 64:128], ALU.mult), reads=["vecs"], writes=["tmp0"])
            P.add("dve", lambda e: e.tensor_tensor(tmp[0][:, 64:128], lam[:, 128:192], lam[:, 192:256], ALU.mult), reads=["vecs", "tmp0"], writes=["tmp0"])
            P.add("dve", lambda e: e.reduce_sum(tmp[1][:, 0:2], tmp[0][:, 0:128].rearrange("p (a b) -> p a b", a=2), mybir.AxisListType.X),
                  reads=["tmp0"], writes=["tmp1"])
            P.add("act", lambda e: e.activation(tmp[1][:, 2:4], tmp[1][:, 0:2], AF.Exp), reads=["tmp1"], writes=["tmp1"])
            P.add("dve", lambda e: e.scalar_tensor_tensor(small[:, NEGLAM + slot:NEGLAM + slot + 1], tmp[1][:, 3:4], -li, tmp[1][:, 2:3], ALU.add, ALU.subtract),
                  reads=["tmp1"], writes=["small"])
            P.add("dve", lambda e: e.tensor_scalar(small[:, SUBS + slot:SUBS + slot + 1], vecs[:, V_SUB + slot:V_SUB + slot + 1], 1.0 - li, None, ALU.mult),
                  reads=["vecs", "small"], writes=["small"])
            neglam = small[:, NEGLAM + slot:NEGLAM + slot + 1]
            subs = small[:, SUBS + slot:SUBS + slot + 1]

            def prefetch(h):
                load_w("pool", wts[h % 2], wd["wqkv%d" % l][h], "wt%d" % (h % 2))

            def prefetch_bm(h):
                P.dma("sp", bmh[h % 2], bm_d[:, 0:2, 2 * h:2 * h + 2, :], writes=["bmh%d" % (h % 2)])
                for j_ in range(2):
                    ch_ = 2 * h + j_
                    P.add("pool", lambda e, o=bmh[h % 2][:, :, j_, :], c_=vecs[:, V_BC + ch_:V_BC + ch_ + 1]: e.tensor_scalar(o, o, c_, None, ALU.subtract),
                          reads=["vecs", "bmh%d" % (h % 2)], writes=["bmh%d" % (h % 2)])

            tiles = []
            gi = 0
            for h in range(8):
                for g in range(4):
                    for j in range(2):
                        for kt in range(4 * g + 4):
                            tiles.append(dict(h=h, g=g, j=j, kt=kt, gi=gi, first=(kt == 0), last=(kt == 4 * g + 3),
                                              head_first=(g == 0 and j == 0 and kt == 0)))
                        gi += 1
            N = len(tiles)
            deferred = []

            def sS(n):
                t = tiles[n]
                h, g, j, kt = t["h"], t["g"], t["j"], t["kt"]
                if t["head_first"]:
                    if h == 0:
                        prefetch(0)
                        prefetch_bm(0)
                    if h + 1 < 8:
                        prefetch(h + 1)
                    proj_qkv_head(wts[h % 2], "wt%d" % (h % 2), qT, kT, vSb[h % 2], 0.125, "", banks=(7, n % NS), vtag="v%d" % (h % 2), kpad=kz, evac="act")
                if g == 1 and j == 0 and kt == 0 and h + 1 < 8:
                    prefetch_bm(h + 1)
                i = kt - 4 * g
                q0 = max(i, 0) * 128
                sb = n % NS
                rows = slice(j * 64, (j + 1) * 64)
                P.add("pe", lambda e, o=ps[sb][:, q0:512], l_=kz[j][:, kt * 128:(kt + 1) * 128], r=qT[:, g * 512 + q0:(g + 1) * 512]:
                      e.matmul(o, l_, r, start=True, stop=True),
                      reads=["k_%da" % (kt // 4), "k_%db" % (kt // 4), "kz0pad", "kz1pad", "q_%d" % g], writes=["ps%d" % sb])

            def sB(n):
                t = tiles[n]
                h, g, j, kt = t["h"], t["g"], t["j"], t["kt"]
                i = kt - 4 * g
                if i < -1:
                    return
                q0 = max(i, 0) * 128
                sb = n % NS
                bm = bmh[h % 2]
                bn = "bmh%d" % (h % 2)
                if i >= 0 and i < 3:
                    P.add("dve", lambda e, o=ps[sb][:, q0:q0 + 256].rearrange("p (a b) -> p a b", a=2), b=bm[:, 0:2, j, :]: e.tensor_tensor(o, o, b, ALU.add),
                          reads=["ps%d" % sb, bn], writes=["ps%d" % sb])
                elif i == 3:
                    P.add("dve", lambda e, o=ps[sb][:, q0:q0 + 128], b=bm[:, 0, j, :]: e.tensor_tensor(o, o, b, ALU.add),
                          reads=["ps%d" % sb, bn], writes=["ps%d" % sb])
                else:
                    P.add("dve", lambda e, o=ps[sb][:, 0:128], b=bm[:, 1, j, :]: e.tensor_tensor(o, o, b, ALU.add),
                          reads=["ps%d" % sb, bn], writes=["ps%d" % sb])

            def sP(n):
                t = tiles[n]
                h, g, j, kt = t["h"], t["g"], t["j"], t["kt"]
                i = kt - 4 * g
                q0 = max(i, 0) * 128
                sb = n % NS
                Pt = Pb[n % NP]
                pn = "P%d" % (n % NP)
                P.add("act", lambda e, o=Pt[:, q0:512], i_=ps[sb][:, q0:512]: e.activation(o, i_, AF.Exp),
                      reads=["ps%d" % sb], writes=[pn + "a"])

            def sV(n):
                t = tiles[n]
                h, g, j, kt, gi_ = t["h"], t["g"], t["j"], t["kt"], t["gi"]
                i = kt - 4 * g
                q0 = max(i, 0) * 128
                Pt = Pb[n % NP]
                pn = "P%d" % (n % NP)
                ob, db = 3 + gi_ % 2, 5 + gi_ % 2
                first, last = t["first"], t["last"]
                vS = vSb[h % 2]
                P.add("pe", lambda e, o=ps[ob][:, q0:512], l_=vS[:, kt, :], r=Pt[:, q0:512]: e.matmul(o, l_, r, start=first, stop=last),
                      reads=[pn + "a", pn + "b", "v%d_%d" % (h % 2, kt // 4)], writes=["ps%d" % ob])
                P.add("pe", lambda e, o=ps[db][:, q0:512], r=Pt[:, q0:512]: e.matmul(o, ones_b, r, start=first, stop=last),
                      reads=[pn + "a", pn + "b", "ones_b"], writes=["ps%d" % db])
                if not last:
                    return
                r0, a0, a1, sq, rs = tmp
                aj = a0 if j == 0 else a1
                an = "tmp1" if j == 0 else "tmp2"
                Q = deferred.append
                Q(lambda d_=ps[db], dn="ps%d" % db: P.add("act", lambda e: e.activation(r0, d_, AF.Ln), reads=[dn], writes=["tmp0"]))
                Q(lambda: P.add("act", lambda e: e.activation(r0, r0, AF.Exp, scale=-1.0), reads=["tmp0"], writes=["tmp0"]))
                Q(lambda o_=ps[ob], a_=aj, on_="ps%d" % ob, an_=an: P.add("dve", lambda e: e.tensor_tensor(a_, o_, r0, ALU.mult), reads=[on_, "tmp0"], writes=[an_]))
                if j == 1:
                    sl = slice(g * 512, (g + 1) * 512)
                    Q(lambda: P.add("dve", lambda e: e.scalar_tensor_tensor(a0, a1, neglam, a0, ALU.mult, ALU.add), reads=["tmp1", "tmp2", "small"], writes=["tmp1"]))
                    Q(lambda: P.add("act", lambda e: e.activation(sqb, a0, AF.Square), reads=["tmp1"], writes=["tmp3"]))
                    Q(lambda: P.add("pe", lambda e: e.matmul(ps[7], ones_eb, sqb, start=True, stop=True), reads=["tmp3", "ones_e"], writes=["ps7"]))
                    Q(lambda: P.add("act", lambda e: e.activation(rs, ps[7], AF.Ln, bias=eps_ap, scale=1.0), reads=["ps7", "small"], writes=["tmp4"]))
                    Q(lambda: P.add("act", lambda e: e.activation(rs, rs, AF.Exp, scale=-0.5), reads=["tmp4"], writes=["tmp4"]))
                    Q(lambda o=oT[:, h, sl], on_="o%d_%d" % (h, g): P.add("dve", lambda e: e.scalar_tensor_tensor(o, a0, subs, rs, ALU.mult, ALU.mult),
                                                                  reads=["tmp1", "tmp4", "small"], writes=[on_]))

            def sD(n):
                k_ = 2 if len(deferred) > 6 else 1
                for _ in range(k_):
                    if deferred:
                        deferred.pop(0)()

            run_pipeline(N, [(0, sS), (1, sB), (2, sP), (3, sV), (3, sD)])
            while deferred:
                deferred.pop(0)()

        def attn_sb(l):
            nonlocal oT
            oT = A.alloc([8, S], BF16)
            mk = A.alloc([2, 128], F32)
            P.dma("sp", mk, masks_d, writes=["mk"])
            wts = [A.alloc([8, 384], BF16) for _ in range(2)]
            qT = A.alloc([S], BF16)
            kT = None
            kz = [A.alloc([S], BF16) for _ in range(2)]
            P.add("pool", lambda e: e.memset(kz[0][64:128, :], 0.0), writes=["kz0pad"])
            P.add("pool", lambda e: e.memset(kz[1][0:64, :], 0.0), writes=["kz1pad"])
            vSb = [A.alloc([16, 128], BF16) for _ in range(2)]
            NZ, NW, NL, NA = 5, 4, 2, 2
            HSEL = ["dve"]
            LSEL = ["pool"]
            Wb = [A.alloc([512], F32) for _ in range(NW)]
            NE = 2 if all(h_ == "dve" for h_ in HSEL) else NW
            Eb = [A.alloc([512], F32) for _ in range(NE)]
            Lh = [A.alloc([512], BF16) for _ in range(NL)]
            Ll = [A.alloc([512], BF16) for _ in range(NL)]
            Ab = [A.alloc([512], BF16) for _ in range(NA)]
            Cb = [A.alloc([512], F32) for _ in range(2)]

            tiles = []
            for c in range(8):
                for r in range(2):
                    for g in range(4):
                        for kt in range(4 * g + 3, -1, -1):
                            tiles.append(dict(c=c, r=r, g=g, kt=kt, first=(kt == 4 * g + 3), last=(kt == 0),
                                              pair_first=(r == 0 and g == 0 and kt == 3)))
            N = len(tiles)
            for n in range(N):
                t = tiles[n]
                i = t["kt"] - 4 * t["g"]
                t["i"] = i
                t["q0"] = max(i, 0) * 128
                t["pq0"] = tiles[n - 1]["q0"] if not t["first"] else 512

            def prefetch(c):
                load_w("pool", wts[c % 2], wd["wqkv%d" % l][c], "wt%d" % (c % 2))

            def sZ(n):
                t = tiles[n]
                c, r, g, kt, q0 = t["c"], t["r"], t["g"], t["kt"], t["q0"]
                if t["pair_first"]:
                    if c == 0:
                        prefetch(0)
                    if c + 1 < 8:
                        prefetch(c + 1)
                    proj_qkv_head(wts[c % 2], "wt%d" % (c % 2), qT, kT, vSb[c % 2], 0.125, "", banks=(5, 6), vtag="v%d" % (c % 2), kpad=kz)
                rows = slice(r * 64, (r + 1) * 64)
                zb = n % NZ
                P.add("pe", lambda e, o=ps[zb][:, q0:512], l_=kz[r][:, kt * 128:(kt + 1) * 128], r_=qT[:, g * 512 + q0:(g + 1) * 512]:
                      e.matmul(o, l_, r_, start=True, stop=False, skip_group_check=True),
                      reads=["k_%da" % (kt // 4), "k_%db" % (kt // 4), "kz0pad", "kz1pad", "q_%d" % g], writes=["ps%d" % zb])

            def sE(n):
                t = tiles[n]
                q0 = t["q0"]
                zb = n % NZ
                W = Wb[n % NW]
                wn = "W%d" % (n % NW)
                E = Eb[n % NE]
                en = "E%d" % (n % NE)
                P.add("act", lambda e, o=E[:, q0:512], i_=ps[zb][:, q0:512]: e.activation(o, i_, AF.Exp), reads=["ps%d" % zb], writes=[en])
                P.add("act", lambda e, o=W[:, q0:512], i_=E[:, q0:512]: e.activation(o, i_, AF.Ln, bias=1.0, scale=1.0), reads=[en], writes=[wn])

            def sM(n):
                t = tiles[n]
                q0 = t["q0"]
                W = Wb[n % NW]
                wn = "W%d" % (n % NW)
                if t["i"] >= 0:
                    P.add("dve", lambda e, o=W[:, q0:q0 + 128], m=mk[:, 0, :]: e.tensor_tensor(o, o, m, ALU.mult), reads=[wn, "mk"], writes=[wn])

            def sL(n):
                t = tiles[n]
                q0 = t["q0"]
                W = Wb[n % NW]
                wn = "W%d" % (n % NW)
                LH, LL = Lh[n % NL], Ll[n % NL]
                hsel = HSEL[n % len(HSEL)]
                if hsel == "act":
                    P.add("act", lambda e, o=LH[:, q0:512], i_=Eb[n % NE][:, q0:512]: e.activation(o, i_, AF.Ln, bias=1.0, scale=1.0),
                          reads=["E%d" % (n % NE)], writes=["LH%d" % (n % NL)])
                else:
                    P.add("dve", lambda e, o=LH[:, q0:512], i_=W[:, q0:512]: e.tensor_copy(o, i_), reads=[wn], writes=["LH%d" % (n % NL)])
                P.add(LSEL[n % len(LSEL)], lambda e, o=LL[:, q0:512], i_=W[:, q0:512], hh=LH[:, q0:512]: e.tensor_tensor(o, i_, hh, ALU.subtract),
                      reads=[wn, "LH%d" % (n % NL)], writes=["LL%d" % (n % NL)])

            def sT(n):
                t = tiles[n]
                q0 = t["q0"]
                zb = n % NZ
                cb = 5 + n % 2
                LH, LL = Lh[n % NL], Ll[n % NL]
                hn, ln_ = "LH%d" % (n % NL), "LL%d" % (n % NL)
                P.add("pe", lambda e, o=ps[zb][:, q0:512], r_=LH[:, q0:512]: e.matmul(o, ntri_b, r_, start=False, stop=False, skip_group_check=True),
                      reads=[hn, "tri_b"], writes=["ps%d" % zb])
                P.add("pe", lambda e, o=ps[zb][:, q0:512], r_=LL[:, q0:512]: e.matmul(o, ntri_b, r_, start=False, stop=True, skip_group_check=True),
                      reads=[ln_, "tri_b"], writes=["ps%d" % zb])
                if not t["last"]:
                    P.add("pe", lambda e, o=ps[cb][:, q0:512], r_=LH[:, q0:512]: e.matmul(o, nones_b, r_, start=True, stop=False),
                          reads=[hn, "ones_b"], writes=["ps%d" % cb])
                    P.add("pe", lambda e, o=ps[cb][:, q0:512], r_=LL[:, q0:512]: e.matmul(o, nones_b, r_, start=False, stop=True),
                          reads=[ln_, "ones_b"], writes=["ps%d" % cb])

            def sC(n):
                t = tiles[n]
                if t["last"]:
                    return
                q0, pq0 = t["q0"], t["pq0"]
                cb = 5 + n % 2
                C, Cp = Cb[n % 2], Cb[(n - 1) % 2]
                cn, cpn = "C%d" % (n % 2), "C%d" % ((n - 1) % 2)
                if pq0 > q0:
                    P.add("dve", lambda e, o=C[:, q0:pq0], i_=ps[cb][:, q0:pq0]: e.tensor_copy(o, i_), reads=["ps%d" % cb], writes=[cn + "a"])
                if pq0 < 512:
                    P.add("dve", lambda e, o=C[:, pq0:512], i_=ps[cb][:, pq0:512], p_=Cp[:, pq0:512]: e.tensor_tensor(o, i_, p_, ALU.add),
                          reads=["ps%d" % cb, cpn + "a", cpn + "b"], writes=[cn + "b"])

            def sX(n):
                t = tiles[n]
                q0, pq0 = t["q0"], t["pq0"]
                zb = n % NZ
                W = Wb[n % NW]
                wn = "W%d" % (n % NW)
                Cp = Cb[(n - 1) % 2]
                cpn = "C%d" % ((n - 1) % 2)
                c0 = q0
                if t["i"] >= 0:
                    P.add("dve", lambda e, o=W[:, q0:q0 + 128], i_=ps[zb][:, q0:q0 + 128], m=mk[:, 1, :]: e.tensor_tensor(o, i_, m, ALU.add),
                          reads=["ps%d" % zb, "mk", wn], writes=[wn])
                    c0 = q0 + 128
                if c0 < 512:
                    P.add("dve", lambda e, o=W[:, c0:512], i_=ps[zb][:, c0:512], c_=Cp[:, c0:512]: e.tensor_tensor(o, i_, c_, ALU.add),
                          reads=["ps%d" % zb, cpn + "a", cpn + "b", wn], writes=[wn])

            def sA(n):
                t = tiles[n]
                q0 = t["q0"]
                W = Wb[n % NW]
                wn = "W%d" % (n % NW)
                P.add("act", lambda e, o=Ab[n % NA][:, q0:512], i_=W[:, q0:512]: e.activation(o, i_, AF.Exp), reads=[wn], writes=["A%d" % (n % NA)])

            def sV(n):
                t = tiles[n]
                c, r, g, kt, q0 = t["c"], t["r"], t["g"], t["kt"], t["q0"]
                rows = slice(r * 64, (r + 1) * 64)
                vS = vSb[c % 2]
                P.add("pe", lambda e, o=ps[7][:, q0:512], l_=vS[:, kt, :], r_=Ab[n % NA][:, q0:512], st=t["first"], sp_=t["last"]:
                      e.matmul(o, l_, r_, start=st, stop=sp_, skip_group_check=True),
                      reads=["A%d" % (n % NA), "v%d_%d" % (c % 2, kt // 4)], writes=["ps7"])
                if t["last"]:
                    sl = slice(g * 512, (g + 1) * 512)
                    P.add("act", lambda e, o=oT[rows, c, sl], i_=ps[7][rows, :]: e.activation(o, i_, AF.Identity),
                          reads=["ps7"], writes=["o%d_%d_%d" % (c, g, r)])

            run_pipeline(N, list(zip([0, 1, 2, 2, 3, 3, 3, 4, 5], [sZ, sE, sM, sL, sT, sC, sX, sA, sV])))

        def attn_swa(l):
            nonlocal oT
            oT = A.alloc([8, S], BF16)
            bmh = [A.alloc([2, 2, 128], F32) for _ in range(2)]
            wts = [A.alloc([8, 384], BF16) for _ in range(2)]
            qz = [A.alloc([S], BF16) for _ in range(2)]
            P.add("pool", lambda e: e.memset(qz[0][64:128, :], 0.0), writes=["qz0pad"])
            P.add("pool", lambda e: e.memset(qz[1][0:64, :], 0.0), writes=["qz1pad"])
            kT = A.alloc([S], BF16)
            vSb = [A.alloc([16, 128], BF16) for _ in range(2)]
            Tt = [[A.alloc([512], F32) for _ in range(2)] for _ in range(2)]
            Pt = [[A.alloc([512], BF16) for _ in range(2)] for _ in range(3)]
            rr = [A.alloc([512], F32) for _ in range(2)]
            P.add("act", lambda e: e.activation(small[:, ESINK:ESINK + 16], vecs[:, V_SINK:V_SINK + 16], AF.Exp), reads=["vecs", "small"], writes=["small"])

            def prefetch(c):
                load_w("pool", wts[c % 2], wd["wqkv%d" % l][c], "wt%d" % (c % 2))

            def prefetch_bm(c):
                P.dma("sp", bmh[c % 2][:, 0], bm_d[:, 0, 2 * c:2 * c + 2, :], writes=["bmh%da" % (c % 2)])
                P.dma("sp", bmh[c % 2][:, 1], bm_d[:, 2, 2 * c:2 * c + 2, :], writes=["bmh%db" % (c % 2)])

            its = [(c, r, qq) for c in range(8) for r in range(2) for qq in range(4)]
            N = len(its)

            def sS(n):
                c, r, qq = its[n]
                if r == 0 and qq == 0:
                    if c == 0:
                        prefetch(0)
                        prefetch_bm(0)
                    if c + 1 < 8:
                        prefetch(c + 1)
                    proj_qkv_head(wts[c % 2], "wt%d" % (c % 2), None, kT, vSb[(c // 2) % 2], 0.125, "", banks=(0, 1),
                                  vtag="v%d" % ((c // 2) % 2), qpad=qz, do_kv=(c % 2 == 0))
                if r == 1 and qq == 0 and c + 1 < 8:
                    prefetch_bm(c + 1)
                b0, b1 = 2 * (n % 2), 2 * (n % 2) + 1
                for t4 in range(4):
                    qt = qq * 4 + t4
                    cs = slice(t4 * 128, (t4 + 1) * 128)
                    qrd = ["q_%da" % qq, "q_%db" % qq, "qz0pad", "qz1pad"]
                    P.add("pe", lambda e, o=ps[b0][:, cs], l_=kT[:, qt * 128:(qt + 1) * 128], r_=qz[r][:, qt * 128:(qt + 1) * 128]:
                          e.matmul(o, l_, r_, start=True, stop=True), reads=["k_%d" % qq] + qrd, writes=["ps%d" % b0])
                    if qt > 0:
                        P.add("pe", lambda e, o=ps[b1][:, cs], l_=kT[:, (qt - 1) * 128:qt * 128], r_=qz[r][:, qt * 128:(qt + 1) * 128]:
                              e.matmul(o, l_, r_, start=True, stop=True), reads=["k_%d" % ((qt - 1) // 4)] + qrd, writes=["ps%d" % b1])

            def sB(n):
                c, r, qq = its[n]
                b0, b1 = 2 * (n % 2), 2 * (n % 2) + 1
                p0 = 0 if qq > 0 else 128
                np1 = (512 - p0) // 128
                bm = bmh[c % 2]
                P.add("dve", lambda e, o=ps[b0].rearrange("p (a b) -> p a b", a=4),
                      b=bm[:, 0, r, :].unsqueeze(1).broadcast_to([128, 4, 128]): e.tensor_tensor(o, o, b, ALU.add),
                      reads=["ps%d" % b0, "bmh%da" % (c % 2)], writes=["ps%d" % b0])
                P.add("dve", lambda e, o=ps[b1][:, p0:512].rearrange("p (a b) -> p a b", a=np1),
                      b=bm[:, 1, r, :].unsqueeze(1).broadcast_to([128, np1, 128]): e.tensor_tensor(o, o, b, ALU.add),
                      reads=["ps%d" % b1, "bmh%db" % (c % 2)], writes=["ps%d" % b1])

            def sP(n):
                c, r, qq = its[n]
                b0, b1 = 2 * (n % 2), 2 * (n % 2) + 1
                P0, P1 = Pt[n % 3]
                p0 = 0 if qq > 0 else 128
                P.add("act", lambda e, o=P0, i_=ps[b0]: e.activation(o, i_, AF.Exp), reads=["ps%d" % b0], writes=["Pt%d_0" % (n % 3)])
                P.add("act", lambda e, o=P1[:, p0:512], i_=ps[b1][:, p0:512]: e.activation(o, i_, AF.Exp), reads=["ps%d" % b1], writes=["Pt%d_1" % (n % 3)])

            def sV(n):
                c, r, qq = its[n]
                P0, P1 = Pt[n % 3]
                ob, db = 4 + n % 2, 6 + n % 2
                vS = vSb[(c // 2) % 2]
                vt = "v%d" % ((c // 2) % 2)
                for t4 in range(4):
                    qt = qq * 4 + t4
                    cs = slice(t4 * 128, (t4 + 1) * 128)
                    has_prev = qt > 0
                    P.add("pe", lambda e, o=ps[ob][:, cs], l_=vS[:, qt, :], r_=P0[:, cs], sp_=(not has_prev):
                          e.matmul(o, l_, r_, start=True, stop=sp_), reads=["Pt%d_0" % (n % 3), vt + "_%d" % qq], writes=["ps%d" % ob])
                    if has_prev:
                        P.add("pe", lambda e, o=ps[ob][:, cs], l_=vS[:, qt - 1, :], r_=P1[:, cs]:
                              e.matmul(o, l_, r_, start=False, stop=True), reads=["Pt%d_1" % (n % 3), vt + "_%d" % ((qt - 1) // 4)], writes=["ps%d" % ob])
                    P.add("pe", lambda e, o=ps[db][:, cs], r_=P0[:, cs], sp_=(not has_prev):
                          e.matmul(o, ones_b, r_, start=True, stop=sp_), reads=["Pt%d_0" % (n % 3), "ones_b"], writes=["ps%d" % db])
                    if has_prev:
                        P.add("pe", lambda e, o=ps[db][:, cs], r_=P1[:, cs]:
                              e.matmul(o, ones_b, r_, start=False, stop=True), reads=["Pt%d_1" % (n % 3), "ones_b"], writes=["ps%d" % db])

            def sN(n):
                c, r, qq = its[n]
                hq = 2 * c + r
                rows = slice(r * 64, (r + 1) * 64)
                sl = slice(qq * 512, (qq + 1) * 512)
                ob, db = 4 + n % 2, 6 + n % 2
                rb = rr[n % 2]
                rn = "rr%d" % (n % 2)
                P.add("act", lambda e, o=rb, i_=ps[db], s_=small[:, ESINK + hq:ESINK + hq + 1]: e.activation(o, i_, AF.Ln, bias=s_, scale=1.0),
                      reads=["ps%d" % db, "small"], writes=[rn])
                P.add("act", lambda e, o=rb: e.activation(o, o, AF.Exp, scale=-1.0), reads=[rn], writes=[rn])
                P.add("dve", lambda e, o=oT[rows, c, sl], i_=ps[ob][rows, :], r_=rb[rows, :]: e.tensor_tensor(o, i_, r_, ALU.mult),
                      reads=["ps%d" % ob, rn], writes=["o%d_%d_%d" % (c, qq, r)])

            run_pipeline(N, [(0, sS), (0, sB), (1, sP), (1, sV), (2, sN)])

        def ffn(l):
            wup = [A.alloc([8, 256], BF16) for _ in range(3)]
            wdn = [A.alloc([6, D], BF16) for _ in range(2)]
            ag = A.alloc([6, S], BF16)
            Ub = [[A.alloc([514], F32) for _ in range(2)] for _ in range(2)]
            Tb = [[A.alloc([512], F32) for _ in range(2)] for _ in range(2)]
            Sg = [A.alloc([512], F32) for _ in range(2)]
            for j in range(2):
                load_w("pool", wup[j], wd["wup%d" % l][j], "wup%d" % j)
            j0_, nj_ = FFN_GROUPS[0]
            load_w("pool", wdn[0][:, 0:nj_, :], wd["wdn%d" % l][j0_:j0_ + nj_].rearrange("j p m -> p j m"), "wdn0")
            cnt = [0, 0]
            pre_up = set()
            pending_up = []
            for gi, (j0, nj) in enumerate(FFN_GROUPS):
                ab = gi % 2
                if gi + 1 < len(FFN_GROUPS):
                    j1, nj1 = FFN_GROUPS[gi + 1]
                    load_w("pool", wdn[1 - ab][:, 0:nj1, :], wd["wdn%d" % l][j1:j1 + nj1].rearrange("j p m -> p j m"), "wdn%d" % (1 - ab))
                tl = [(jj, tg) for jj in range(nj) for tg in range(4)]
                base = cnt[0]
                cnt[0] += len(tl)

                def s_up(m, gi=gi, j0=j0, tl=tl, base=base):
                    if (gi, m) in pre_up:
                        return
                    pre_up.add((gi, m))
                    jj, tg = tl[m]
                    j = j0 + jj
                    n = base + m
                    wb = j % 3
                    if tg == 0 and j + 2 < NJ:
                        load_w("pool", wup[(j + 2) % 3], wd["wup%d" % l][j + 2], "wup%d" % ((j + 2) % 3))
                    sl = slice(tg * 512, (tg + 1) * 512)
                    pb = (n % 2) * 2
                    for half in (1, 0):
                        for kc in range(8):
                            P.add("pe", lambda e, o=ps[pb + half], w=wup[wb][:, kc, half * 128:(half + 1) * 128], r=hT[:, kc, sl], st=(kc == 0), sp_=(kc == 7):
                                  e.matmul(o, w, r, start=st, stop=sp_), reads=["wup%d" % wb, "h%d_%d" % (kc, tg)], writes=["ps%d" % (pb + half)])

                def s_conv(m):
                    jj, tg = tl[m]
                    j = j0 + jj
                    n = base + m
                    b = n % 2
                    pb = (n % 2) * 2
                    cv = V_CONV + (l * NJ + j) * 8
                    taps = []
                    for half in range(2):
                        U = ps[pb + half]
                        un = "ps%d" % (pb + half)
                        ub, T = Ub[half][b], Tb[half][b]
                        umn, uhn, tn = "Um%d_%d" % (half, b), "Uh%d_%d" % (half, b), "T%d_%d" % (half, b)
                        w0 = vecs[:, cv + half * 4 + 0:cv + half * 4 + 1]
                        w1 = vecs[:, cv + half * 4 + 1:cv + half * 4 + 2]
                        w2 = vecs[:, cv + half * 4 + 2:cv + half * 4 + 3]
                        bb = vecs[:, cv + half * 4 + 3:cv + half * 4 + 4]
                        P.add("act", lambda e, o=ub[:, 2:514], i=U: e.activation(o, i, AF.Identity), reads=[un], writes=[umn])
                        P.add("act", lambda e, o=T, i=U, s_=w2, b_=bb: e.activation(o, i, AF.Identity, bias=b_, scale=s_), reads=[un, "vecs"], writes=[tn])
                        if tg == 0:
                            P.add("pool", lambda e, o=ub[:, 0:2]: e.memset(o, 0.0), writes=[uhn])
                        else:
                            P.add("pool", lambda e, o=ub[:, 0:2], i=Ub[half][1 - b][:, 512:514]: e.tensor_copy(o, i),
                                  reads=["Um%d_%d" % (half, 1 - b)], writes=[uhn])
                        taps.append((ub, T, umn, uhn, tn, w0, w1))
                    for which in (1, 0):
                        for (ub, T, umn, uhn, tn, w0, w1) in taps:
                            if which == 1:
                                P.add("dve", lambda e, o=T, i=ub[:, 1:513], s_=w1: e.scalar_tensor_tensor(o, i, s_, o, ALU.mult, ALU.add),
                                      reads=[umn, uhn, "vecs", tn], writes=[tn])
                            else:
                                P.add("dve", lambda e, o=T, i=ub[:, 0:512], s_=w0: e.scalar_tensor_tensor(o, i, s_, o, ALU.mult, ALU.add),
                                      reads=[umn, uhn, "vecs", tn], writes=[tn])

                def s_gate(m):
                    jj, tg = tl[m]
                    n = base + m
                    b = n % 2
                    sl = slice(tg * 512, (tg + 1) * 512)
                    P.add("act", lambda e, o=Sg[b], i=Tb[0][b]: e.activation(o, i, AF.Silu), reads=["T0_%d" % b], writes=["Sg%d" % b])
                    P.add("pool", lambda e, o=ag[:, jj, sl], a_=Sg[b], b_=Tb[1][b]: e.tensor_tensor(o, a_, b_, ALU.mult),
                          reads=["Sg%d" % b, "T1_%d" % b], writes=["a_%d_%d" % (jj, tg)])

                run_pipeline(len(tl), [(0, s_up), (0, s_conv), (1, s_gate)])
                if gi + 1 < len(FFN_GROUPS):
                    j0n, njn = FFN_GROUPS[gi + 1]
                    tln = [(jj, tg) for jj in range(njn) for tg in range(4)]
                    basen = cnt[0]
                    for m_ in range(FFN_PRE):
                        s_up(m_, gi=gi + 1, j0=j0n, tl=tln, base=basen)
                for dc in range(8):
                    for tg in range(4):
                        sl = slice(tg * 512, (tg + 1) * 512)
                        bank = 4 + (cnt[1] % 2)
                        cnt[1] += 1
                        for jj in range(nj):
                            P.add("pe", lambda e, o=ps[bank], w=wdn[ab][:, jj, dc * 128:(dc + 1) * 128], r=ag[:, jj, sl], st=(jj == 0), sp_=(jj == nj - 1):
                                  e.matmul(o, w, r, start=st, stop=sp_), reads=["wdn%d" % ab, "a_%d_%d" % (jj, tg)], writes=["ps%d" % bank])
                        P.add("dve", lambda e, o=xT[:, dc, sl], i=ps[bank]: e.tensor_tensor(o, i, o, ALU.add),
                              reads=["ps%d" % bank, "x%d_%d" % (dc, tg)], writes=["x%d_%d" % (dc, tg)])

        def final_norm():
            sq = [A.alloc([512], BF16) for _ in range(3)]
            rstd = [A.alloc([512], F32) for _ in range(2)]
            yb = [A.alloc([512], F32) for _ in range(4)]
            k = 0
            m = 0
            outs = []
            for tg in range(4):
                sl = slice(tg * 512, (tg + 1) * 512)
                bank = 6 + (tg % 2)
                for c in range(8):
                    b = k % 3
                    k += 1
                    P.add("act", lambda e, o=sq[b], i=xT[:, c, sl]: e.activation(o, i, AF.Square), reads=["x%d_%d" % (c, tg)], writes=["sq%d" % b])
                    P.add("pe", lambda e, o=ps[bank], r=sq[b], st=(c == 0), sp_=(c == 7): e.matmul(o, ones_dmb, r, start=st, stop=sp_),
                          reads=["sq%d" % b, "ones_dm"], writes=["ps%d" % bank])
                rb = rstd[tg % 2]
                P.add("act", lambda e, o=rb, i=ps[bank]: e.activation(o, i, AF.Ln, bias=eps_ap, scale=1.0),
                      reads=["ps%d" % bank, "small"], writes=["rstd%d" % (tg % 2)])
                P.add("act", lambda e, o=rb: e.activation(o, o, AF.Exp, scale=-0.5),
                      reads=["rstd%d" % (tg % 2)], writes=["rstd%d" % (tg % 2)])
                for c in range(8):
                    y = yb[m % 4]
                    yn = "y%d" % (m % 4)
                    m += 1
                    P.add("dve", lambda e, o=y, i=xT[:, c, sl], g=vecs[:, V_FIN + c:V_FIN + c + 1], r=rb: e.scalar_tensor_tensor(o, i, g, r, ALU.mult, ALU.mult),
                          reads=["x%d_%d" % (c, tg), "vecs", "rstd%d" % (tg % 2)], writes=[yn])
                    on = "out%d" % ((m - 1) % 4)
                    P.dma("sp", yout[c * 128:(c + 1) * 128, sl], y, reads=[yn], writes=[on])
                    if on not in outs:
                        outs.append(on)
            return outs

        oT = None
        for l in layers:
            mixer = l % 3
            A.off = persist_end
            P.barrier()
            rmsnorm_to_h(V_AN + 8 * l, "a%d" % l)
            if mixer == 0:
                attn_da(l)
            elif mixer == 1:
                attn_sb(l)
            else:
                attn_swa(l)
            oproj(l, mixer != 0)
            A.off = persist_end
            P.barrier()
            rmsnorm_to_h(V_FN + 8 * l, "f%d" % l)
            if not DEBUG_NOFFN:
                ffn(l)
        outs = []
        A.off = persist_end
        P.barrier()
        if do_final:
            outs = final_norm()
        else:
            for c in range(8):
                on = "out%d" % c
                P.dma("sp", yout[c * 128:(c + 1) * 128, :], xT[:, c, :], reads=["x%d_%d" % (c, t) for t in range(4)], writes=[on])
                outs.append(on)
        P.finish(outs)
        P.emit()
    return nc


LAYER_GROUPS = [[0, 1, 2, 3]]


def kernel(**inputs):
    x = np.asarray(inputs["x"], np.float32)
    B = x.shape[0]
    shared = _prep_shared(inputs)
    cur = [np.ascontiguousarray(x[b].T) for b in range(B)]
    for gi, layers in enumerate(LAYER_GROUPS):
        do_final = (gi == len(LAYER_GROUPS) - 1)
        nc = build_nc(layers, do_final)
        lw = {}
        for l in layers:
            lw.update(_prep_layer(inputs, l))
        in_maps = []
        for b in range(B):
            m = dict(shared)
            m.update(lw)
            m["xin"] = cur[b]
            in_maps.append(m)
        res = run_bass_kernel_spmd(nc, in_maps, core_ids=list(range(B)))
        cur = [np.asarray(res.results[b]["yout"], np.float32) for b in range(B)]
    out = np.stack([c.T for c in cur], axis=0)
    return np.ascontiguousarray(out.astype(np.float32))
```

```python
import math
import numpy as np
from contextlib import ExitStack
import concourse.bass as bass
import concourse.mybir as mybir
from concourse.bass_utils import run_bass_kernel_spmd

F32 = mybir.dt.float32
BF16 = mybir.dt.bfloat16
AF = mybir.ActivationFunctionType
ALU = mybir.AluOpType

D = 1024
S = 2048
DEPTH = 4
DFF = 2752
NJ = 22
EPS = 1e-6
NEGM = -30000.0
FFN_GROUPS = [(0, 6), (6, 6), (12, 5), (17, 5)]
DEBUG_NOFFN = False
FFN_PRE = 2
ATTACH_WAIT = True


class _I:
    __slots__ = ("eng", "fn", "deps", "dma", "sig", "sem", "val", "key")


class Prog:
    ENGS = ("pe", "act", "dve", "pool", "sp")
    SEM_LIMIT = 20000

    def __init__(self, nc, same_eng_sync=True):
        self.nc = nc
        self.same = same_eng_sync
        self.streams = {e: [] for e in self.ENGS}
        self.last_writer = {}
        self.readers = {}
        self.n = 0

    def add(self, eng, fn, reads=(), writes=(), dma=False, nodep=()):
        ins = _I()
        ins.eng, ins.fn, ins.dma, ins.sig, ins.sem, ins.val = eng, fn, dma, False, None, 0
        ins.key = writes[0] if (dma and writes) else None
        deps = set()
        for r in reads:
            w = self.last_writer.get(r)
            if w is not None:
                deps.add(w)
        reads = list(reads) + list(nodep)
        for r in writes:
            w = self.last_writer.get(r)
            if w is not None:
                deps.add(w)
            for rd in self.readers.get(r, ()):
                deps.add(rd)
        for r in reads:
            self.readers.setdefault(r, []).append(ins)
        for r in writes:
            self.last_writer[r] = ins
            self.readers[r] = []
        deps.discard(ins)
        ins.deps = deps
        self.streams[eng].append(ins)
        self.n += 1
        return ins

    def dma(self, q, out, in_, reads=(), writes=()):
        return self.add(q, lambda e: e.dma_start(out=out, in_=in_), reads=reads, writes=writes, dma=True)

    def barrier(self):
        lasts = set()
        for w in self.last_writer.values():
            lasts.add(w)
        for rl in self.readers.values():
            for r in rl:
                lasts.add(r)
        for e in self.ENGS:
            if self.streams[e]:
                lasts.add(self.streams[e][-1])
        for e in self.ENGS:
            ins = _I()
            ins.eng, ins.fn, ins.dma, ins.sig, ins.sem, ins.val, ins.key = e, None, False, False, None, 0, None
            ins.deps = set(lasts)
            self.streams[e].append(ins)
        self.last_writer = {}
        self.readers = {}

    def finish(self, outs):
        self.add("sp", None, reads=list(outs))

    def _skip(self, ins, d):
        if d.dma:
            return False
        if d.eng == ins.eng:
            if d.eng == "pe":
                return True
            if not self.same:
                return True
        return False

    def emit(self):
        nc = self.nc
        for e in self.ENGS:
            for ins in self.streams[e]:
                for d in ins.deps:
                    if d.fn is not None and not self._skip(ins, d):
                        d.sig = True
        semnames = []
        dmacnt = {}
        for e in self.ENGS:
            cnt, idx = 0, 0
            for ins in self.streams[e]:
                if ins.dma:
                    k = ("dma", ins.key)
                    dmacnt[k] = dmacnt.get(k, 0) + 1
                    ins.sem, ins.val = k, 16 * dmacnt[k]
                    if k not in semnames:
                        semnames.append(k)
                elif ins.sig:
                    cnt += 1
                    if cnt > self.SEM_LIMIT:
                        idx += 1
                        cnt = 1
                    ins.sem, ins.val = (e, idx), cnt
                    if ins.sem not in semnames:
                        semnames.append(ins.sem)
        self.nsems = len(semnames)
        with ExitStack() as es:
            sems = {}
            for i, k in enumerate(semnames):
                sems[k] = es.enter_context(nc.semaphore("s%d" % i))
            block = es.enter_context(nc.Block())

            def replay(ename):
                def body(eng):
                    known = {}
                    for ins in self.streams[ename]:
                        need = {}
                        for d in ins.deps:
                            if d.fn is None or self._skip(ins, d):
                                continue
                            if need.get(d.sem, 0) < d.val:
                                need[d.sem] = d.val
                        todo = [(k, v) for k, v in need.items() if known.get(k, 0) < v]
                        for k, v in todo:
                            known[k] = v
                        attach = None
                        if ATTACH_WAIT and ins.fn is not None and todo:
                            attach = todo.pop()
                        for k, v in todo:
                            eng.wait_ge(sems[k], v)
                        if ins.fn is not None:
                            bi = ins.fn(eng)
                            if attach is not None:
                                bi._wait_ge(sems[attach[0]], attach[1])
                            if ins.dma:
                                bi.then_inc(sems[ins.sem], 16)
                            elif ins.sig:
                                bi.then_inc(sems[ins.sem], 1)
                return body

            block.tensor(replay("pe"))
            block.scalar(replay("act"))
            block.vector(replay("dve"))
            block.gpsimd(replay("pool"))
            block.sync(replay("sp"))


def _t5_bucket(dist):
    max_exact = 16
    d = np.maximum(dist, 0)
    large = max_exact + (np.log(np.maximum(d, 1).astype(np.float32) / np.float32(max_exact))
                         / np.float32(math.log(128 / max_exact)) * np.float32(32 - max_exact)).astype(np.int32)
    large = np.minimum(large, 31)
    return np.where(d < max_exact, d, large)


V_AN = 0
V_FN = 32
V_FIN = 64
V_SUB = 72
V_SINK = 74
V_BC = 90
V_LAM = 106
V_CONV = 106 + 512
NV = V_CONV + 4 * NJ * 8


def _prep_shared(inp):
    rel_bias = np.asarray(inp["rel_bias"], np.float32)
    vecs = np.zeros((128, NV), np.float32)
    pc = lambda v: np.asarray(v, np.float32).reshape(8, 128).T
    for l in range(DEPTH):
        vecs[:, V_AN + 8 * l:V_AN + 8 * l + 8] = pc(inp["attn_norm"][l])
        vecs[:, V_FN + 8 * l:V_FN + 8 * l + 8] = pc(inp["ffn_norm"][l])
    vecs[:, V_FIN:V_FIN + 8] = pc(inp["final_norm"])
    vecs[:, V_SUB:V_SUB + 2] = np.asarray(inp["da_subln"], np.float32).T
    vecs[:, V_SINK:V_SINK + 16] = np.asarray(inp["sw_sinks"], np.float32).reshape(1, 16)
    vecs[:, V_BC:V_BC + 16] = rel_bias[31][None, :]
    vecs[:, V_LAM:V_LAM + 512] = np.asarray(inp["da_lambda"], np.float32).reshape(1, 512)
    cw = np.asarray(inp["ffn_conv_w"], np.float32)
    cb = np.asarray(inp["ffn_conv_b"], np.float32)
    for l in range(DEPTH):
        for half in range(2):
            w = np.zeros((3, NJ * 128), np.float32)
            b = np.zeros((NJ * 128,), np.float32)
            w[:, :DFF] = cw[l][:, half * DFF:(half + 1) * DFF]
            b[:DFF] = cb[l][half * DFF:(half + 1) * DFF]
            w = w.reshape(3, NJ, 128)
            b = b.reshape(NJ, 128)
            for j in range(NJ):
                base = V_CONV + (l * NJ + j) * 8 + half * 4
                vecs[:, base + 0] = w[0, j]
                vecs[:, base + 1] = w[1, j]
                vecs[:, base + 2] = w[2, j]
                vecs[:, base + 3] = b[j]
    kk = np.arange(128)[:, None]
    qq = np.arange(128)[None, :]
    d0 = qq - kk
    d1 = qq - kk + 128
    b0 = rel_bias[_t5_bucket(d0)]
    b1 = rel_bias[_t5_bucket(d1)]
    bm = np.zeros((128, 3, 16, 128), np.float32)
    neg = np.float32(NEGM)
    bm[:, 0] = np.where((d0 >= 0)[:, :, None], b0, neg).transpose(0, 2, 1)
    bm[:, 1] = b1.transpose(0, 2, 1)
    bm[:, 2] = np.where((d1 < 128)[:, :, None], b1, neg).transpose(0, 2, 1)
    masks = np.zeros((128, 2, 128), np.float32)
    masks[:, 0] = (kk < qq).astype(np.float32)
    masks[:, 1] = np.where(kk < qq, np.float32(0), neg)
    tri = (kk >= qq).astype(np.float32)
    consts = np.zeros((128, 2, 128), np.float32)
    consts[:, 0] = 1.0
    consts[:, 1] = tri
    return dict(vecs=vecs, bm=bm, masks=masks, consts=consts)


def _prep_layer(inp, l):
    mixer, slot = l % 3, l // 3
    out = {}
    if mixer in (0, 1):
        w = np.asarray(inp["da_w_qkv" if mixer == 0 else "sb_w_qkv"][slot], np.float32)
        q = w[:, 0:1024].reshape(D, 8, 128)
        k = w[:, 1024:2048].reshape(D, 8, 128)
        v = w[:, 2048:3072].reshape(D, 8, 128)
        wh = np.concatenate([q, k, v], axis=2)
        wh = wh.reshape(8, 128, 8, 384).transpose(2, 1, 0, 3)
        out["wqkv%d" % l] = np.ascontiguousarray(wh)
    else:
        w = np.asarray(inp["sw_w_qkv"][slot], np.float32)
        q = w[:, 0:1024]
        k = w[:, 1024:1280].reshape(D, 4, 64)
        v = w[:, 1280:1536].reshape(D, 4, 64)
        kd = np.concatenate([k, k], axis=2)
        vd = np.concatenate([v, v], axis=2)
        gi = np.arange(8) // 2
        wh = np.concatenate([q.reshape(D, 8, 128), kd[:, gi, :], vd[:, gi, :]], axis=2)
        wh = wh.reshape(8, 128, 8, 384).transpose(2, 1, 0, 3)
        out["wqkv%d" % l] = np.ascontiguousarray(wh)
    wo = np.asarray(inp["w_o"][l], np.float32)
    wo = wo.reshape(8, 128, 8, 128).transpose(2, 1, 0, 3)
    out["wo%d" % l] = np.ascontiguousarray(wo)
    wu = np.asarray(inp["ffn_w_up"][l], np.float32)
    g = np.zeros((D, NJ * 128), np.float32)
    v = np.zeros((D, NJ * 128), np.float32)
    g[:, :DFF] = wu[:, :DFF]
    v[:, :DFF] = wu[:, DFF:]
    gv = np.concatenate([g.reshape(D, NJ, 128), v.reshape(D, NJ, 128)], axis=2)
    gv = gv.reshape(8, 128, NJ, 256).transpose(2, 1, 0, 3)
    out["wup%d" % l] = np.ascontiguousarray(gv)
    wd = np.zeros((NJ * 128, D), np.float32)
    wd[:DFF] = np.asarray(inp["ffn_w_down"][l], np.float32)
    out["wdn%d" % l] = np.ascontiguousarray(wd.reshape(NJ, 128, D))
    return out


def _lambda_init(layer):
    return 0.8 - 0.6 * math.exp(-0.3 * layer)


class Pool_:
    def __init__(self, ap_u16, nbytes):
        self.ap = ap_u16
        self.n = nbytes
        self.off = 0

    def alloc(self, free_shape, dt):
        esz = 4 if dt == F32 else 2
        nel = int(np.prod(free_shape))
        nb = nel * esz
        off = (self.off + 63) // 64 * 64
        assert off + nb <= self.n, ("SBUF pool overflow", off, nb, self.n)
        self.off = off + nb
        a = self.ap[:, off // 2:(off + nb) // 2]
        if dt == F32:
            a = a.bitcast(F32)
        if len(free_shape) == 2:
            a = a.rearrange("p (a b) -> p a b", a=free_shape[0])
        elif len(free_shape) == 3:
            a = a.rearrange("p (a b c) -> p a b c", a=free_shape[0], b=free_shape[1])
        return a


def build_nc(layers, do_final):
    nc = bass.Bass("TRN2", target_bir_lowering=False)
    nc.allow_low_precision("bf16 matmul operands with fp32 PSUM accumulation by design")
    dram = lambda n, s, k="ExternalInput": nc.dram_tensor(n, list(s), F32, kind=k).ap()
    xin = dram("xin", [D, S])
    yout = dram("yout", [D, S], "ExternalOutput")
    vecs_d = dram("vecs", [128, NV])
    bm_d = dram("bm", [128, 3, 16, 128])
    masks_d = dram("masks", [128, 2, 128])
    consts_d = dram("consts", [128, 2, 128])
    wd = {}
    for l in layers:
        mixer = l % 3
        wd["wqkv%d" % l] = dram("wqkv%d" % l, [8, 128, 8, 384])
        wd["wo%d" % l] = dram("wo%d" % l, [8, 128, 8, 128])
        wd["wup%d" % l] = dram("wup%d" % l, [NJ, 128, 8, 256])
        wd["wdn%d" % l] = dram("wdn%d" % l, [NJ, 128, D])

    with ExitStack() as es:
        POOLB = 212480
        pool_t = es.enter_context(nc.sbuf_tensor("pool", [128, POOLB // 2], BF16))
        A = Pool_(pool_t[:, :], POOLB)
        ps = [es.enter_context(nc.psum_tensor("ps%d" % i, [128, 512], F32))[:, :] for i in range(8)]
        P = Prog(nc)

        xT = A.alloc([8, S], F32)
        hT = A.alloc([8, S], BF16)
        vecs = A.alloc([NV], F32)
        ones_b = A.alloc([128], BF16)
        tri_b = A.alloc([128], BF16)
        nones_b = A.alloc([128], BF16)
        ntri_b = A.alloc([128], BF16)
        ones_dmb = A.alloc([128], BF16)
        ones_eb = A.alloc([128], BF16)
        ones_dm = A.alloc([128], F32)
        ones_e = A.alloc([128], F32)
        cst_f = A.alloc([2, 128], F32)
        small = A.alloc([64], F32)
        persist_end = A.off

        eps_ap = small[:, 4:5]
        NEGLAM = 0
        SUBS = 2
        ESINK = 8

        for c in range(8):
            P.dma("sp", xT[:, c, :], xin[c * 128:(c + 1) * 128, :], writes=["x%d_%d" % (c, t) for t in range(4)])
        P.dma("sp", vecs, vecs_d, writes=["vecs"])
        P.dma("sp", cst_f, consts_d, writes=["cstf"])
        P.add("dve", lambda e: e.memset(small[:, 4:5], EPS), writes=["small"])
        P.add("dve", lambda e: e.tensor_copy(ones_b, cst_f[:, 0, :]), reads=["cstf"], writes=["ones_b"])
        P.add("dve", lambda e: e.tensor_copy(tri_b, cst_f[:, 1, :]), reads=["cstf"], writes=["tri_b"])
        P.add("dve", lambda e: e.tensor_scalar(nones_b, cst_f[:, 0, :], -1.0, None, ALU.mult), reads=["cstf"], writes=["ones_b"])
        P.add("dve", lambda e: e.tensor_scalar(ntri_b, cst_f[:, 1, :], -1.0, None, ALU.mult), reads=["cstf"], writes=["tri_b"])
        P.add("dve", lambda e: e.tensor_scalar(ones_dm, cst_f[:, 0, :], 1.0 / D, None, ALU.mult), reads=["cstf"], writes=["ones_dm"])
        P.add("dve", lambda e: e.tensor_scalar(ones_dmb, cst_f[:, 0, :], 1.0 / D, None, ALU.mult), reads=["cstf"], writes=["ones_dm"])
        P.add("dve", lambda e: e.tensor_scalar(ones_eb, cst_f[:, 0, :], 1.0 / 128, None, ALU.mult), reads=["cstf"], writes=["ones_e"])
        P.add("dve", lambda e: e.tensor_scalar(ones_e, cst_f[:, 0, :], 1.0 / 128, None, ALU.mult), reads=["cstf"], writes=["ones_e"])

        def rmsnorm_to_h(gcol, tag):
            sq = [A.alloc([512], BF16) for _ in range(4)]
            rstd = [A.alloc([512], F32) for _ in range(2)]
            k = 0
            for tg in range(4):
                sl = slice(tg * 512, (tg + 1) * 512)
                bank = 6 + (tg % 2)
                for c in range(8):
                    b = k % 4
                    k += 1
                    eng = "act"
                    if eng == "act":
                        P.add("act", lambda e, o=sq[b], i=xT[:, c, sl]: e.activation(o, i, AF.Square),
                              reads=["x%d_%d" % (c, tg)], writes=["sq%d" % b])
                    else:
                        P.add("pool", lambda e, o=sq[b], i=xT[:, c, sl]: e.tensor_tensor(o, i, i, ALU.mult),
                              reads=["x%d_%d" % (c, tg)], writes=["sq%d" % b])
                    P.add("pe", lambda e, o=ps[bank], r=sq[b], st=(c == 0), sp_=(c == 7): e.matmul(o, ones_dmb, r, start=st, stop=sp_),
                          reads=["sq%d" % b, "ones_dm"], writes=["ps%d" % bank])
                rb = rstd[tg % 2]
                P.add("act", lambda e, o=rb, i=ps[bank]: e.activation(o, i, AF.Ln, bias=eps_ap, scale=1.0),
                      reads=["ps%d" % bank, "small"], writes=["rstd%d" % (tg % 2)])
                P.add("act", lambda e, o=rb: e.activation(o, o, AF.Exp, scale=-0.5),
                      reads=["rstd%d" % (tg % 2)], writes=["rstd%d" % (tg % 2)])
                for c in range(8):
                    P.add("dve", lambda e, o=hT[:, c, sl], i=xT[:, c, sl], g=vecs[:, gcol + c:gcol + c + 1], r=rb:
                          e.scalar_tensor_tensor(o, i, g, r, ALU.mult, ALU.mult),
                          reads=["x%d_%d" % (c, tg), "vecs", "rstd%d" % (tg % 2)], writes=["h%d_%d" % (c, tg)])

        H_ALL = ["h%d_%d" % (c, t) for c in range(8) for t in range(4)]

        def load_w(q, dst, src, name):
            P.dma(q, dst, src, writes=[name])

        def proj_qkv_head(wt, wname, qT, kT, vS, qscale, tag, banks, vtag="v", kpad=None, qpad=None, do_kv=True, evac="act"):
            bi = 0
            for which, dst in ((0, qT), (1, kT)):
                if which == 1 and not do_kv:
                    continue
                for tg in range(4):
                    sl = slice(tg * 512, (tg + 1) * 512)
                    bank = banks[bi % len(banks)]
                    bi += 1
                    for kc in range(8):
                        P.add("pe", lambda e, o=ps[bank], w=wt[:, kc, which * 128:(which + 1) * 128], r=hT[:, kc, sl], st=(kc == 0), sp_=(kc == 7):
                              e.matmul(o, w, r, start=st, stop=sp_),
                              reads=[wname, "h%d_%d" % (kc, tg)], writes=["ps%d" % bank])
                    rn = ("q" if which == 0 else "k") + tag + "_%d" % tg
                    if which == 0 and qpad is not None:
                        P.add("act", lambda e, o=qpad[0][0:64, sl], i=ps[bank][0:64, :]: e.activation(o, i, AF.Identity, scale=qscale),
                              reads=["ps%d" % bank], writes=[rn + "a"])
                        P.add("act", lambda e, o=qpad[1][64:128, sl], i=ps[bank][64:128, :]: e.activation(o, i, AF.Identity, scale=qscale),
                              reads=["ps%d" % bank], writes=[rn + "b"])
                    elif which == 0 and qscale != 1.0:
                        if evac == "dve":
                            P.add("dve", lambda e, o=dst[:, sl], i=ps[bank]: e.tensor_scalar(o, i, qscale, None, ALU.mult),
                                  reads=["ps%d" % bank], writes=[rn])
                        else:
                            P.add("act", lambda e, o=dst[:, sl], i=ps[bank]: e.activation(o, i, AF.Identity, scale=qscale),
                                  reads=["ps%d" % bank], writes=[rn])
                    elif which == 1 and kpad is not None:
                        P.add("dve", lambda e, o=kpad[0][0:64, sl], i=ps[bank][0:64, :]: e.tensor_copy(o, i),
                              reads=["ps%d" % bank], writes=[rn + "a"])
                        if evac == "dve":
                            P.add("dve", lambda e, o=kpad[1][64:128, sl], i=ps[bank][64:128, :]: e.tensor_copy(o, i),
                                  reads=["ps%d" % bank], writes=[rn + "b"])
                        else:
                            P.add("act", lambda e, o=kpad[1][64:128, sl], i=ps[bank][64:128, :]: e.activation(o, i, AF.Identity),
                                  reads=["ps%d" % bank], writes=[rn + "b"])
                    else:
                        P.add("dve", lambda e, o=dst[:, sl], i=ps[bank]: e.tensor_copy(o, i),
                              reads=["ps%d" % bank], writes=[rn])
            for t4 in range(4 if do_kv else 0):
                bank = banks[bi % len(banks)]
                bi += 1
                for tt in range(4):
                    tok = slice((t4 * 4 + tt) * 128, (t4 * 4 + tt + 1) * 128)
                    for kc in range(8):
                        P.add("pe", lambda e, o=ps[bank][:, tt * 128:(tt + 1) * 128], l=hT[:, kc, tok], r=wt[:, kc, 256:384], st=(kc == 0), sp_=(kc == 7):
                              e.matmul(o, l, r, start=st, stop=sp_),
                              reads=[wname, "h%d_%d" % (kc, t4)], writes=["ps%d" % bank])
                if evac == "dve":
                    P.add("dve", lambda e, o=vS[:, t4 * 4:(t4 + 1) * 4, :], i=ps[bank].rearrange("p (a b) -> p a b", a=4): e.tensor_copy(o, i),
                          reads=["ps%d" % bank], writes=[vtag + "_%d" % t4])
                else:
                    P.add("act", lambda e, o=vS[:, t4 * 4:(t4 + 1) * 4, :], i=ps[bank].rearrange("p (a b) -> p a b", a=4): e.activation(o, i, AF.Identity),
                          reads=["ps%d" % bank], writes=[vtag + "_%d" % t4])

        def oproj(l, parts):
            wo = [A.alloc([8, 128], BF16) for _ in range(2)]
            for dc in range(8):
                wb = dc % 2
                load_w("pool", wo[wb], wd["wo%d" % l][dc], "wo%d" % wb)
                for tg in range(4):
                    sl = slice(tg * 512, (tg + 1) * 512)
                    bank = 6 + ((dc * 4 + tg) % 2)
                    for kc in range(8):
                        P.add("pe", lambda e, o=ps[bank], w=wo[wb][:, kc, :], r=oT[:, kc, sl], st=(kc == 0), sp_=(kc == 7):
                              e.matmul(o, w, r, start=st, stop=sp_),
                              reads=["wo%d" % wb] + (["o%d_%d_0" % (kc, tg), "o%d_%d_1" % (kc, tg)] if parts else ["o%d_%d" % (kc, tg)]), writes=["ps%d" % bank])
                    P.add("dve", lambda e, o=xT[:, dc, sl], i=ps[bank]: e.tensor_tensor(o, i, o, ALU.add),
                          reads=["ps%d" % bank, "x%d_%d" % (dc, tg)], writes=["x%d_%d" % (dc, tg)])

        def run_pipeline(N, stages):
            maxlag = max(lg for lg, _ in stages)
            for it in range(N + maxlag):
                for lg, fn in stages:
                    n = it - lg
                    if 0 <= n < N:
                        fn(n)

        def attn_da(l):
            slot = l // 3
            li = _lambda_init(l)
            nonlocal oT
            oT = A.alloc([8, S], BF16)
            bmh = [A.alloc([2, 2, 128], F32) for _ in range(2)]
            wts = [A.alloc([8, 384], BF16) for _ in range(2)]
            qT = A.alloc([S], BF16)
            kT = None
            kz = [A.alloc([S], BF16) for _ in range(2)]
            P.add("pool", lambda e: e.memset(kz[0][64:128, :], 0.0), writes=["kz0pad"])
            P.add("pool", lambda e: e.memset(kz[1][0:64, :], 0.0), writes=["kz1pad"])
            vSb = [A.alloc([16, 128], BF16) for _ in range(2)]
            NS, NP = 3, 3
            Pb = [A.alloc([512], BF16) for _ in range(NP)]
            tD = [A.alloc([256], F32) for _ in range(2)]
            tmp = [A.alloc([512], F32) for _ in range(5)]
            sqb = tmp[3].bitcast(BF16)[:, 0:512]
            lam = vecs[:, V_LAM + slot * 256:V_LAM + slot * 256 + 256]
            P.add("dve", lambda e: e.tensor_tensor(tmp[0][:, 0:64], lam[:, 0:64], lam[:, 64:128], ALU.mult), reads=["vecs"], writes=["tmp0"])
            P.add("dve", lambda e: e.tensor_tensor(tmp[0][:, 64:128], lam[:, 128:192], lam[:, 192:256], ALU.mult), reads=["vecs", "tmp0"], writes=["tmp0"])
            P.add("dve", lambda e: e.reduce_sum(tmp[1][:, 0:2], tmp[0][:, 0:128].rearrange("p (a b) -> p a b", a=2), mybir.AxisListType.X),
                  reads=["tmp0"], writes=["tmp1"])
            P.add("act", lambda e: e.activation(tmp[1][:, 2:4], tmp[1][:, 0:2], AF.Exp), reads=["tmp1"], writes=["tmp1"])
            P.add("dve", lambda e: e.scalar_tensor_tensor(small[:, NEGLAM + slot:NEGLAM + slot + 1], tmp[1][:, 3:4], -li, tmp[1][:, 2:3], ALU.add, ALU.subtract),
                  reads=["tmp1"], writes=["small"])
            P.add("dve", lambda e: e.tensor_scalar(small[:, SUBS + slot:SUBS + slot + 1], vecs[:, V_SUB + slot:V_SUB + slot + 1], 1.0 - li, None, ALU.mult),
                  reads=["vecs", "small"], writes=["small"])
            neglam = small[:, NEGLAM + slot:NEGLAM + slot + 1]
            subs = small[:, SUBS + slot:SUBS + slot + 1]

            def prefetch(h):
                load_w("pool", wts[h % 2], wd["wqkv%d" % l][h], "wt%d" % (h % 2))

            def prefetch_bm(h):
                P.dma("sp", bmh[h % 2], bm_d[:, 0:2, 2 * h:2 * h + 2, :], writes=["bmh%d" % (h % 2)])
                for j_ in range(2):
                    ch_ = 2 * h + j_
                    P.add("pool", lambda e, o=bmh[h % 2][:, :, j_, :], c_=vecs[:, V_BC + ch_:V_BC + ch_ + 1]: e.tensor_scalar(o, o, c_, None, ALU.subtract),
                          reads=["vecs", "bmh%d" % (h % 2)], writes=["bmh%d" % (h % 2)])

            tiles = []
            gi = 0
            for h in range(8):
                for g in range(4):
                    for j in range(2):
                        for kt in range(4 * g + 4):
                            tiles.append(dict(h=h, g=g, j=j, kt=kt, gi=gi, first=(kt == 0), last=(kt == 4 * g + 3),
                                              head_first=(g == 0 and j == 0 and kt == 0)))
                        gi += 1
            N = len(tiles)
            deferred = []

            def sS(n):
                t = tiles[n]
                h, g, j, kt = t["h"], t["g"], t["j"], t["kt"]
                if t["head_first"]:
                    if h == 0:
                        prefetch(0)
                        prefetch_bm(0)
                    if h + 1 < 8:
                        prefetch(h + 1)
                    proj_qkv_head(wts[h % 2], "wt%d" % (h % 2), qT, kT, vSb[h % 2], 0.125, "", banks=(7, n % NS), vtag="v%d" % (h % 2), kpad=kz, evac="act")
                if g == 1 and j == 0 and kt == 0 and h + 1 < 8:
                    prefetch_bm(h + 1)
                i = kt - 4 * g
                q0 = max(i, 0) * 128
                sb = n % NS
                rows = slice(j * 64, (j + 1) * 64)
                P.add("pe", lambda e, o=ps[sb][:, q0:512], l_=kz[j][:, kt * 128:(kt + 1) * 128], r=qT[:, g * 512 + q0:(g + 1) * 512]:
                      e.matmul(o, l_, r, start=True, stop=True),
                      reads=["k_%da" % (kt // 4), "k_%db" % (kt // 4), "kz0pad", "kz1pad", "q_%d" % g], writes=["ps%d" % sb])

            def sB(n):
                t = tiles[n]
                h, g, j, kt = t["h"], t["g"], t["j"], t["kt"]
                i = kt - 4 * g
                if i < -1:
                    return
                q0 = max(i, 0) * 128
                sb = n % NS
                bm = bmh[h % 2]
                bn = "bmh%d" % (h % 2)
                if i >= 0 and i < 3:
                    P.add("dve", lambda e, o=ps[sb][:, q0:q0 + 256].rearrange("p (a b) -> p a b", a=2), b=bm[:, 0:2, j, :]: e.tensor_tensor(o, o, b, ALU.add),
                          reads=["ps%d" % sb, bn], writes=["ps%d" % sb])
                elif i == 3:
                    P.add("dve", lambda e, o=ps[sb][:, q0:q0 + 128], b=bm[:, 0, j, :]: e.tensor_tensor(o, o, b, ALU.add),
                          reads=["ps%d" % sb, bn], writes=["ps%d" % sb])
                else:
                    P.add("dve", lambda e, o=ps[sb][:, 0:128], b=bm[:, 1, j, :]: e.tensor_tensor(o, o, b, ALU.add),
                          reads=["ps%d" % sb, bn], writes=["ps%d" % sb])

            def sP(n):
                t = tiles[n]
                h, g, j, kt = t["h"], t["g"], t["j"], t["kt"]
                i = kt - 4 * g
                q0 = max(i, 0) * 128
                sb = n % NS
                Pt = Pb[n % NP]
                pn = "P%d" % (n % NP)
                P.add("act", lambda e, o=Pt[:, q0:512], i_=ps[sb][:, q0:512]: e.activation(o, i_, AF.Exp),
                      reads=["ps%d" % sb], writes=[pn + "a"])

            def sV(n):
                t = tiles[n]
                h, g, j, kt, gi_ = t["h"], t["g"], t["j"], t["kt"], t["gi"]
                i = kt - 4 * g
                q0 = max(i, 0) * 128
                Pt = Pb[n % NP]
                pn = "P%d" % (n % NP)
                ob, db = 3 + gi_ % 2, 5 + gi_ % 2
                first, last = t["first"], t["last"]
                vS = vSb[h % 2]
                P.add("pe", lambda e, o=ps[ob][:, q0:512], l_=vS[:, kt, :], r=Pt[:, q0:512]: e.matmul(o, l_, r, start=first, stop=last),
                      reads=[pn + "a", pn + "b", "v%d_%d" % (h % 2, kt // 4)], writes=["ps%d" % ob])
                P.add("pe", lambda e, o=ps[db][:, q0:512], r=Pt[:, q0:512]: e.matmul(o, ones_b, r, start=first, stop=last),
                      reads=[pn + "a", pn + "b", "ones_b"], writes=["ps%d" % db])
                if not last:
                    return
                r0, a0, a1, sq, rs = tmp
                aj = a0 if j == 0 else a1
                an = "tmp1" if j == 0 else "tmp2"
                Q = deferred.append
                Q(lambda d_=ps[db], dn="ps%d" % db: P.add("act", lambda e: e.activation(r0, d_, AF.Ln), reads=[dn], writes=["tmp0"]))
                Q(lambda: P.add("act", lambda e: e.activation(r0, r0, AF.Exp, scale=-1.0), reads=["tmp0"], writes=["tmp0"]))
                Q(lambda o_=ps[ob], a_=aj, on_="ps%d" % ob, an_=an: P.add("dve", lambda e: e.tensor_tensor(a_, o_, r0, ALU.mult), reads=[on_, "tmp0"], writes=[an_]))
                if j == 1:
                    sl = slice(g * 512, (g + 1) * 512)
                    Q(lambda: P.add("dve", lambda e: e.scalar_tensor_tensor(a0, a1, neglam, a0, ALU.mult, ALU.add), reads=["tmp1", "tmp2", "small"], writes=["tmp1"]))
                    Q(lambda: P.add("act", lambda e: e.activation(sqb, a0, AF.Square), reads=["tmp1"], writes=["tmp3"]))
                    Q(lambda: P.add("pe", lambda e: e.matmul(ps[7], ones_eb, sqb, start=True, stop=True), reads=["tmp3", "ones_e"], writes=["ps7"]))
                    Q(lambda: P.add("act", lambda e: e.activation(rs, ps[7], AF.Ln, bias=eps_ap, scale=1.0), reads=["ps7", "small"], writes=["tmp4"]))
                    Q(lambda: P.add("act", lambda e: e.activation(rs, rs, AF.Exp, scale=-0.5), reads=["tmp4"], writes=["tmp4"]))
                    Q(lambda o=oT[:, h, sl], on_="o%d_%d" % (h, g): P.add("dve", lambda e: e.scalar_tensor_tensor(o, a0, subs, rs, ALU.mult, ALU.mult),
                                                                  reads=["tmp1", "tmp4", "small"], writes=[on_]))

            def sD(n):
                k_ = 2 if len(deferred) > 6 else 1
                for _ in range(k_):
                    if deferred:
                        deferred.pop(0)()

            run_pipeline(N, [(0, sS), (1, sB), (2, sP), (3, sV), (3, sD)])
            while deferred:
                deferred.pop(0)()

        def attn_sb(l):
            nonlocal oT
            oT = A.alloc([8, S], BF16)
            mk = A.alloc([2, 128], F32)
            P.dma("sp", mk, masks_d, writes=["mk"])
            wts = [A.alloc([8, 384], BF16) for _ in range(2)]
            qT = A.alloc([S], BF16)
            kT = None
            kz = [A.alloc([S], BF16) for _ in range(2)]
            P.add("pool", lambda e: e.memset(kz[0][64:128, :], 0.0), writes=["kz0pad"])
            P.add("pool", lambda e: e.memset(kz[1][0:64, :], 0.0), writes=["kz1pad"])
            vSb = [A.alloc([16, 128], BF16) for _ in range(2)]
            NZ, NW, NL, NA = 5, 4, 2, 2
            HSEL = ["dve"]
            LSEL = ["pool"]
            Wb = [A.alloc([512], F32) for _ in range(NW)]
            NE = 2 if all(h_ == "dve" for h_ in HSEL) else NW
            Eb = [A.alloc([512], F32) for _ in range(NE)]
            Lh = [A.alloc([512], BF16) for _ in range(NL)]
            Ll = [A.alloc([512], BF16) for _ in range(NL)]
            Ab = [A.alloc([512], BF16) for _ in range(NA)]
            Cb = [A.alloc([512], F32) for _ in range(2)]

            tiles = []
            for c in range(8):
                for r in range(2):
                    for g in range(4):
                        for kt in range(4 * g + 3, -1, -1):
                            tiles.append(dict(c=c, r=r, g=g, kt=kt, first=(kt == 4 * g + 3), last=(kt == 0),
                                              pair_first=(r == 0 and g == 0 and kt == 3)))
            N = len(tiles)
            for n in range(N):
                t = tiles[n]
                i = t["kt"] - 4 * t["g"]
                t["i"] = i
                t["q0"] = max(i, 0) * 128
                t["pq0"] = tiles[n - 1]["q0"] if not t["first"] else 512

            def prefetch(c):
                load_w("pool", wts[c % 2], wd["wqkv%d" % l][c], "wt%d" % (c % 2))

            def sZ(n):
                t = tiles[n]
                c, r, g, kt, q0 = t["c"], t["r"], t["g"], t["kt"], t["q0"]
                if t["pair_first"]:
                    if c == 0:
                        prefetch(0)
                    if c + 1 < 8:
                        prefetch(c + 1)
                    proj_qkv_head(wts[c % 2], "wt%d" % (c % 2), qT, kT, vSb[c % 2], 0.125, "", banks=(5, 6), vtag="v%d" % (c % 2), kpad=kz)
                rows = slice(r * 64, (r + 1) * 64)
                zb = n % NZ
                P.add("pe", lambda e, o=ps[zb][:, q0:512], l_=kz[r][:, kt * 128:(kt + 1) * 128], r_=qT[:, g * 512 + q0:(g + 1) * 512]:
                      e.matmul(o, l_, r_, start=True, stop=False, skip_group_check=True),
                      reads=["k_%da" % (kt // 4), "k_%db" % (kt // 4), "kz0pad", "kz1pad", "q_%d" % g], writes=["ps%d" % zb])

            def sE(n):
                t = tiles[n]
                q0 = t["q0"]
                zb = n % NZ
                W = Wb[n % NW]
                wn = "W%d" % (n % NW)
                E = Eb[n % NE]
                en = "E%d" % (n % NE)
                P.add("act", lambda e, o=E[:, q0:512], i_=ps[zb][:, q0:512]: e.activation(o, i_, AF.Exp), reads=["ps%d" % zb], writes=[en])
                P.add("act", lambda e, o=W[:, q0:512], i_=E[:, q0:512]: e.activation(o, i_, AF.Ln, bias=1.0, scale=1.0), reads=[en], writes=[wn])

            def sM(n):
                t = tiles[n]
                q0 = t["q0"]
                W = Wb[n % NW]
                wn = "W%d" % (n % NW)
                if t["i"] >= 0:
                    P.add("dve", lambda e, o=W[:, q0:q0 + 128], m=mk[:, 0, :]: e.tensor_tensor(o, o, m, ALU.mult), reads=[wn, "mk"], writes=[wn])

            def sL(n):
                t = tiles[n]
                q0 = t["q0"]
                W = Wb[n % NW]
                wn = "W%d" % (n % NW)
                LH, LL = Lh[n % NL], Ll[n % NL]
                hsel = HSEL[n % len(HSEL)]
                if hsel == "act":
                    P.add("act", lambda e, o=LH[:, q0:512], i_=Eb[n % NE][:, q0:512]: e.activation(o, i_, AF.Ln, bias=1.0, scale=1.0),
                          reads=["E%d" % (n % NE)], writes=["LH%d" % (n % NL)])
                else:
                    P.add("dve", lambda e, o=LH[:, q0:512], i_=W[:, q0:512]: e.tensor_copy(o, i_), reads=[wn], writes=["LH%d" % (n % NL)])
                P.add(LSEL[n % len(LSEL)], lambda e, o=LL[:, q0:512], i_=W[:, q0:512], hh=LH[:, q0:512]: e.tensor_tensor(o, i_, hh, ALU.subtract),
                      reads=[wn, "LH%d" % (n % NL)], writes=["LL%d" % (n % NL)])

            def sT(n):
                t = tiles[n]
                q0 = t["q0"]
                zb = n % NZ
                cb = 5 + n % 2
                LH, LL = Lh[n % NL], Ll[n % NL]
                hn, ln_ = "LH%d" % (n % NL), "LL%d" % (n % NL)
                if not t["last"]:
                    P.add("pe", lambda e, o=ps[cb][:, q0:512], r_=LH[:, q0:512]: e.matmul(o, nones_b, r_, start=True, stop=False),
                          reads=[hn, "ones_b"], writes=["ps%d" % cb])
                P.add("pe", lambda e, o=ps[zb][:, q0:512], r_=LH[:, q0:512]: e.matmul(o, ntri_b, r_, start=False, stop=False, skip_group_check=True),
                      reads=[hn, "tri_b"], writes=["ps%d" % zb])
                if not t["last"]:
                    P.add("pe", lambda e, o=ps[cb][:, q0:512], r_=LL[:, q0:512]: e.matmul(o, nones_b, r_, start=False, stop=True),
                          reads=[ln_, "ones_b"], writes=["ps%d" % cb])
                P.add("pe", lambda e, o=ps[zb][:, q0:512], r_=LL[:, q0:512]: e.matmul(o, ntri_b, r_, start=False, stop=True, skip_group_check=True),
                      reads=[ln_, "tri_b"], writes=["ps%d" % zb])

            def sC(n):
                t = tiles[n]
                if t["last"]:
                    return
                q0, pq0 = t["q0"], t["pq0"]
                cb = 5 + n % 2
                C, Cp = Cb[n % 2], Cb[(n - 1) % 2]
                cn, cpn = "C%d" % (n % 2), "C%d" % ((n - 1) % 2)
                if pq0 > q0:
                    P.add("dve", lambda e, o=C[:, q0:pq0], i_=ps[cb][:, q0:pq0]: e.tensor_copy(o, i_), reads=["ps%d" % cb], writes=[cn + "a"])
                if pq0 < 512:
                    P.add("dve", lambda e, o=C[:, pq0:512], i_=ps[cb][:, pq0:512], p_=Cp[:, pq0:512]: e.tensor_tensor(o, i_, p_, ALU.add),
                          reads=["ps%d" % cb, cpn + "a", cpn + "b"], writes=[cn + "b"])

            def sX(n):
                t = tiles[n]
                q0, pq0 = t["q0"], t["pq0"]
                zb = n % NZ
                W = Wb[n % NW]
                wn = "W%d" % (n % NW)
                Cp = Cb[(n - 1) % 2]
                cpn = "C%d" % ((n - 1) % 2)
                c0 = q0
                if t["i"] >= 0:
                    P.add("dve", lambda e, o=W[:, q0:q0 + 128], i_=ps[zb][:, q0:q0 + 128], m=mk[:, 1, :]: e.tensor_tensor(o, i_, m, ALU.add),
                          reads=["ps%d" % zb, "mk", wn], writes=[wn])
                    c0 = q0 + 128
                if c0 < 512:
                    P.add("dve", lambda e, o=W[:, c0:512], i_=ps[zb][:, c0:512], c_=Cp[:, c0:512]: e.tensor_tensor(o, i_, c_, ALU.add),
                          reads=["ps%d" % zb, cpn + "a", cpn + "b", wn], writes=[wn])

            def sA(n):
                t = tiles[n]
                q0 = t["q0"]
                W = Wb[n % NW]
                wn = "W%d" % (n % NW)
                P.add("act", lambda e, o=Ab[n % NA][:, q0:512], i_=W[:, q0:512]: e.activation(o, i_, AF.Exp), reads=[wn], writes=["A%d" % (n % NA)])

            def sV(n):
                t = tiles[n]
                c, r, g, kt, q0 = t["c"], t["r"], t["g"], t["kt"], t["q0"]
                rows = slice(r * 64, (r + 1) * 64)
                vS = vSb[c % 2]
                P.add("pe", lambda e, o=ps[7][:, q0:512], l_=vS[:, kt, :], r_=Ab[n % NA][:, q0:512], st=t["first"], sp_=t["last"]:
                      e.matmul(o, l_, r_, start=st, stop=sp_, skip_group_check=True),
                      reads=["A%d" % (n % NA), "v%d_%d" % (c % 2, kt // 4)], writes=["ps7"])
                if t["last"]:
                    sl = slice(g * 512, (g + 1) * 512)
                    P.add("act", lambda e, o=oT[rows, c, sl], i_=ps[7][rows, :]: e.activation(o, i_, AF.Identity),
                          reads=["ps7"], writes=["o%d_%d_%d" % (c, g, r)])

            run_pipeline(N, list(zip([0, 1, 2, 2, 3, 3, 3, 4, 5], [sZ, sE, sM, sL, sT, sC, sX, sA, sV])))

        def attn_swa(l):
            nonlocal oT
            oT = A.alloc([8, S], BF16)
            bmh = [A.alloc([2, 2, 128], F32) for _ in range(2)]
            wts = [A.alloc([8, 384], BF16) for _ in range(2)]
            qz = [A.alloc([S], BF16) for _ in range(2)]
            P.add("pool", lambda e: e.memset(qz[0][64:128, :], 0.0), writes=["qz0pad"])
            P.add("pool", lambda e: e.memset(qz[1][0:64, :], 0.0), writes=["qz1pad"])
            kT = A.alloc([S], BF16)
            vSb = [A.alloc([16, 128], BF16) for _ in range(2)]
            Tt = [[A.alloc([512], F32) for _ in range(2)] for _ in range(2)]
            Pt = [[A.alloc([512], BF16) for _ in range(2)] for _ in range(3)]
            rr = [A.alloc([512], F32) for _ in range(2)]
            P.add("act", lambda e: e.activation(small[:, ESINK:ESINK + 16], vecs[:, V_SINK:V_SINK + 16], AF.Exp), reads=["vecs", "small"], writes=["small"])

            def prefetch(c):
                load_w("pool", wts[c % 2], wd["wqkv%d" % l][c], "wt%d" % (c % 2))

            def prefetch_bm(c):
                P.dma("sp", bmh[c % 2][:, 0], bm_d[:, 0, 2 * c:2 * c + 2, :], writes=["bmh%da" % (c % 2)])
                P.dma("sp", bmh[c % 2][:, 1], bm_d[:, 2, 2 * c:2 * c + 2, :], writes=["bmh%db" % (c % 2)])

            its = [(c, r, qq) for c in range(8) for r in range(2) for qq in range(4)]
            N = len(its)

            def sS(n):
                c, r, qq = its[n]
                if r == 0 and qq == 0:
                    if c == 0:
                        prefetch(0)
                        prefetch_bm(0)
                    if c + 1 < 8:
                        prefetch(c + 1)
                    proj_qkv_head(wts[c % 2], "wt%d" % (c % 2), None, kT, vSb[(c // 2) % 2], 0.125, "", banks=(0, 1),
                                  vtag="v%d" % ((c // 2) % 2), qpad=qz, do_kv=(c % 2 == 0))
                if r == 1 and qq == 0 and c + 1 < 8:
                    prefetch_bm(c + 1)
                b0, b1 = 2 * (n % 2), 2 * (n % 2) + 1
                for t4 in range(4):
                    qt = qq * 4 + t4
                    cs = slice(t4 * 128, (t4 + 1) * 128)
                    qrd = ["q_%da" % qq, "q_%db" % qq, "qz0pad", "qz1pad"]
                    P.add("pe", lambda e, o=ps[b0][:, cs], l_=kT[:, qt * 128:(qt + 1) * 128], r_=qz[r][:, qt * 128:(qt + 1) * 128]:
                          e.matmul(o, l_, r_, start=True, stop=True), reads=["k_%d" % qq] + qrd, writes=["ps%d" % b0])
                    if qt > 0:
                        P.add("pe", lambda e, o=ps[b1][:, cs], l_=kT[:, (qt - 1) * 128:qt * 128], r_=qz[r][:, qt * 128:(qt + 1) * 128]:
                              e.matmul(o, l_, r_, start=True, stop=True), reads=["k_%d" % ((qt - 1) // 4)] + qrd, writes=["ps%d" % b1])

            def sB(n):
                c, r, qq = its[n]
                b0, b1 = 2 * (n % 2), 2 * (n % 2) + 1
                p0 = 0 if qq > 0 else 128
                np1 = (512 - p0) // 128
                bm = bmh[c % 2]
                P.add("dve", lambda e, o=ps[b0].rearrange("p (a b) -> p a b", a=4),
                      b=bm[:, 0, r, :].unsqueeze(1).broadcast_to([128, 4, 128]): e.tensor_tensor(o, o, b, ALU.add),
                      reads=["ps%d" % b0, "bmh%da" % (c % 2)], writes=["ps%d" % b0])
                P.add("dve", lambda e, o=ps[b1][:, p0:512].rearrange("p (a b) -> p a b", a=np1),
                      b=bm[:, 1, r, :].unsqueeze(1).broadcast_to([128, np1, 128]): e.tensor_tensor(o, o, b, ALU.add),
                      reads=["ps%d" % b1, "bmh%db" % (c % 2)], writes=["ps%d" % b1])

            def sP(n):
                c, r, qq = its[n]
                b0, b1 = 2 * (n % 2), 2 * (n % 2) + 1
                P0, P1 = Pt[n % 3]
                p0 = 0 if qq > 0 else 128
                P.add("act", lambda e, o=P0, i_=ps[b0]: e.activation(o, i_, AF.Exp), reads=["ps%d" % b0], writes=["Pt%d_0" % (n % 3)])
                P.add("act", lambda e, o=P1[:, p0:512], i_=ps[b1][:, p0:512]: e.activation(o, i_, AF.Exp), reads=["ps%d" % b1], writes=["Pt%d_1" % (n % 3)])

            def sV(n):
                c, r, qq = its[n]
                P0, P1 = Pt[n % 3]
                ob, db = 4 + n % 2, 6 + n % 2
                vS = vSb[(c // 2) % 2]
                vt = "v%d" % ((c // 2) % 2)
                for t4 in range(4):
                    qt = qq * 4 + t4
                    cs = slice(t4 * 128, (t4 + 1) * 128)
                    has_prev = qt > 0
                    P.add("pe", lambda e, o=ps[ob][:, cs], l_=vS[:, qt, :], r_=P0[:, cs], sp_=(not has_prev):
                          e.matmul(o, l_, r_, start=True, stop=sp_), reads=["Pt%d_0" % (n % 3), vt + "_%d" % qq], writes=["ps%d" % ob])
                    if has_prev:
                        P.add("pe", lambda e, o=ps[ob][:, cs], l_=vS[:, qt - 1, :], r_=P1[:, cs]:
                              e.matmul(o, l_, r_, start=False, stop=True), reads=["Pt%d_1" % (n % 3), vt + "_%d" % ((qt - 1) // 4)], writes=["ps%d" % ob])
                    P.add("pe", lambda e, o=ps[db][:, cs], r_=P0[:, cs], sp_=(not has_prev):
                          e.matmul(o, ones_b, r_, start=True, stop=sp_), reads=["Pt%d_0" % (n % 3), "ones_b"], writes=["ps%d" % db])
                    if has_prev:
                        P.add("pe", lambda e, o=ps[db][:, cs], r_=P1[:, cs]:
                              e.matmul(o, ones_b, r_, start=False, stop=True), reads=["Pt%d_1" % (n % 3), "ones_b"], writes=["ps%d" % db])

            def sN(n):
                c, r, qq = its[n]
                hq = 2 * c + r
                rows = slice(r * 64, (r + 1) * 64)
                sl = slice(qq * 512, (qq + 1) * 512)
                ob, db = 4 + n % 2, 6 + n % 2
                rb = rr[n % 2]
                rn = "rr%d" % (n % 2)
                P.add("act", lambda e, o=rb, i_=ps[db], s_=small[:, ESINK + hq:ESINK + hq + 1]: e.activation(o, i_, AF.Ln, bias=s_, scale=1.0),
                      reads=["ps%d" % db, "small"], writes=[rn])
                P.add("act", lambda e, o=rb: e.activation(o, o, AF.Exp, scale=-1.0), reads=[rn], writes=[rn])
                P.add("dve", lambda e, o=oT[rows, c, sl], i_=ps[ob][rows, :], r_=rb[rows, :]: e.tensor_tensor(o, i_, r_, ALU.mult),
                      reads=["ps%d" % ob, rn], writes=["o%d_%d_%d" % (c, qq, r)])

            run_pipeline(N, [(0, sS), (0, sB), (1, sP), (1, sV), (2, sN)])

        def ffn(l):
            wup = [A.alloc([8, 256], BF16) for _ in range(3)]
            wdn = [A.alloc([6, D], BF16) for _ in range(2)]
            ag = A.alloc([6, S], BF16)
            Ub = [[A.alloc([514], F32) for _ in range(2)] for _ in range(2)]
            Tb = [[A.alloc([512], F32) for _ in range(2)] for _ in range(2)]
            Sg = [A.alloc([512], F32) for _ in range(2)]
            for j in range(2):
                load_w("pool", wup[j], wd["wup%d" % l][j], "wup%d" % j)
            j0_, nj_ = FFN_GROUPS[0]
            load_w("pool", wdn[0][:, 0:nj_, :], wd["wdn%d" % l][j0_:j0_ + nj_].rearrange("j p m -> p j m"), "wdn0")
            cnt = [0, 0]
            pre_up = set()
            pending_up = []
            for gi, (j0, nj) in enumerate(FFN_GROUPS):
                ab = gi % 2
                if gi + 1 < len(FFN_GROUPS):
                    j1, nj1 = FFN_GROUPS[gi + 1]
                    load_w("pool", wdn[1 - ab][:, 0:nj1, :], wd["wdn%d" % l][j1:j1 + nj1].rearrange("j p m -> p j m"), "wdn%d" % (1 - ab))
                tl = [(jj, tg) for jj in range(nj) for tg in range(4)]
                base = cnt[0]
                cnt[0] += len(tl)

                def s_up(m, gi=gi, j0=j0, tl=tl, base=base):
                    if (gi, m) in pre_up:
                        return
                    pre_up.add((gi, m))
                    jj, tg = tl[m]
                    j = j0 + jj
                    n = base + m
                    wb = j % 3
                    if tg == 0 and j + 2 < NJ:
                        load_w("pool", wup[(j + 2) % 3], wd["wup%d" % l][j + 2], "wup%d" % ((j + 2) % 3))
                    sl = slice(tg * 512, (tg + 1) * 512)
                    pb = (n % 2) * 2
                    for half in (1, 0):
                        for kc in range(8):
                            P.add("pe", lambda e, o=ps[pb + half], w=wup[wb][:, kc, half * 128:(half + 1) * 128], r=hT[:, kc, sl], st=(kc == 0), sp_=(kc == 7):
                                  e.matmul(o, w, r, start=st, stop=sp_), reads=["wup%d" % wb, "h%d_%d" % (kc, tg)], writes=["ps%d" % (pb + half)])

                def s_conv(m):
                    jj, tg = tl[m]
                    j = j0 + jj
                    n = base + m
                    b = n % 2
                    pb = (n % 2) * 2
                    cv = V_CONV + (l * NJ + j) * 8
                    taps = []
                    for half in range(2):
                        U = ps[pb + half]
                        un = "ps%d" % (pb + half)
                        ub, T = Ub[half][b], Tb[half][b]
                        umn, uhn, tn = "Um%d_%d" % (half, b), "Uh%d_%d" % (half, b), "T%d_%d" % (half, b)
                        w0 = vecs[:, cv + half * 4 + 0:cv + half * 4 + 1]
                        w1 = vecs[:, cv + half * 4 + 1:cv + half * 4 + 2]
                        w2 = vecs[:, cv + half * 4 + 2:cv + half * 4 + 3]
                        bb = vecs[:, cv + half * 4 + 3:cv + half * 4 + 4]
                        P.add("act", lambda e, o=ub[:, 2:514], i=U: e.activation(o, i, AF.Identity), reads=[un], writes=[umn])
                        P.add("act", lambda e, o=T, i=U, s_=w2, b_=bb: e.activation(o, i, AF.Identity, bias=b_, scale=s_), reads=[un, "vecs"], writes=[tn])
                        if tg == 0:
                            P.add("pool", lambda e, o=ub[:, 0:2]: e.memset(o, 0.0), writes=[uhn])
                        else:
                            P.add("pool", lambda e, o=ub[:, 0:2], i=Ub[half][1 - b][:, 512:514]: e.tensor_copy(o, i),
                                  reads=["Um%d_%d" % (half, 1 - b)], writes=[uhn])
                        taps.append((ub, T, umn, uhn, tn, w0, w1))
                    for which in (1, 0):
                        for (ub, T, umn, uhn, tn, w0, w1) in taps:
                            if which == 1:
                                P.add("dve", lambda e, o=T, i=ub[:, 1:513], s_=w1: e.scalar_tensor_tensor(o, i, s_, o, ALU.mult, ALU.add),
                                      reads=[umn, uhn, "vecs", tn], writes=[tn])
                            else:
                                P.add("dve", lambda e, o=T, i=ub[:, 0:512], s_=w0: e.scalar_tensor_tensor(o, i, s_, o, ALU.mult, ALU.add),
                                      reads=[umn, uhn, "vecs", tn], writes=[tn])

                def s_gate(m):
                    jj, tg = tl[m]
                    n = base + m
                    b = n % 2
                    sl = slice(tg * 512, (tg + 1) * 512)
                    P.add("act", lambda e, o=Sg[b], i=Tb[0][b]: e.activation(o, i, AF.Silu), reads=["T0_%d" % b], writes=["Sg%d" % b])
                    P.add("pool", lambda e, o=ag[:, jj, sl], a_=Sg[b], b_=Tb[1][b]: e.tensor_tensor(o, a_, b_, ALU.mult),
                          reads=["Sg%d" % b, "T1_%d" % b], writes=["a_%d_%d" % (jj, tg)])

                run_pipeline(len(tl), [(0, s_up), (0, s_conv), (1, s_gate)])
                if gi + 1 < len(FFN_GROUPS):
                    j0n, njn = FFN_GROUPS[gi + 1]
                    tln = [(jj, tg) for jj in range(njn) for tg in range(4)]
                    basen = cnt[0]
                    for m_ in range(FFN_PRE):
                        s_up(m_, gi=gi + 1, j0=j0n, tl=tln, base=basen)
                for dc in range(8):
                    for tg in range(4):
                        sl = slice(tg * 512, (tg + 1) * 512)
                        bank = 4 + (cnt[1] % 2)
                        cnt[1] += 1
                        for jj in range(nj):
                            P.add("pe", lambda e, o=ps[bank], w=wdn[ab][:, jj, dc * 128:(dc + 1) * 128], r=ag[:, jj, sl], st=(jj == 0), sp_=(jj == nj - 1):
                                  e.matmul(o, w, r, start=st, stop=sp_), reads=["wdn%d" % ab, "a_%d_%d" % (jj, tg)], writes=["ps%d" % bank])
                        P.add("dve", lambda e, o=xT[:, dc, sl], i=ps[bank]: e.tensor_tensor(o, i, o, ALU.add),
                              reads=["ps%d" % bank, "x%d_%d" % (dc, tg)], writes=["x%d_%d" % (dc, tg)])

        def final_norm():
            sq = [A.alloc([512], BF16) for _ in range(3)]
            rstd = [A.alloc([512], F32) for _ in range(2)]
            yb = [A.alloc([512], F32) for _ in range(4)]
            k = 0
            m = 0
            outs = []
            for tg in range(4):
                sl = slice(tg * 512, (tg + 1) * 512)
                bank = 6 + (tg % 2)
                for c in range(8):
                    b = k % 3
                    k += 1
                    P.add("act", lambda e, o=sq[b], i=xT[:, c, sl]: e.activation(o, i, AF.Square), reads=["x%d_%d" % (c, tg)], writes=["sq%d" % b])
                    P.add("pe", lambda e, o=ps[bank], r=sq[b], st=(c == 0), sp_=(c == 7): e.matmul(o, ones_dmb, r, start=st, stop=sp_),
                          reads=["sq%d" % b, "ones_dm"], writes=["ps%d" % bank])
                rb = rstd[tg % 2]
                P.add("act", lambda e, o=rb, i=ps[bank]: e.activation(o, i, AF.Ln, bias=eps_ap, scale=1.0),
                      reads=["ps%d" % bank, "small"], writes=["rstd%d" % (tg % 2)])
                P.add("act", lambda e, o=rb: e.activation(o, o, AF.Exp, scale=-0.5),
                      reads=["rstd%d" % (tg % 2)], writes=["rstd%d" % (tg % 2)])
                for c in range(8):
                    y = yb[m % 4]
                    yn = "y%d" % (m % 4)
                    m += 1
                    P.add("dve", lambda e, o=y, i=xT[:, c, sl], g=vecs[:, V_FIN + c:V_FIN + c + 1], r=rb: e.scalar_tensor_tensor(o, i, g, r, ALU.mult, ALU.mult),
                          reads=["x%d_%d" % (c, tg), "vecs", "rstd%d" % (tg % 2)], writes=[yn])
                    on = "out%d" % ((m - 1) % 4)
                    P.dma("sp", yout[c * 128:(c + 1) * 128, sl], y, reads=[yn], writes=[on])
                    if on not in outs:
                        outs.append(on)
            return outs

        oT = None
        for l in layers:
            mixer = l % 3
            A.off = persist_end
            P.barrier()
            rmsnorm_to_h(V_AN + 8 * l, "a%d" % l)
            if mixer == 0:
                attn_da(l)
            elif mixer == 1:
                attn_sb(l)
            else:
                attn_swa(l)
            oproj(l, mixer != 0)
            A.off = persist_end
            P.barrier()
            rmsnorm_to_h(V_FN + 8 * l, "f%d" % l)
            if not DEBUG_NOFFN:
                ffn(l)
        outs = []
        A.off = persist_end
        P.barrier()
        if do_final:
            outs = final_norm()
        else:
            for c in range(8):
                on = "out%d" % c
                P.dma("sp", yout[c * 128:(c + 1) * 128, :], xT[:, c, :], reads=["x%d_%d" % (c, t) for t in range(4)], writes=[on])
                outs.append(on)
        P.finish(outs)
        P.emit()
    return nc


LAYER_GROUPS = [[0, 1, 2, 3]]


def kernel(**inputs):
    x = np.asarray(inputs["x"], np.float32)
    B = x.shape[0]
    shared = _prep_shared(inputs)
    cur = [np.ascontiguousarray(x[b].T) for b in range(B)]
    for gi, layers in enumerate(LAYER_GROUPS):
        do_final = (gi == len(LAYER_GROUPS) - 1)
        nc = build_nc(layers, do_final)
        lw = {}
        for l in layers:
            lw.update(_prep_layer(inputs, l))
        in_maps = []
        for b in range(B):
            m = dict(shared)
            m.update(lw)
            m["xin"] = cur[b]
            in_maps.append(m)
        res = run_bass_kernel_spmd(nc, in_maps, core_ids=list(range(B)))
        cur = [np.asarray(res.results[b]["yout"], np.float32) for b in range(B)]
    out = np.stack([c.T for c in cur], axis=0)
    return np.ascontiguousarray(out.astype(np.float32))
```

```python
import math
import numpy as np
from contextlib import ExitStack
import concourse.bass as bass
import concourse.mybir as mybir
from concourse.bass_utils import run_bass_kernel_spmd

F32 = mybir.dt.float32
BF16 = mybir.dt.bfloat16
AF = mybir.ActivationFunctionType
ALU = mybir.AluOpType

D = 1024
S = 2048
DEPTH = 4
DFF = 2752
NJ = 22
EPS = 1e-6
NEGM = -30000.0
FFN_GROUPS = [(0, 6), (6, 6), (12, 5), (17, 5)]
DEBUG_NOFFN = False
FFN_PRE = 2
ATTACH_WAIT = True


class _I:
    __slots__ = ("eng", "fn", "deps", "dma", "sig", "sem", "val", "key")


class Prog:
    ENGS = ("pe", "act", "dve", "pool", "sp")
    SEM_LIMIT = 20000

    def __init__(self, nc, same_eng_sync=True):
        self.nc = nc
        self.same = same_eng_sync
        self.streams = {e: [] for e in self.ENGS}
        self.last_writer = {}
        self.readers = {}
        self.n = 0

    def add(self, eng, fn, reads=(), writes=(), dma=False, nodep=()):
        ins = _I()
        ins.eng, ins.fn, ins.dma, ins.sig, ins.sem, ins.val = eng, fn, dma, False, None, 0
        ins.key = writes[0] if (dma and writes) else None
        deps = set()
        for r in reads:
            w = self.last_writer.get(r)
            if w is not None:
                deps.add(w)
        reads = list(reads) + list(nodep)
        for r in writes:
            w = self.last_writer.get(r)
            if w is not None:
                deps.add(w)
            for rd in self.readers.get(r, ()):
                deps.add(rd)
        for r in reads:
            self.readers.setdefault(r, []).append(ins)
        for r in writes:
            self.last_writer[r] = ins
            self.readers[r] = []
        deps.discard(ins)
        ins.deps = deps
        self.streams[eng].append(ins)
        self.n += 1
        return ins

    def dma(self, q, out, in_, reads=(), writes=()):
        return self.add(q, lambda e: e.dma_start(out=out, in_=in_), reads=reads, writes=writes, dma=True)

    def barrier(self):
        lasts = set()
        for w in self.last_writer.values():
            lasts.add(w)
        for rl in self.readers.values():
            for r in rl:
                lasts.add(r)
        for e in self.ENGS:
            if self.streams[e]:
                lasts.add(self.streams[e][-1])
        for e in self.ENGS:
            ins = _I()
            ins.eng, ins.fn, ins.dma, ins.sig, ins.sem, ins.val, ins.key = e, None, False, False, None, 0, None
            ins.deps = set(lasts)
            self.streams[e].append(ins)
        self.last_writer = {}
        self.readers = {}

    def finish(self, outs):
        self.add("sp", None, reads=list(outs))

    def _skip(self, ins, d):
        if d.dma:
            return False
        if d.eng == ins.eng:
            if d.eng == "pe":
                return True
            if not self.same:
                return True
        return False

    def emit(self):
        nc = self.nc
        for e in self.ENGS:
            for ins in self.streams[e]:
                for d in ins.deps:
                    if d.fn is not None and not self._skip(ins, d):
                        d.sig = True
        semnames = []
        dmacnt = {}
        for e in self.ENGS:
            cnt, idx = 0, 0
            for ins in self.streams[e]:
                if ins.dma:
                    k = ("dma", ins.key)
                    dmacnt[k] = dmacnt.get(k, 0) + 1
                    ins.sem, ins.val = k, 16 * dmacnt[k]
                    if k not in semnames:
                        semnames.append(k)
                elif ins.sig:
                    cnt += 1
                    if cnt > self.SEM_LIMIT:
                        idx += 1
                        cnt = 1
                    ins.sem, ins.val = (e, idx), cnt
                    if ins.sem not in semnames:
                        semnames.append(ins.sem)
        self.nsems = len(semnames)
        with ExitStack() as es:
            sems = {}
            for i, k in enumerate(semnames):
                sems[k] = es.enter_context(nc.semaphore("s%d" % i))
            block = es.enter_context(nc.Block())

            def replay(ename):
                def body(eng):
                    known = {}
                    for ins in self.streams[ename]:
                        need = {}
                        for d in ins.deps:
                            if d.fn is None or self._skip(ins, d):
                                continue
                            if need.get(d.sem, 0) < d.val:
                                need[d.sem] = d.val
                        todo = [(k, v) for k, v in need.items() if known.get(k, 0) < v]
                        for k, v in todo:
                            known[k] = v
                        attach = None
                        if ATTACH_WAIT and ins.fn is not None and todo:
                            attach = todo.pop()
                        for k, v in todo:
                            eng.wait_ge(sems[k], v)
                        if ins.fn is not None:
                            bi = ins.fn(eng)
                            if attach is not None:
                                bi._wait_ge(sems[attach[0]], attach[1])
                            if ins.dma:
                                bi.then_inc(sems[ins.sem], 16)
                            elif ins.sig:
                                bi.then_inc(sems[ins.sem], 1)
                return body

            block.tensor(replay("pe"))
            block.scalar(replay("act"))
            block.vector(replay("dve"))
            block.gpsimd(replay("pool"))
            block.sync(replay("sp"))


def _t5_bucket(dist):
    max_exact = 16
    d = np.maximum(dist, 0)
    large = max_exact + (np.log(np.maximum(d, 1).astype(np.float32) / np.float32(max_exact))
                         / np.float32(math.log(128 / max_exact)) * np.float32(32 - max_exact)).astype(np.int32)
    large = np.minimum(large, 31)
    return np.where(d < max_exact, d, large)


V_AN = 0
V_FN = 32
V_FIN = 64
V_SUB = 72
V_SINK = 74
V_BC = 90
V_LAM = 106
V_CONV = 106 + 512
NV = V_CONV + 4 * NJ * 8


def _prep_shared(inp):
    rel_bias = np.asarray(inp["rel_bias"], np.float32)
    vecs = np.zeros((128, NV), np.float32)
    pc = lambda v: np.asarray(v, np.float32).reshape(8, 128).T
    for l in range(DEPTH):
        vecs[:, V_AN + 8 * l:V_AN + 8 * l + 8] = pc(inp["attn_norm"][l])
        vecs[:, V_FN + 8 * l:V_FN + 8 * l + 8] = pc(inp["ffn_norm"][l])
    vecs[:, V_FIN:V_FIN + 8] = pc(inp["final_norm"])
    vecs[:, V_SUB:V_SUB + 2] = np.asarray(inp["da_subln"], np.float32).T
    vecs[:, V_SINK:V_SINK + 16] = np.asarray(inp["sw_sinks"], np.float32).reshape(1, 16)
    vecs[:, V_BC:V_BC + 16] = rel_bias[31][None, :]
    vecs[:, V_LAM:V_LAM + 512] = np.asarray(inp["da_lambda"], np.float32).reshape(1, 512)
    cw = np.asarray(inp["ffn_conv_w"], np.float32)
    cb = np.asarray(inp["ffn_conv_b"], np.float32)
    for l in range(DEPTH):
        for half in range(2):
            w = np.zeros((3, NJ * 128), np.float32)
            b = np.zeros((NJ * 128,), np.float32)
            w[:, :DFF] = cw[l][:, half * DFF:(half + 1) * DFF]
            b[:DFF] = cb[l][half * DFF:(half + 1) * DFF]
            w = w.reshape(3, NJ, 128)
            b = b.reshape(NJ, 128)
            for j in range(NJ):
                base = V_CONV + (l * NJ + j) * 8 + half * 4
                vecs[:, base + 0] = w[0, j]
                vecs[:, base + 1] = w[1, j]
                vecs[:, base + 2] = w[2, j]
                vecs[:, base + 3] = b[j]
    kk = np.arange(128)[:, None]
    qq = np.arange(128)[None, :]
    d0 = qq - kk
    d1 = qq - kk + 128
    b0 = rel_bias[_t5_bucket(d0)]
    b1 = rel_bias[_t5_bucket(d1)]
    bm = np.zeros((128, 3, 16, 128), np.float32)
    neg = np.float32(NEGM)
    bm[:, 0] = np.where((d0 >= 0)[:, :, None], b0, neg).transpose(0, 2, 1)
    bm[:, 1] = b1.transpose(0, 2, 1)
    bm[:, 2] = np.where((d1 < 128)[:, :, None], b1, neg).transpose(0, 2, 1)
    masks = np.zeros((128, 2, 128), np.float32)
    masks[:, 0] = (kk < qq).astype(np.float32)
    masks[:, 1] = np.where(kk < qq, np.float32(0), neg)
    tri = (kk >= qq).astype(np.float32)
    consts = np.zeros((128, 2, 128), np.float32)
    consts[:, 0] = 1.0
    consts[:, 1] = tri
    return dict(vecs=vecs, bm=bm, masks=masks, consts=consts)


def _prep_layer(inp, l):
    mixer, slot = l % 3, l // 3
    out = {}
    if mixer in (0, 1):
        w = np.asarray(inp["da_w_qkv" if mixer == 0 else "sb_w_qkv"][slot], np.float32)
        q = w[:, 0:1024].reshape(D, 8, 128)
        k = w[:, 1024:2048].reshape(D, 8, 128)
        v = w[:, 2048:3072].reshape(D, 8, 128)
        wh = np.concatenate([q, k, v], axis=2)
        wh = wh.reshape(8, 128, 8, 384).transpose(2, 1, 0, 3)
        out["wqkv%d" % l] = np.ascontiguousarray(wh)
    else:
        w = np.asarray(inp["sw_w_qkv"][slot], np.float32)
        q = w[:, 0:1024]
        k = w[:, 1024:1280].reshape(D, 4, 64)
        v = w[:, 1280:1536].reshape(D, 4, 64)
        kd = np.concatenate([k, k], axis=2)
        vd = np.concatenate([v, v], axis=2)
        gi = np.arange(8) // 2
        wh = np.concatenate([q.reshape(D, 8, 128), kd[:, gi, :], vd[:, gi, :]], axis=2)
        wh = wh.reshape(8, 128, 8, 384).transpose(2, 1, 0, 3)
        out["wqkv%d" % l] = np.ascontiguousarray(wh)
    wo = np.asarray(inp["w_o"][l], np.float32)
    wo = wo.reshape(8, 128, 8, 128).transpose(2, 1, 0, 3)
    out["wo%d" % l] = np.ascontiguousarray(wo)
    wu = np.asarray(inp["ffn_w_up"][l], np.float32)
    g = np.zeros((D, NJ * 128), np.float32)
    v = np.zeros((D, NJ * 128), np.float32)
    g[:, :DFF] = wu[:, :DFF]
    v[:, :DFF] = wu[:, DFF:]
    gv = np.concatenate([g.reshape(D, NJ, 128), v.reshape(D, NJ, 128)], axis=2)
    gv = gv.reshape(8, 128, NJ, 256).transpose(2, 1, 0, 3)
    out["wup%d" % l] = np.ascontiguousarray(gv)
    wd = np.zeros((NJ * 128, D), np.float32)
    wd[:DFF] = np.asarray(inp["ffn_w_down"][l], np.float32)
    out["wdn%d" % l] = np.ascontiguousarray(wd.reshape(NJ, 128, D))
    return out


def _lambda_init(layer):
    return 0.8 - 0.6 * math.exp(-0.3 * layer)


class Pool_:
    def __init__(self, ap_u16, nbytes):
        self.ap = ap_u16
        self.n = nbytes
        self.off = 0

    def alloc(self, free_shape, dt):
        esz = 4 if dt == F32 else 2
        nel = int(np.prod(free_shape))
        nb = nel * esz
        off = (self.off + 63) // 64 * 64
        assert off + nb <= self.n, ("SBUF pool overflow", off, nb, self.n)
        self.off = off + nb
        a = self.ap[:, off // 2:(off + nb) // 2]
        if dt == F32:
            a = a.bitcast(F32)
        if len(free_shape) == 2:
            a = a.rearrange("p (a b) -> p a b", a=free_shape[0])
        elif len(free_shape) == 3:
            a = a.rearrange("p (a b c) -> p a b c", a=free_shape[0], b=free_shape[1])
        return a


def build_nc(layers, do_final):
    nc = bass.Bass("TRN2", target_bir_lowering=False)
    nc.allow_low_precision("bf16 matmul operands with fp32 PSUM accumulation by design")
    dram = lambda n, s, k="ExternalInput": nc.dram_tensor(n, list(s), F32, kind=k).ap()
    xin = dram("xin", [D, S])
    yout = dram("yout", [D, S], "ExternalOutput")
    vecs_d = dram("vecs", [128, NV])
    bm_d = dram("bm", [128, 3, 16, 128])
    masks_d = dram("masks", [128, 2, 128])
    consts_d = dram("consts", [128, 2, 128])
    wd = {}
    for l in layers:
        mixer = l % 3
        wd["wqkv%d" % l] = dram("wqkv%d" % l, [8, 128, 8, 384])
        wd["wo%d" % l] = dram("wo%d" % l, [8, 128, 8, 128])
        wd["wup%d" % l] = dram("wup%d" % l, [NJ, 128, 8, 256])
        wd["wdn%d" % l] = dram("wdn%d" % l, [NJ, 128, D])

    with ExitStack() as es:
        POOLB = 212480
        pool_t = es.enter_context(nc.sbuf_tensor("pool", [128, POOLB // 2], BF16))
        A = Pool_(pool_t[:, :], POOLB)
        ps = [es.enter_context(nc.psum_tensor("ps%d" % i, [128, 512], F32))[:, :] for i in range(8)]
        P = Prog(nc)

        xT = A.alloc([8, S], F32)
        hT = A.alloc([8, S], BF16)
        vecs = A.alloc([NV], F32)
        ones_b = A.alloc([128], BF16)
        tri_b = A.alloc([128], BF16)
        nones_b = A.alloc([128], BF16)
        ntri_b = A.alloc([128], BF16)
        ones_dmb = A.alloc([128], BF16)
        ones_eb = A.alloc([128], BF16)
        ones_dm = A.alloc([128], F32)
        ones_e = A.alloc([128], F32)
        cst_f = A.alloc([2, 128], F32)
        small = A.alloc([64], F32)
        persist_end = A.off

        eps_ap = small[:, 4:5]
        NEGLAM = 0
        SUBS = 2
        ESINK = 8

        for c in range(8):
            P.dma("sp", xT[:, c, :], xin[c * 128:(c + 1) * 128, :], writes=["x%d_%d" % (c, t) for t in range(4)])
        P.dma("sp", vecs, vecs_d, writes=["vecs"])
        P.dma("sp", cst_f, consts_d, writes=["cstf"])
        P.add("dve", lambda e: e.memset(small[:, 4:5], EPS), writes=["small"])
        P.add("dve", lambda e: e.tensor_copy(ones_b, cst_f[:, 0, :]), reads=["cstf"], writes=["ones_b"])
        P.add("dve", lambda e: e.tensor_copy(tri_b, cst_f[:, 1, :]), reads=["cstf"], writes=["tri_b"])
        P.add("dve", lambda e: e.tensor_scalar(nones_b, cst_f[:, 0, :], -1.0, None, ALU.mult), reads=["cstf"], writes=["ones_b"])
        P.add("dve", lambda e: e.tensor_scalar(ntri_b, cst_f[:, 1, :], -1.0, None, ALU.mult), reads=["cstf"], writes=["tri_b"])
        P.add("dve", lambda e: e.tensor_scalar(ones_dm, cst_f[:, 0, :], 1.0 / D, None, ALU.mult), reads=["cstf"], writes=["ones_dm"])
        P.add("dve", lambda e: e.tensor_scalar(ones_dmb, cst_f[:, 0, :], 1.0 / D, None, ALU.mult), reads=["cstf"], writes=["ones_dm"])
        P.add("dve", lambda e: e.tensor_scalar(ones_eb, cst_f[:, 0, :], 1.0 / 128, None, ALU.mult), reads=["cstf"], writes=["ones_e"])
        P.add("dve", lambda e: e.tensor_scalar(ones_e, cst_f[:, 0, :], 1.0 / 128, None, ALU.mult), reads=["cstf"], writes=["ones_e"])

        def rmsnorm_to_h(gcol, tag):
            sq = [A.alloc([512], BF16) for _ in range(4)]
            rstd = [A.alloc([512], F32) for _ in range(2)]
            k = 0
            for tg in range(4):
                sl = slice(tg * 512, (tg + 1) * 512)
                bank = 6 + (tg % 2)
                for c in range(8):
                    b = k % 4
                    k += 1
                    eng = "act"
                    if eng == "act":
                        P.add("act", lambda e, o=sq[b], i=xT[:, c, sl]: e.activation(o, i, AF.Square),
                              reads=["x%d_%d" % (c, tg)], writes=["sq%d" % b])
                    else:
                        P.add("pool", lambda e, o=sq[b], i=xT[:, c, sl]: e.tensor_tensor(o, i, i, ALU.mult),
                              reads=["x%d_%d" % (c, tg)], writes=["sq%d" % b])
                    P.add("pe", lambda e, o=ps[bank], r=sq[b], st=(c == 0), sp_=(c == 7): e.matmul(o, ones_dmb, r, start=st, stop=sp_),
                          reads=["sq%d" % b, "ones_dm"], writes=["ps%d" % bank])
                rb = rstd[tg % 2]
                P.add("act", lambda e, o=rb, i=ps[bank]: e.activation(o, i, AF.Ln, bias=eps_ap, scale=1.0),
                      reads=["ps%d" % bank, "small"], writes=["rstd%d" % (tg % 2)])
                P.add("act", lambda e, o=rb: e.activation(o, o, AF.Exp, scale=-0.5),
                      reads=["rstd%d" % (tg % 2)], writes=["rstd%d" % (tg % 2)])
                for c in range(8):
                    P.add("dve", lambda e, o=hT[:, c, sl], i=xT[:, c, sl], g=vecs[:, gcol + c:gcol + c + 1], r=rb:
                          e.scalar_tensor_tensor(o, i, g, r, ALU.mult, ALU.mult),
                          reads=["x%d_%d" % (c, tg), "vecs", "rstd%d" % (tg % 2)], writes=["h%d_%d" % (c, tg)])

        H_ALL = ["h%d_%d" % (c, t) for c in range(8) for t in range(4)]

        def load_w(q, dst, src, name):
            P.dma(q, dst, src, writes=[name])

        def proj_qkv_head(wt, wname, qT, kT, vS, qscale, tag, banks, vtag="v", kpad=None, qpad=None, do_kv=True, evac="act"):
            bi = 0
            for which, dst in ((0, qT), (1, kT)):
                if which == 1 and not do_kv:
                    continue
                for tg in range(4):
                    sl = slice(tg * 512, (tg + 1) * 512)
                    bank = banks[bi % len(banks)]
                    bi += 1
                    for kc in range(8):
                        P.add("pe", lambda e, o=ps[bank], w=wt[:, kc, which * 128:(which + 1) * 128], r=hT[:, kc, sl], st=(kc == 0), sp_=(kc == 7):
                              e.matmul(o, w, r, start=st, stop=sp_),
                              reads=[wname, "h%d_%d" % (kc, tg)], writes=["ps%d" % bank])
                    rn = ("q" if which == 0 else "k") + tag + "_%d" % tg
                    if which == 0 and qpad is not None:
                        P.add("act", lambda e, o=qpad[0][0:64, sl], i=ps[bank][0:64, :]: e.activation(o, i, AF.Identity, scale=qscale),
                              reads=["ps%d" % bank], writes=[rn + "a"])
                        P.add("act", lambda e, o=qpad[1][64:128, sl], i=ps[bank][64:128, :]: e.activation(o, i, AF.Identity, scale=qscale),
                              reads=["ps%d" % bank], writes=[rn + "b"])
                    elif which == 0 and qscale != 1.0:
                        if evac == "dve":
                            P.add("dve", lambda e, o=dst[:, sl], i=ps[bank]: e.tensor_scalar(o, i, qscale, None, ALU.mult),
                                  reads=["ps%d" % bank], writes=[rn])
                        else:
                            P.add("act", lambda e, o=dst[:, sl], i=ps[bank]: e.activation(o, i, AF.Identity, scale=qscale),
                                  reads=["ps%d" % bank], writes=[rn])
                    elif which == 1 and kpad is not None:
                        P.add("dve", lambda e, o=kpad[0][0:64, sl], i=ps[bank][0:64, :]: e.tensor_copy(o, i),
                              reads=["ps%d" % bank], writes=[rn + "a"])
                        if evac == "dve":
                            P.add("dve", lambda e, o=kpad[1][64:128, sl], i=ps[bank][64:128, :]: e.tensor_copy(o, i),
                                  reads=["ps%d" % bank], writes=[rn + "b"])
                        else:
                            P.add("act", lambda e, o=kpad[1][64:128, sl], i=ps[bank][64:128, :]: e.activation(o, i, AF.Identity),
                                  reads=["ps%d" % bank], writes=[rn + "b"])
                    else:
                        P.add("dve", lambda e, o=dst[:, sl], i=ps[bank]: e.tensor_copy(o, i),
                              reads=["ps%d" % bank], writes=[rn])
            for t4 in range(4 if do_kv else 0):
                bank = banks[bi % len(banks)]
                bi += 1
                for tt in range(4):
                    tok = slice((t4 * 4 + tt) * 128, (t4 * 4 + tt + 1) * 128)
                    for kc in range(8):
                        P.add("pe", lambda e, o=ps[bank][:, tt * 128:(tt + 1) * 128], l=hT[:, kc, tok], r=wt[:, kc, 256:384], st=(kc == 0), sp_=(kc == 7):
                              e.matmul(o, l, r, start=st, stop=sp_),
                              reads=[wname, "h%d_%d" % (kc, t4)], writes=["ps%d" % bank])
                if evac == "dve":
                    P.add("dve", lambda e, o=vS[:, t4 * 4:(t4 + 1) * 4, :], i=ps[bank].rearrange("p (a b) -> p a b", a=4): e.tensor_copy(o, i),
                          reads=["ps%d" % bank], writes=[vtag + "_%d" % t4])
                else:
                    P.add("act", lambda e, o=vS[:, t4 * 4:(t4 + 1) * 4, :], i=ps[bank].rearrange("p (a b) -> p a b", a=4): e.activation(o, i, AF.Identity),
                          reads=["ps%d" % bank], writes=[vtag + "_%d" % t4])

        def oproj(l, parts):
            wo = [A.alloc([8, 128], BF16) for _ in range(2)]
            for dc in range(8):
                wb = dc % 2
                load_w("pool", wo[wb], wd["wo%d" % l][dc], "wo%d" % wb)
                for tg in range(4):
                    sl = slice(tg * 512, (tg + 1) * 512)
                    bank = 6 + ((dc * 4 + tg) % 2)
                    for kc in range(8):
                        P.add("pe", lambda e, o=ps[bank], w=wo[wb][:, kc, :], r=oT[:, kc, sl], st=(kc == 0), sp_=(kc == 7):
                              e.matmul(o, w, r, start=st, stop=sp_),
                              reads=["wo%d" % wb] + (["o%d_%d_0" % (kc, tg), "o%d_%d_1" % (kc, tg)] if parts else ["o%d_%d" % (kc, tg)]), writes=["ps%d" % bank])
                    P.add("dve", lambda e, o=xT[:, dc, sl], i=ps[bank]: e.tensor_tensor(o, i, o, ALU.add),
                          reads=["ps%d" % bank, "x%d_%d" % (dc, tg)], writes=["x%d_%d" % (dc, tg)])

        def run_pipeline(N, stages):
            maxlag = max(lg for lg, _ in stages)
            for it in range(N + maxlag):
                for lg, fn in stages:
                    n = it - lg
                    if 0 <= n < N:
                        fn(n)

        def attn_da(l):
            slot = l // 3
            li = _lambda_init(l)
            nonlocal oT
            oT = A.alloc([8, S], BF16)
            bmh = [A.alloc([2, 2, 128], F32) for _ in range(2)]
            wts = [A.alloc([8, 384], BF16) for _ in range(2)]
            qT = A.alloc([S], BF16)
            kT = None
            kz = [A.alloc([S], BF16) for _ in range(2)]
            P.add("pool", lambda e: e.memset(kz[0][64:128, :], 0.0), writes=["kz0pad"])
            P.add("pool", lambda e: e.memset(kz[1][0:64, :], 0.0), writes=["kz1pad"])
            vSb = [A.alloc([16, 128], BF16) for _ in range(2)]
            NS, NP = 3, 3
            Pb = [A.alloc([512], BF16) for _ in range(NP)]
            tD = [A.alloc([256], F32) for _ in range(2)]
            tmp = [A.alloc([512], F32) for _ in range(5)]
            sqb = tmp[3].bitcast(BF16)[:, 0:512]
            lam = vecs[:, V_LAM + slot * 256:V_LAM + slot * 256 + 256]
            P.add("dve", lambda e: e.tensor_tensor(tmp[0][:, 0:64], lam[:, 0:64], lam[:, 64:128], ALU.mult), reads=["vecs"], writes=["tmp0"])
            P.add("dve", lambda e: e.tensor_tensor(tmp[0][:, 64:128], lam[:, 128:192], lam[:, 192:256], ALU.mult), reads=["vecs", "tmp0"], writes=["tmp0"])
            P.add("dve", lambda e: e.reduce_sum(tmp[1][:, 0:2], tmp[0][:, 0:128].rearrange("p (a b) -> p a b", a=2), mybir.AxisListType.X),
                  reads=["tmp0"], writes=["tmp1"])
            P.add("act", lambda e: e.activation(tmp[1][:, 2:4], tmp[1][:, 0:2], AF.Exp), reads=["tmp1"], writes=["tmp1"])
            P.add("dve", lambda e: e.scalar_tensor_tensor(small[:, NEGLAM + slot:NEGLAM + slot + 1], tmp[1][:, 3:4], -li, tmp[1][:, 2:3], ALU.add, ALU.subtract),
                  reads=["tmp1"], writes=["small"])
            P.add("dve", lambda e: e.tensor_scalar(small[:, SUBS + slot:SUBS + slot + 1], vecs[:, V_SUB + slot:V_SUB + slot + 1], 1.0 - li, None, ALU.mult),
                  reads=["vecs", "small"], writes=["small"])
            neglam = small[:, NEGLAM + slot:NEGLAM + slot + 1]
            subs = small[:, SUBS + slot:SUBS + slot + 1]

            def prefetch(h):
                load_w("pool", wts[h % 2], wd["wqkv%d" % l][h], "wt%d" % (h % 2))

            def prefetch_bm(h):
                P.dma("sp", bmh[h % 2], bm_d[:, 0:2, 2 * h:2 * h + 2, :], writes=["bmh%d" % (h % 2)])
                for j_ in range(2):
                    ch_ = 2 * h + j_
                    P.add("pool", lambda e, o=bmh[h % 2][:, :, j_, :], c_=vecs[:, V_BC + ch_:V_BC + ch_ + 1]: e.tensor_scalar(o, o, c_, None, ALU.subtract),
                          reads=["vecs", "bmh%d" % (h % 2)], writes=["bmh%d" % (h % 2)])

            tiles = []
            gi = 0
            for h in range(8):
                for g in range(4):
                    for j in range(2):
                        for kt in range(4 * g + 4):
                            tiles.append(dict(h=h, g=g, j=j, kt=kt, gi=gi, first=(kt == 0), last=(kt == 4 * g + 3),
                                              head_first=(g == 0 and j == 0 and kt == 0)))
                        gi += 1
            N = len(tiles)
            deferred = []

            def sS(n):
                t = tiles[n]
                h, g, j, kt = t["h"], t["g"], t["j"], t["kt"]
                if t["head_first"]:
                    if h == 0:
                        prefetch(0)
                        prefetch_bm(0)
                    if h + 1 < 8:
                        prefetch(h + 1)
                    proj_qkv_head(wts[h % 2], "wt%d" % (h % 2), qT, kT, vSb[h % 2], 0.125, "", banks=(7, n % NS), vtag="v%d" % (h % 2), kpad=kz, evac="act")
                if g == 1 and j == 0 and kt == 0 and h + 1 < 8:
                    prefetch_bm(h + 1)
                i = kt - 4 * g
                q0 = max(i, 0) * 128
                sb = n % NS
                rows = slice(j * 64, (j + 1) * 64)
                P.add("pe", lambda e, o=ps[sb][:, q0:512], l_=kz[j][:, kt * 128:(kt + 1) * 128], r=qT[:, g * 512 + q0:(g + 1) * 512]:
                      e.matmul(o, l_, r, start=True, stop=True),
                      reads=["k_%da" % (kt // 4), "k_%db" % (kt // 4), "kz0pad", "kz1pad", "q_%d" % g], writes=["ps%d" % sb])

            def sB(n):
                t = tiles[n]
                h, g, j, kt = t["h"], t["g"], t["j"], t["kt"]
                i = kt - 4 * g
                if i < -1:
                    return
                q0 = max(i, 0) * 128
                sb = n % NS
                bm = bmh[h % 2]
                bn = "bmh%d" % (h % 2)
                if i >= 0 and i < 3:
                    P.add("dve", lambda e, o=ps[sb][:, q0:q0 + 256].rearrange("p (a b) -> p a b", a=2), b=bm[:, 0:2, j, :]: e.tensor_tensor(o, o, b, ALU.add),
                          reads=["ps%d" % sb, bn], writes=["ps%d" % sb])
                elif i == 3:
                    P.add("dve", lambda e, o=ps[sb][:, q0:q0 + 128], b=bm[:, 0, j, :]: e.tensor_tensor(o, o, b, ALU.add),
                          reads=["ps%d" % sb, bn], writes=["ps%d" % sb])
                else:
                    P.add("dve", lambda e, o=ps[sb][:, 0:128], b=bm[:, 1, j, :]: e.tensor_tensor(o, o, b, ALU.add),
                          reads=["ps%d" % sb, bn], writes=["ps%d" % sb])

            def sP(n):
                t = tiles[n]
                h, g, j, kt = t["h"], t["g"], t["j"], t["kt"]
                i = kt - 4 * g
                q0 = max(i, 0) * 128
                sb = n % NS
                Pt = Pb[n % NP]
                pn = "P%d" % (n % NP)
                P.add("act", lambda e, o=Pt[:, q0:512], i_=ps[sb][:, q0:512]: e.activation(o, i_, AF.Exp),
                      reads=["ps%d" % sb], writes=[pn + "a"])

            def sV(n):
                t = tiles[n]
                h, g, j, kt, gi_ = t["h"], t["g"], t["j"], t["kt"], t["gi"]
                i = kt - 4 * g
                q0 = max(i, 0) * 128
                Pt = Pb[n % NP]
                pn = "P%d" % (n % NP)
                ob, db = 3 + gi_ % 2, 5 + gi_ % 2
                first, last = t["first"], t["last"]
                vS = vSb[h % 2]
                P.add("pe", lambda e, o=ps[ob][:, q0:512], l_=vS[:, kt, :], r=Pt[:, q0:512]: e.matmul(o, l_, r, start=first, stop=last),
                      reads=[pn + "a", pn + "b", "v%d_%d" % (h % 2, kt // 4)], writes=["ps%d" % ob])
                P.add("pe", lambda e, o=ps[db][:, q0:512], r=Pt[:, q0:512]: e.matmul(o, ones_b, r, start=first, stop=last),
                      reads=[pn + "a", pn + "b", "ones_b"], writes=["ps%d" % db])
                if not last:
                    return
                r0, a0, a1, sq, rs = tmp
                aj = a0 if j == 0 else a1
                an = "tmp1" if j == 0 else "tmp2"
                Q = deferred.append
                Q(lambda d_=ps[db], dn="ps%d" % db: P.add("act", lambda e: e.activation(r0, d_, AF.Ln), reads=[dn], writes=["tmp0"]))
                Q(lambda: P.add("act", lambda e: e.activation(r0, r0, AF.Exp, scale=-1.0), reads=["tmp0"], writes=["tmp0"]))
                Q(lambda o_=ps[ob], a_=aj, on_="ps%d" % ob, an_=an: P.add("dve", lambda e: e.tensor_tensor(a_, o_, r0, ALU.mult), reads=[on_, "tmp0"], writes=[an_]))
                if j == 1:
                    sl = slice(g * 512, (g + 1) * 512)
                    Q(lambda: P.add("dve", lambda e: e.scalar_tensor_tensor(a0, a1, neglam, a0, ALU.mult, ALU.add), reads=["tmp1", "tmp2", "small"], writes=["tmp1"]))
                    Q(lambda: P.add("act", lambda e: e.activation(sqb, a0, AF.Square), reads=["tmp1"], writes=["tmp3"]))
                    Q(lambda: P.add("pe", lambda e: e.matmul(ps[7], ones_eb, sqb, start=True, stop=True), reads=["tmp3", "ones_e"], writes=["ps7"]))
                    Q(lambda: P.add("act", lambda e: e.activation(rs, ps[7], AF.Ln, bias=eps_ap, scale=1.0), reads=["ps7", "small"], writes=["tmp4"]))
                    Q(lambda: P.add("act", lambda e: e.activation(rs, rs, AF.Exp, scale=-0.5), reads=["tmp4"], writes=["tmp4"]))
                    Q(lambda o=oT[:, h, sl], on_="o%d_%d" % (h, g): P.add("dve", lambda e: e.scalar_tensor_tensor(o, a0, subs, rs, ALU.mult, ALU.mult),
                                                                  reads=["tmp1", "tmp4", "small"], writes=[on_]))

            def sD(n):
                k_ = 2 if len(deferred) > 6 else 1
                for _ in range(k_):
                    if deferred:
                        deferred.pop(0)()

            run_pipeline(N, [(0, sS), (1, sB), (2, sP), (3, sV), (3, sD)])
            while deferred:
                deferred.pop(0)()

        def attn_sb(l):
            nonlocal oT
            oT = A.alloc([8, S], BF16)
            mk = A.alloc([2, 128], F32)
            P.dma("sp", mk, masks_d, writes=["mk"])
            wts = [A.alloc([8, 384], BF16) for _ in range(2)]
            qT = A.alloc([S], BF16)
            kT = None
            kz = [A.alloc([S], BF16) for _ in range(2)]
            P.add("pool", lambda e: e.memset(kz[0][64:128, :], 0.0), writes=["kz0pad"])
            P.add("pool", lambda e: e.memset(kz[1][0:64, :], 0.0), writes=["kz1pad"])
            vSb = [A.alloc([16, 128], BF16) for _ in range(2)]
            NZ, NW, NL, NA = 5, 4, 2, 2
            HSEL = ["dve"]
            LSEL = ["pool"]
            Wb = [A.alloc([512], F32) for _ in range(NW)]
            NE = 2 if all(h_ == "dve" for h_ in HSEL) else NW
            Eb = [A.alloc([512], F32) for _ in range(NE)]
            Lh = [A.alloc([512], BF16) for _ in range(NL)]
            Ll = [A.alloc([512], BF16) for _ in range(NL)]
            Ab = [A.alloc([512], BF16) for _ in range(NA)]
            Cb = [A.alloc([512], F32) for _ in range(2)]

            tiles = []
            for c in range(8):
                for r in range(2):
                    for g in range(4):
                        for kt in range(4 * g + 3, -1, -1):
                            tiles.append(dict(c=c, r=r, g=g, kt=kt, first=(kt == 4 * g + 3), last=(kt == 0),
                                              pair_first=(r == 0 and g == 0 and kt == 3)))
            N = len(tiles)
            for n in range(N):
                t = tiles[n]
                i = t["kt"] - 4 * t["g"]
                t["i"] = i
                t["q0"] = max(i, 0) * 128
                t["pq0"] = tiles[n - 1]["q0"] if not t["first"] else 512

            def prefetch(c):
                load_w("pool", wts[c % 2], wd["wqkv%d" % l][c], "wt%d" % (c % 2))

            def sZ(n):
                t = tiles[n]
                c, r, g, kt, q0 = t["c"], t["r"], t["g"], t["kt"], t["q0"]
                if t["pair_first"]:
                    if c == 0:
                        prefetch(0)
                    if c + 1 < 8:
                        prefetch(c + 1)
                    proj_qkv_head(wts[c % 2], "wt%d" % (c % 2), qT, kT, vSb[c % 2], 0.125, "", banks=(5, 6), vtag="v%d" % (c % 2), kpad=kz)
                rows = slice(r * 64, (r + 1) * 64)
                zb = n % NZ
                P.add("pe", lambda e, o=ps[zb][:, q0:512], l_=kz[r][:, kt * 128:(kt + 1) * 128], r_=qT[:, g * 512 + q0:(g + 1) * 512]:
                      e.matmul(o, l_, r_, start=True, stop=False, skip_group_check=True),
                      reads=["k_%da" % (kt // 4), "k_%db" % (kt // 4), "kz0pad", "kz1pad", "q_%d" % g], writes=["ps%d" % zb])

            def sE(n):
                t = tiles[n]
                q0 = t["q0"]
                zb = n % NZ
                W = Wb[n % NW]
                wn = "W%d" % (n % NW)
                E = Eb[n % NE]
                en = "E%d" % (n % NE)
                P.add("act", lambda e, o=E[:, q0:512], i_=ps[zb][:, q0:512]: e.activation(o, i_, AF.Exp), reads=["ps%d" % zb], writes=[en])
                P.add("act", lambda e, o=W[:, q0:512], i_=E[:, q0:512]: e.activation(o, i_, AF.Ln, bias=1.0, scale=1.0), reads=[en], writes=[wn])

            def sM(n):
                t = tiles[n]
                q0 = t["q0"]
                W = Wb[n % NW]
                wn = "W%d" % (n % NW)
                if t["i"] >= 0:
                    P.add("dve", lambda e, o=W[:, q0:q0 + 128], m=mk[:, 0, :]: e.tensor_tensor(o, o, m, ALU.mult), reads=[wn, "mk"], writes=[wn])

            def sL(n):
                t = tiles[n]
                q0 = t["q0"]
                W = Wb[n % NW]
                wn = "W%d" % (n % NW)
                LH, LL = Lh[n % NL], Ll[n % NL]
                hsel = HSEL[n % len(HSEL)]
                if hsel == "act":
                    P.add("act", lambda e, o=LH[:, q0:512], i_=Eb[n % NE][:, q0:512]: e.activation(o, i_, AF.Ln, bias=1.0, scale=1.0),
                          reads=["E%d" % (n % NE)], writes=["LH%d" % (n % NL)])
                else:
                    P.add("dve", lambda e, o=LH[:, q0:512], i_=W[:, q0:512]: e.tensor_copy(o, i_), reads=[wn], writes=["LH%d" % (n % NL)])
                P.add(LSEL[n % len(LSEL)], lambda e, o=LL[:, q0:512], i_=W[:, q0:512], hh=LH[:, q0:512]: e.tensor_tensor(o, i_, hh, ALU.subtract),
                      reads=[wn, "LH%d" % (n % NL)], writes=["LL%d" % (n % NL)])

            def sT(n):
                t = tiles[n]
                q0 = t["q0"]
                zb = n % NZ
                cb = 5 + n % 2
                LH, LL = Lh[n % NL], Ll[n % NL]
                hn, ln_ = "LH%d" % (n % NL), "LL%d" % (n % NL)
                if not t["last"]:
                    P.add("pe", lambda e, o=ps[cb][:, q0:512], r_=LH[:, q0:512]: e.matmul(o, nones_b, r_, start=True, stop=False),
                          reads=[hn, "ones_b"], writes=["ps%d" % cb])
                P.add("pe", lambda e, o=ps[zb][:, q0:512], r_=LH[:, q0:512]: e.matmul(o, ntri_b, r_, start=False, stop=False, skip_group_check=True),
                      reads=[hn, "tri_b"], writes=["ps%d" % zb])
                if not t["last"]:
                    P.add("pe", lambda e, o=ps[cb][:, q0:512], r_=LL[:, q0:512]: e.matmul(o, nones_b, r_, start=False, stop=True),
                          reads=[ln_, "ones_b"], writes=["ps%d" % cb])
                P.add("pe", lambda e, o=ps[zb][:, q0:512], r_=LL[:, q0:512]: e.matmul(o, ntri_b, r_, start=False, stop=True, skip_group_check=True),
                      reads=[ln_, "tri_b"], writes=["ps%d" % zb])

            def sC(n):
                t = tiles[n]
                if t["last"]:
                    return
                q0, pq0 = t["q0"], t["pq0"]
                cb = 5 + n % 2
                C, Cp = Cb[n % 2], Cb[(n - 1) % 2]
                cn, cpn = "C%d" % (n % 2), "C%d" % ((n - 1) % 2)
                if pq0 > q0:
                    P.add("dve", lambda e, o=C[:, q0:pq0], i_=ps[cb][:, q0:pq0]: e.tensor_copy(o, i_), reads=["ps%d" % cb], writes=[cn + "a"])
                if pq0 < 512:
                    P.add("dve", lambda e, o=C[:, pq0:512], i_=ps[cb][:, pq0:512], p_=Cp[:, pq0:512]: e.tensor_tensor(o, i_, p_, ALU.add),
                          reads=["ps%d" % cb, cpn + "a", cpn + "b"], writes=[cn + "b"])

            def sX(n):
                t = tiles[n]
                q0, pq0 = t["q0"], t["pq0"]
                zb = n % NZ
                W = Wb[n % NW]
                wn = "W%d" % (n % NW)
                Cp = Cb[(n - 1) % 2]
                cpn = "C%d" % ((n - 1) % 2)
                c0 = q0
                if t["i"] >= 0:
                    P.add("dve", lambda e, o=W[:, q0:q0 + 128], i_=ps[zb][:, q0:q0 + 128], m=mk[:, 1, :]: e.tensor_tensor(o, i_, m, ALU.add),
                          reads=["ps%d" % zb, "mk", wn], writes=[wn])
                    c0 = q0 + 128
                if c0 < 512:
                    P.add("dve", lambda e, o=W[:, c0:512], i_=ps[zb][:, c0:512], c_=Cp[:, c0:512]: e.tensor_tensor(o, i_, c_, ALU.add),
                          reads=["ps%d" % zb, cpn + "a", cpn + "b", wn], writes=[wn])

            def sA(n):
                t = tiles[n]
                q0 = t["q0"]
                W = Wb[n % NW]
                wn = "W%d" % (n % NW)
                P.add("act", lambda e, o=Ab[n % NA][:, q0:512], i_=W[:, q0:512]: e.activation(o, i_, AF.Exp), reads=[wn], writes=["A%d" % (n % NA)])

            def sV(n):
                t = tiles[n]
                c, r, g, kt, q0 = t["c"], t["r"], t["g"], t["kt"], t["q0"]
                rows = slice(r * 64, (r + 1) * 64)
                vS = vSb[c % 2]
                P.add("pe", lambda e, o=ps[7][:, q0:512], l_=vS[:, kt, :], r_=Ab[n % NA][:, q0:512], st=t["first"], sp_=t["last"]:
                      e.matmul(o, l_, r_, start=st, stop=sp_, skip_group_check=True),
                      reads=["A%d" % (n % NA), "v%d_%d" % (c % 2, kt // 4)], writes=["ps7"])
                if t["last"]:
                    sl = slice(g * 512, (g + 1) * 512)
                    P.add("act", lambda e, o=oT[rows, c, sl], i_=ps[7][rows, :]: e.activation(o, i_, AF.Identity),
                          reads=["ps7"], writes=["o%d_%d_%d" % (c, g, r)])

            run_pipeline(N, list(zip([0, 0, 1, 1, 2, 2, 2, 3, 4], [sZ, sE, sM, sL, sT, sC, sX, sA, sV])))

        def attn_swa(l):
            nonlocal oT
            oT = A.alloc([8, S], BF16)
            bmh = [A.alloc([2, 2, 128], F32) for _ in range(2)]
            wts = [A.alloc([8, 384], BF16) for _ in range(2)]
            qz = [A.alloc([S], BF16) for _ in range(2)]
            P.add("pool", lambda e: e.memset(qz[0][64:128, :], 0.0), writes=["qz0pad"])
            P.add("pool", lambda e: e.memset(qz[1][0:64, :], 0.0), writes=["qz1pad"])
            kT = A.alloc([S], BF16)
            vSb = [A.alloc([16, 128], BF16) for _ in range(2)]
            Tt = [[A.alloc([512], F32) for _ in range(2)] for _ in range(2)]
            Pt = [[A.alloc([512], BF16) for _ in range(2)] for _ in range(3)]
            rr = [A.alloc([512], F32) for _ in range(2)]
            P.add("act", lambda e: e.activation(small[:, ESINK:ESINK + 16], vecs[:, V_SINK:V_SINK + 16], AF.Exp), reads=["vecs", "small"], writes=["small"])

            def prefetch(c):
                load_w("pool", wts[c % 2], wd["wqkv%d" % l][c], "wt%d" % (c % 2))

            def prefetch_bm(c):
                P.dma("sp", bmh[c % 2][:, 0], bm_d[:, 0, 2 * c:2 * c + 2, :], writes=["bmh%da" % (c % 2)])
                P.dma("sp", bmh[c % 2][:, 1], bm_d[:, 2, 2 * c:2 * c + 2, :], writes=["bmh%db" % (c % 2)])

            its = [(c, r, qq) for c in range(8) for r in range(2) for qq in range(4)]
            N = len(its)

            def sS(n):
                c, r, qq = its[n]
                if r == 0 and qq == 0:
                    if c == 0:
                        prefetch(0)
                        prefetch_bm(0)
                    if c + 1 < 8:
                        prefetch(c + 1)
                    proj_qkv_head(wts[c % 2], "wt%d" % (c % 2), None, kT, vSb[(c // 2) % 2], 0.125, "", banks=(0, 1),
                                  vtag="v%d" % ((c // 2) % 2), qpad=qz, do_kv=(c % 2 == 0))
                if r == 1 and qq == 0 and c + 1 < 8:
                    prefetch_bm(c + 1)
                b0, b1 = 2 * (n % 2), 2 * (n % 2) + 1
                for t4 in range(4):
                    qt = qq * 4 + t4
                    cs = slice(t4 * 128, (t4 + 1) * 128)
                    qrd = ["q_%da" % qq, "q_%db" % qq, "qz0pad", "qz1pad"]
                    P.add("pe", lambda e, o=ps[b0][:, cs], l_=kT[:, qt * 128:(qt + 1) * 128], r_=qz[r][:, qt * 128:(qt + 1) * 128]:
                          e.matmul(o, l_, r_, start=True, stop=True), reads=["k_%d" % qq] + qrd, writes=["ps%d" % b0])
                    if qt > 0:
                        P.add("pe", lambda e, o=ps[b1][:, cs], l_=kT[:, (qt - 1) * 128:qt * 128], r_=qz[r][:, qt * 128:(qt + 1) * 128]:
                              e.matmul(o, l_, r_, start=True, stop=True), reads=["k_%d" % ((qt - 1) // 4)] + qrd, writes=["ps%d" % b1])

            def sB(n):
                c, r, qq = its[n]
                b0, b1 = 2 * (n % 2), 2 * (n % 2) + 1
                p0 = 0 if qq > 0 else 128
                np1 = (512 - p0) // 128
                bm = bmh[c % 2]
                P.add("dve", lambda e, o=ps[b0].rearrange("p (a b) -> p a b", a=4),
                      b=bm[:, 0, r, :].unsqueeze(1).broadcast_to([128, 4, 128]): e.tensor_tensor(o, o, b, ALU.add),
                      reads=["ps%d" % b0, "bmh%da" % (c % 2)], writes=["ps%d" % b0])
                P.add("dve", lambda e, o=ps[b1][:, p0:512].rearrange("p (a b) -> p a b", a=np1),
                      b=bm[:, 1, r, :].unsqueeze(1).broadcast_to([128, np1, 128]): e.tensor_tensor(o, o, b, ALU.add),
                      reads=["ps%d" % b1, "bmh%db" % (c % 2)], writes=["ps%d" % b1])

            def sP(n):
                c, r, qq = its[n]
                b0, b1 = 2 * (n % 2), 2 * (n % 2) + 1
                P0, P1 = Pt[n % 3]
                p0 = 0 if qq > 0 else 128
                P.add("act", lambda e, o=P0, i_=ps[b0]: e.activation(o, i_, AF.Exp), reads=["ps%d" % b0], writes=["Pt%d_0" % (n % 3)])
                P.add("act", lambda e, o=P1[:, p0:512], i_=ps[b1][:, p0:512]: e.activation(o, i_, AF.Exp), reads=["ps%d" % b1], writes=["Pt%d_1" % (n % 3)])

            def sV(n):
                c, r, qq = its[n]
                P0, P1 = Pt[n % 3]
                ob, db = 4 + n % 2, 6 + n % 2
                vS = vSb[(c // 2) % 2]
                vt = "v%d" % ((c // 2) % 2)
                for t4 in range(4):
                    qt = qq * 4 + t4
                    cs = slice(t4 * 128, (t4 + 1) * 128)
                    has_prev = qt > 0
                    P.add("pe", lambda e, o=ps[ob][:, cs], l_=vS[:, qt, :], r_=P0[:, cs], sp_=(not has_prev):
                          e.matmul(o, l_, r_, start=True, stop=sp_), reads=["Pt%d_0" % (n % 3), vt + "_%d" % qq], writes=["ps%d" % ob])
                    if has_prev:
                        P.add("pe", lambda e, o=ps[ob][:, cs], l_=vS[:, qt - 1, :], r_=P1[:, cs]:
                              e.matmul(o, l_, r_, start=False, stop=True), reads=["Pt%d_1" % (n % 3), vt + "_%d" % ((qt - 1) // 4)], writes=["ps%d" % ob])
                    P.add("pe", lambda e, o=ps[db][:, cs], r_=P0[:, cs], sp_=(not has_prev):
                          e.matmul(o, ones_b, r_, start=True, stop=sp_), reads=["Pt%d_0" % (n % 3), "ones_b"], writes=["ps%d" % db])
                    if has_prev:
                        P.add("pe", lambda e, o=ps[db][:, cs], r_=P1[:, cs]:
                              e.matmul(o, ones_b, r_, start=False, stop=True), reads=["Pt%d_1" % (n % 3), "ones_b"], writes=["ps%d" % db])

            def sN(n):
                c, r, qq = its[n]
                hq = 2 * c + r
                rows = slice(r * 64, (r + 1) * 64)
                sl = slice(qq * 512, (qq + 1) * 512)
                ob, db = 4 + n % 2, 6 + n % 2
                rb = rr[n % 2]
                rn = "rr%d" % (n % 2)
                P.add("act", lambda e, o=rb, i_=ps[db], s_=small[:, ESINK + hq:ESINK + hq + 1]: e.activation(o, i_, AF.Ln, bias=s_, scale=1.0),
                      reads=["ps%d" % db, "small"], writes=[rn])
                P.add("act", lambda e, o=rb: e.activation(o, o, AF.Exp, scale=-1.0), reads=[rn], writes=[rn])
                P.add("dve", lambda e, o=oT[rows, c, sl], i_=ps[ob][rows, :], r_=rb[rows, :]: e.tensor_tensor(o, i_, r_, ALU.mult),
                      reads=["ps%d" % ob, rn], writes=["o%d_%d_%d" % (c, qq, r)])

            run_pipeline(N, [(0, sS), (0, sB), (1, sP), (1, sV), (2, sN)])

        def ffn(l):
            wup = [A.alloc([8, 256], BF16) for _ in range(3)]
            wdn = [A.alloc([6, D], BF16) for _ in range(2)]
            ag = A.alloc([6, S], BF16)
            Ub = [[A.alloc([514], F32) for _ in range(2)] for _ in range(2)]
            Tb = [[A.alloc([512], F32) for _ in range(2)] for _ in range(2)]
            Sg = [A.alloc([512], F32) for _ in range(2)]
            for j in range(2):
                load_w("pool", wup[j], wd["wup%d" % l][j], "wup%d" % j)
            j0_, nj_ = FFN_GROUPS[0]
            load_w("pool", wdn[0][:, 0:nj_, :], wd["wdn%d" % l][j0_:j0_ + nj_].rearrange("j p m -> p j m"), "wdn0")
            cnt = [0, 0]
            pre_up = set()
            pending_up = []
            for gi, (j0, nj) in enumerate(FFN_GROUPS):
                ab = gi % 2
                if gi + 1 < len(FFN_GROUPS):
                    j1, nj1 = FFN_GROUPS[gi + 1]
                    load_w("pool", wdn[1 - ab][:, 0:nj1, :], wd["wdn%d" % l][j1:j1 + nj1].rearrange("j p m -> p j m"), "wdn%d" % (1 - ab))
                tl = [(jj, tg) for jj in range(nj) for tg in range(4)]
                base = cnt[0]
                cnt[0] += len(tl)

                def s_up(m, gi=gi, j0=j0, tl=tl, base=base):
                    if (gi, m) in pre_up:
                        return
                    pre_up.add((gi, m))
                    jj, tg = tl[m]
                    j = j0 + jj
                    n = base + m
                    wb = j % 3
                    if tg == 0 and j + 2 < NJ:
                        load_w("pool", wup[(j + 2) % 3], wd["wup%d" % l][j + 2], "wup%d" % ((j + 2) % 3))
                    sl = slice(tg * 512, (tg + 1) * 512)
                    pb = (n % 2) * 2
                    for half in (1, 0):
                        for kc in range(8):
                            P.add("pe", lambda e, o=ps[pb + half], w=wup[wb][:, kc, half * 128:(half + 1) * 128], r=hT[:, kc, sl], st=(kc == 0), sp_=(kc == 7):
                                  e.matmul(o, w, r, start=st, stop=sp_), reads=["wup%d" % wb, "h%d_%d" % (kc, tg)], writes=["ps%d" % (pb + half)])

                def s_conv(m):
                    jj, tg = tl[m]
                    j = j0 + jj
                    n = base + m
                    b = n % 2
                    pb = (n % 2) * 2
                    cv = V_CONV + (l * NJ + j) * 8
                    taps = []
                    for half in range(2):
                        U = ps[pb + half]
                        un = "ps%d" % (pb + half)
                        ub, T = Ub[half][b], Tb[half][b]
                        umn, uhn, tn = "Um%d_%d" % (half, b), "Uh%d_%d" % (half, b), "T%d_%d" % (half, b)
                        w0 = vecs[:, cv + half * 4 + 0:cv + half * 4 + 1]
                        w1 = vecs[:, cv + half * 4 + 1:cv + half * 4 + 2]
                        w2 = vecs[:, cv + half * 4 + 2:cv + half * 4 + 3]
                        bb = vecs[:, cv + half * 4 + 3:cv + half * 4 + 4]
                        P.add("act", lambda e, o=ub[:, 2:514], i=U: e.activation(o, i, AF.Identity), reads=[un], writes=[umn])
                        P.add("act", lambda e, o=T, i=U, s_=w2, b_=bb: e.activation(o, i, AF.Identity, bias=b_, scale=s_), reads=[un, "vecs"], writes=[tn])
                        if tg == 0:
                            P.add("pool", lambda e, o=ub[:, 0:2]: e.memset(o, 0.0), writes=[uhn])
                        else:
                            P.add("pool", lambda e, o=ub[:, 0:2], i=Ub[half][1 - b][:, 512:514]: e.tensor_copy(o, i),
                                  reads=["Um%d_%d" % (half, 1 - b)], writes=[uhn])
                        taps.append((ub, T, umn, uhn, tn, w0, w1))
                    for which in (1, 0):
                        for (ub, T, umn, uhn, tn, w0, w1) in taps:
                            if which == 1:
                                P.add("dve", lambda e, o=T, i=ub[:, 1:513], s_=w1: e.scalar_tensor_tensor(o, i, s_, o, ALU.mult, ALU.add),
                                      reads=[umn, uhn, "vecs", tn], writes=[tn])
                            else:
                                P.add("dve", lambda e, o=T, i=ub[:, 0:512], s_=w0: e.scalar_tensor_tensor(o, i, s_, o, ALU.mult, ALU.add),
                                      reads=[umn, uhn, "vecs", tn], writes=[tn])

                def s_gate(m):
                    jj, tg = tl[m]
                    n = base + m
                    b = n % 2
                    sl = slice(tg * 512, (tg + 1) * 512)
                    P.add("act", lambda e, o=Sg[b], i=Tb[0][b]: e.activation(o, i, AF.Silu), reads=["T0_%d" % b], writes=["Sg%d" % b])
                    P.add("pool", lambda e, o=ag[:, jj, sl], a_=Sg[b], b_=Tb[1][b]: e.tensor_tensor(o, a_, b_, ALU.mult),
                          reads=["Sg%d" % b, "T1_%d" % b], writes=["a_%d_%d" % (jj, tg)])

                run_pipeline(len(tl), [(0, s_up), (0, s_conv), (1, s_gate)])
                if gi + 1 < len(FFN_GROUPS):
                    j0n, njn = FFN_GROUPS[gi + 1]
                    tln = [(jj, tg) for jj in range(njn) for tg in range(4)]
                    basen = cnt[0]
                    for m_ in range(FFN_PRE):
                        s_up(m_, gi=gi + 1, j0=j0n, tl=tln, base=basen)
                for dc in range(8):
                    for tg in range(4):
                        sl = slice(tg * 512, (tg + 1) * 512)
                        bank = 4 + (cnt[1] % 2)
                        cnt[1] += 1
                        for jj in range(nj):
                            P.add("pe", lambda e, o=ps[bank], w=wdn[ab][:, jj, dc * 128:(dc + 1) * 128], r=ag[:, jj, sl], st=(jj == 0), sp_=(jj == nj - 1):
                                  e.matmul(o, w, r, start=st, stop=sp_), reads=["wdn%d" % ab, "a_%d_%d" % (jj, tg)], writes=["ps%d" % bank])
                        P.add("dve", lambda e, o=xT[:, dc, sl], i=ps[bank]: e.tensor_tensor(o, i, o, ALU.add),
                              reads=["ps%d" % bank, "x%d_%d" % (dc, tg)], writes=["x%d_%d" % (dc, tg)])

        def final_norm():
            sq = [A.alloc([512], BF16) for _ in range(3)]
            rstd = [A.alloc([512], F32) for _ in range(2)]
            yb = [A.alloc([512], F32) for _ in range(4)]
            k = 0
            m = 0
            outs = []
            for tg in range(4):
                sl = slice(tg * 512, (tg + 1) * 512)
                bank = 6 + (tg % 2)
                for c in range(8):
                    b = k % 3
                    k += 1
                    P.add("act", lambda e, o=sq[b], i=xT[:, c, sl]: e.activation(o, i, AF.Square), reads=["x%d_%d" % (c, tg)], writes=["sq%d" % b])
                    P.add("pe", lambda e, o=ps[bank], r=sq[b], st=(c == 0), sp_=(c == 7): e.matmul(o, ones_dmb, r, start=st, stop=sp_),
                          reads=["sq%d" % b, "ones_dm"], writes=["ps%d" % bank])
                rb = rstd[tg % 2]
                P.add("act", lambda e, o=rb, i=ps[bank]: e.activation(o, i, AF.Ln, bias=eps_ap, scale=1.0),
                      reads=["ps%d" % bank, "small"], writes=["rstd%d" % (tg % 2)])
                P.add("act", lambda e, o=rb: e.activation(o, o, AF.Exp, scale=-0.5),
                      reads=["rstd%d" % (tg % 2)], writes=["rstd%d" % (tg % 2)])
                for c in range(8):
                    y = yb[m % 4]
                    yn = "y%d" % (m % 4)
                    m += 1
                    P.add("dve", lambda e, o=y, i=xT[:, c, sl], g=vecs[:, V_FIN + c:V_FIN + c + 1], r=rb: e.scalar_tensor_tensor(o, i, g, r, ALU.mult, ALU.mult),
                          reads=["x%d_%d" % (c, tg), "vecs", "rstd%d" % (tg % 2)], writes=[yn])
                    on = "out%d" % ((m - 1) % 4)
                    P.dma("sp", yout[c * 128:(c + 1) * 128, sl], y, reads=[yn], writes=[on])
                    if on not in outs:
                        outs.append(on)
            return outs

        oT = None
        for l in layers:
            mixer = l % 3
            A.off = persist_end
            P.barrier()
            rmsnorm_to_h(V_AN + 8 * l, "a%d" % l)
            if mixer == 0:
                attn_da(l)
            elif mixer == 1:
                attn_sb(l)
            else:
                attn_swa(l)
            oproj(l, mixer != 0)
            A.off = persist_end
            P.barrier()
            rmsnorm_to_h(V_FN + 8 * l, "f%d" % l)
            if not DEBUG_NOFFN:
                ffn(l)
        outs = []
        A.off = persist_end
        P.barrier()
        if do_final:
            outs = final_norm()
        else:
            for c in range(8):
                on = "out%d" % c
                P.dma("sp", yout[c * 128:(c + 1) * 128, :], xT[:, c, :], reads=["x%d_%d" % (c, t) for t in range(4)], writes=[on])
                outs.append(on)
        P.finish(outs)
        P.emit()
    return nc


LAYER_GROUPS = [[0, 1, 2, 3]]


def kernel(**inputs):
    x = np.asarray(inputs["x"], np.float32)
    B = x.shape[0]
    shared = _prep_shared(inputs)
    cur = [np.ascontiguousarray(x[b].T) for b in range(B)]
    for gi, layers in enumerate(LAYER_GROUPS):
        do_final = (gi == len(LAYER_GROUPS) - 1)
        nc = build_nc(layers, do_final)
        lw = {}
        for l in layers:
            lw.update(_prep_layer(inputs, l))
        in_maps = []
        for b in range(B):
            m = dict(shared)
            m.update(lw)
            m["xin"] = cur[b]
            in_maps.append(m)
        res = run_bass_kernel_spmd(nc, in_maps, core_ids=list(range(B)))
        cur = [np.asarray(res.results[b]["yout"], np.float32) for b in range(B)]
    out = np.stack([c.T for c in cur], axis=0)
    return np.ascontiguousarray(out.astype(np.float32))
```
